# Optimizing a Trainium2 kernel written in Bass

```python
import jax, jax.numpy as jnp
from jax import lax
import numpy as np

D_MODEL = 2048
BATCH = 8
SEQ = 2048
DEPTH = 1

MEM_LEN = 256
EPS = 1e-6
CHUNK = 128
A_GROUPS = 4
A_GROUP_CH = 128
A_WIDTH = A_GROUPS * A_GROUP_CH
WINDOW = 128
B_HEADS = 16
B_KV_HEADS = 2
B_HEAD_DIM = 64
B_WIDTH = B_HEADS * B_HEAD_DIM
B_KV_WIDTH = B_KV_HEADS * B_HEAD_DIM
ROPE_DIM = B_HEAD_DIM // 4
ROPE_THETA = 500000.0
C_HEADS = 4
C_HEAD_DIM = 128
C_WIDTH = C_HEADS * C_HEAD_DIM
N_BRANCH = 3
D_FF = 5632
CONV_W = 3

SPLITS = list(np.cumsum([A_WIDTH, A_WIDTH, B_WIDTH, B_KV_WIDTH, B_KV_WIDTH, C_WIDTH]).tolist())
IN_COLS = 2 * A_WIDTH + B_WIDTH + 2 * B_KV_WIDTH + C_WIDTH + N_BRANCH * D_MODEL

kernel_name = "hybrid_gated_parallel_mixers"


def rmsnorm(x, g):
    xf = x.astype(jnp.float32)
    y = xf * lax.rsqrt(jnp.mean(xf * xf, axis=-1, keepdims=True) + EPS)
    return (y * g.astype(jnp.float32)).astype(x.dtype)


def partial_rope(x, pos):
    half = ROPE_DIM // 2
    inv = ROPE_THETA ** (-jnp.arange(half, dtype=jnp.float32) / half)
    ang = pos.astype(jnp.float32)[..., None] * inv
    cos = jnp.cos(ang)[:, :, None, :]
    sin = jnp.sin(ang)[:, :, None, :]
    xr = x[..., :ROPE_DIM].astype(jnp.float32)
    x1, x2 = xr[..., :half], xr[..., half:]
    rot = jnp.concatenate([x1 * cos - x2 * sin, x2 * cos + x1 * sin], axis=-1)
    return jnp.concatenate([rot.astype(x.dtype), x[..., ROPE_DIM:]], axis=-1)


def chunked_spatial_gating(u, v, g_v, w_s, b_s):
    bn, s_len, _ = u.shape
    nc = s_len // CHUNK
    u = jax.nn.gelu(u)
    v = rmsnorm(jax.nn.gelu(v), g_v)
    v = v.reshape(bn, nc, CHUNK, A_GROUPS, A_GROUP_CH)
    causal = jnp.tril(jnp.ones((CHUNK, CHUNK), dtype=bool))
    w = jnp.where(causal[None], w_s, jnp.zeros_like(w_s))
    s = jnp.einsum('gts,bnsgc->bntgc', w, v) + b_s.T[None, None, :, :, None]
    return u * s.reshape(bn, s_len, A_WIDTH)


def sliding_window_gqa(q, k, v, g_q, g_k, sinks, pos):
    bn, s_len = q.shape[:2]
    nb = s_len // WINDOW
    rep = B_HEADS // B_KV_HEADS
    q = partial_rope(rmsnorm(q, g_q), pos)
    k = partial_rope(rmsnorm(k, g_k), pos)
    qb = q.reshape(bn, nb, WINDOW, B_KV_HEADS, rep, B_HEAD_DIM)
    kb = k.reshape(bn, nb, WINDOW, B_KV_HEADS, B_HEAD_DIM)
    vb = v.reshape(bn, nb, WINDOW, B_KV_HEADS, B_HEAD_DIM)
    pad_k = jnp.zeros_like(kb[:, :1])
    pad_v = jnp.zeros_like(vb[:, :1])
    k2 = jnp.concatenate([jnp.concatenate([pad_k, kb[:, :-1]], axis=1), kb], axis=2)
    v2 = jnp.concatenate([jnp.concatenate([pad_v, vb[:, :-1]], axis=1), vb], axis=2)
    s = jnp.einsum('bnqhrd,bnkhd->bnhrqk', qb, k2,
                   preferred_element_type=jnp.float32) * (B_HEAD_DIM ** -0.5)
    qi = jnp.arange(WINDOW)[:, None] + WINDOW
    kj = jnp.arange(2 * WINDOW)[None, :]
    rel = qi - kj
    band = (rel >= 0) & (rel < WINDOW)
    blk = jnp.arange(nb)[:, None, None]
    valid = band[None] & ((kj[None] >= WINDOW) | (blk > 0))
    s = jnp.where(valid[None, :, None, None], s, -jnp.inf)
    sink_col = jnp.broadcast_to(
        sinks.astype(jnp.float32).reshape(1, 1, B_KV_HEADS, rep, 1, 1), s.shape[:-1] + (1,))
    p = jax.nn.softmax(jnp.concatenate([s, sink_col], axis=-1), axis=-1)[..., :-1]
    o = jnp.einsum('bnhrqk,bnkhd->bnqhrd', p.astype(v.dtype), v2)
    return o.reshape(bn, s_len, B_WIDTH)


def memory_cross_attention(q, mem_h, w_mem_kv, g_q, g_k):
    bn, s_len = q.shape[:2]
    m_len = mem_h.shape[1]
    kv = (mem_h @ w_mem_kv).reshape(bn, m_len, 2, C_HEADS, C_HEAD_DIM)
    k, v = kv[:, :, 0], kv[:, :, 1]
    q = rmsnorm(q, g_q)
    k = rmsnorm(k, g_k)
    s = jnp.einsum('bshd,bmhd->bhsm', q, k,
                   preferred_element_type=jnp.float32) * (C_HEAD_DIM ** -0.5)
    p = jax.nn.softmax(s, axis=-1)
    o = jnp.einsum('bhsm,bmhd->bshd', p.astype(v.dtype), v)
    return o.reshape(bn, s_len, C_WIDTH)


def gated_conv_ffn(h, w_up, conv_w, conv_b, w_down):
    up = h @ w_up
    c = up.shape[-1]
    up = lax.conv_general_dilated(
        up, conv_w[:, None, :].astype(up.dtype), window_strides=(1,),
        padding=[(CONV_W - 1, 0)], dimension_numbers=('NWC', 'WIO', 'NWC'),
        feature_group_count=c) + conv_b
    a, b = up[..., :D_FF], up[..., D_FF:]
    return (jax.nn.silu(a) * b) @ w_down


def setup_inputs(seed: int = 0) -> dict:
    key = jax.random.key(seed)
    ks = jax.random.split(key, 26)
    f32 = jnp.float32
    L = DEPTH

    def nrm(k, shape, scale):
        return jax.random.normal(k, shape, f32) * scale

    def gain(k, n):
        return 1.0 + 0.02 * jax.random.normal(k, (L, n), f32)

    x = nrm(ks[0], (BATCH, SEQ, D_MODEL), 1.0)
    mem = nrm(ks[1], (BATCH, MEM_LEN, D_MODEL), 1.0)
    positions = (jax.random.randint(ks[2], (BATCH, 1), 0, 4096, jnp.int32)
                 + jnp.arange(SEQ, dtype=jnp.int32)[None, :])
    return {
        "x": x,
        "mem": mem,
        "positions": positions,
        "g_mix": gain(ks[3], D_MODEL),
        "w_in": nrm(ks[4], (L, D_MODEL, IN_COLS), D_MODEL ** -0.5),
        "g_a_v": gain(ks[5], A_WIDTH),
        "w_spatial": nrm(ks[6], (L, A_GROUPS, CHUNK, CHUNK), CHUNK ** -0.5),
        "b_spatial": 1.0 + 0.1 * jax.random.normal(ks[7], (L, A_GROUPS, CHUNK), f32),
        "g_b_q": gain(ks[8], B_HEAD_DIM),
        "g_b_k": gain(ks[9], B_HEAD_DIM),
        "sinks": nrm(ks[10], (L, B_HEADS), 0.5),
        "g_mem": gain(ks[11], D_MODEL),
        "w_mem_kv": nrm(ks[12], (L, D_MODEL, 2 * C_WIDTH), D_MODEL ** -0.5),
        "g_c_q": gain(ks[13], C_HEAD_DIM),
        "g_c_k": gain(ks[14], C_HEAD_DIM),
        "w_branch_a": nrm(ks[15], (L, A_WIDTH, D_MODEL), A_WIDTH ** -0.5),
        "w_branch_b": nrm(ks[16], (L, B_WIDTH, D_MODEL), B_WIDTH ** -0.5),
        "w_branch_c": nrm(ks[17], (L, C_WIDTH, D_MODEL), C_WIDTH ** -0.5),
        "w_out": nrm(ks[18], (L, D_MODEL, D_MODEL), D_MODEL ** -0.5),
        "g_ffn": gain(ks[19], D_MODEL),
        "w_up": nrm(ks[20], (L, D_MODEL, 2 * D_FF), D_MODEL ** -0.5),
        "conv_w": nrm(ks[21], (L, CONV_W, 2 * D_FF), CONV_W ** -0.5),
        "conv_b": nrm(ks[22], (L, 2 * D_FF), 0.01),
        "w_down": nrm(ks[23], (L, D_FF, D_MODEL), D_FF ** -0.5),
    }


def reference(x, mem, positions, g_mix, w_in, g_a_v, w_spatial, b_spatial, g_b_q, g_b_k,
              sinks, g_mem, w_mem_kv, g_c_q, g_c_k, w_branch_a, w_branch_b, w_branch_c,
              w_out, g_ffn, w_up, conv_w, conv_b, w_down):
    bn, s_len, _ = x.shape
    for l in range(DEPTH):
        h = rmsnorm(x, g_mix[l])
        proj = h @ w_in[l]
        u_a, v_a, q_b, k_b, v_b, q_c, gates = jnp.split(proj, SPLITS, axis=-1)
        y_a = chunked_spatial_gating(u_a, v_a, g_a_v[l], w_spatial[l], b_spatial[l])
        y_b = sliding_window_gqa(
            q_b.reshape(bn, s_len, B_HEADS, B_HEAD_DIM),
            k_b.reshape(bn, s_len, B_KV_HEADS, B_HEAD_DIM),
            v_b.reshape(bn, s_len, B_KV_HEADS, B_HEAD_DIM),
            g_b_q[l], g_b_k[l], sinks[l], positions)
        y_c = memory_cross_attention(
            q_c.reshape(bn, s_len, C_HEADS, C_HEAD_DIM), rmsnorm(mem, g_mem[l]),
            w_mem_kv[l], g_c_q[l], g_c_k[l])
        gate = jax.nn.sigmoid(gates.reshape(bn, s_len, N_BRANCH, D_MODEL))
        merged = (gate[:, :, 0] * (y_a @ w_branch_a[l])
                  + gate[:, :, 1] * (y_b @ w_branch_b[l])
                  + gate[:, :, 2] * (y_c @ w_branch_c[l]))
        x = x + merged @ w_out[l]
        x = x + gated_conv_ffn(rmsnorm(x, g_ffn[l]), w_up[l], conv_w[l], conv_b[l], w_down[l])
    return x
```

```python
import numpy as np
import concourse.bass as bass
import concourse.mybir as mybir

F32 = mybir.dt.float32
BF16 = mybir.dt.bfloat16
I32 = mybir.dt.int32
AF = mybir.ActivationFunctionType
ALU = mybir.AluOpType
AX = mybir.AxisListType
DSZ = {F32: 4, BF16: 2, I32: 4}
PAGE = 256


class Region:
    __slots__ = ("space", "lo", "hi", "last_w", "readers", "name")

    def __init__(self, space, lo, hi, name=""):
        self.space, self.lo, self.hi, self.name = space, lo, hi, name
        self.last_w = {}
        self.readers = {}


class View:
    __slots__ = ("tile", "ap")

    def __init__(self, tile, ap):
        self.tile, self.ap = tile, ap

    def __getitem__(self, key):
        return View(self.tile, self.ap[key])

    def bitcast(self, dt):
        return View(self.tile, self.ap.bitcast(dt))

    def rearrange(self, s, **kw):
        return View(self.tile, self.ap.rearrange(s, **kw))

    def broadcast_to(self, shape):
        return View(self.tile, self.ap.broadcast_to(list(shape)))

    def unsqueeze(self, axis):
        return View(self.tile, self.ap.unsqueeze(axis))


class Tile(View):
    __slots__ = ("region", "shape", "dtype")

    def __init__(self, K, space, base_ap, lo, hi, shape, dtype, name=""):
        self.tile = self
        self.ap = base_ap
        self.region = Region(space, lo, hi, name)
        self.shape, self.dtype = shape, dtype
        K._register(self.region)


class Eng:
    def __init__(self, name, h, sem, every):
        self.name, self.h, self.sem, self.every = name, h, sem, every
        self.count = 0
        self.known = {}


class MK:
    def __init__(self, nc, sb_bytes=200 * 1024):
        self.nc = nc
        self.sb_bytes = sb_bytes
        self.sb = nc.alloc_sbuf_tensor("mk_sb", [128, sb_bytes // 4], F32)
        self.ps = nc.alloc_psum_tensor("mk_ps", [128, 8 * 512], F32)
        self.sb_ap = self.sb.ap() if hasattr(self.sb, "ap") else self.sb[:]
        self.ps_ap = self.ps.ap() if hasattr(self.ps, "ap") else self.ps[:]
        self.pages = {"sb": {}, "ps": {}}
        self._sems = []
        self.eng = {}
        for name, h, every in (("pe", nc.tensor, False), ("act", nc.scalar, True),
                               ("dve", nc.vector, True), ("pool", nc.gpsimd, True),
                               ("sp", nc.sync, True)):
            self.eng[name] = Eng(name, h, self.new_sem("c_" + name), every)
        self.dma_cum = {}
        self.sb_top = 0
        self.n_wait = 0
        self.n_inst = 0

    def new_sem(self, name):
        s = self.nc.alloc_semaphore(name)
        self._sems.append(s)
        return s

    def sb_tile(self, shape, dtype, name="", at=None):
        n = int(np.prod(shape[1:])) * DSZ[dtype]
        n = (n + 31) // 32 * 32
        if at is None:
            at = self.sb_top
            self.sb_top += n
        assert at + n <= self.sb_bytes, f"SBUF overflow {name} {at + n}"
        ap = self.sb_ap[0:shape[0], at // 4:(at + n) // 4]
        if dtype != F32:
            ap = ap.bitcast(dtype)
        nel = int(np.prod(shape[1:]))
        ap = ap[:, 0:nel]
        if len(shape) > 2:
            names = " ".join(f"d{i}" for i in range(len(shape) - 1))
            ap = ap.rearrange(f"p ({names}) -> p {names}",
                              **{f"d{i}": shape[i + 1] for i in range(len(shape) - 2)})
        return Tile(self, "sb", ap, at, at + n, list(shape), dtype, name)

    def ps_tile(self, bank, nbanks=1, shape=None, dtype=F32, name=""):
        lo = bank * 2048
        hi = (bank + nbanks) * 2048
        ap = self.ps_ap[:, bank * 512:(bank + nbanks) * 512]
        if dtype != F32:
            ap = ap.bitcast(dtype)
        if shape is not None:
            nel = int(np.prod(shape[1:]))
            ap = ap[0:shape[0], 0:nel]
            if len(shape) > 2:
                names = " ".join(f"d{i}" for i in range(len(shape) - 1))
                ap = ap.rearrange(f"p ({names}) -> p {names}",
                                  **{f"d{i}": shape[i + 1] for i in range(len(shape) - 2)})
        return Tile(self, "ps", ap, lo, hi, shape, dtype, name)

    def _register(self, r):
        pg = self.pages[r.space]
        for p in range(r.lo // PAGE, (r.hi - 1) // PAGE + 1):
            pg.setdefault(p, []).append(r)

    def _overlaps(self, r):
        pg = self.pages[r.space]
        seen = {id(r): r}
        for p in range(r.lo // PAGE, (r.hi - 1) // PAGE + 1):
            for g in pg.get(p, ()):
                if id(g) not in seen and g.lo < r.hi and r.lo < g.hi:
                    seen[id(g)] = g
        return seen.values()

    def _deps(self, reads, writes, own=None):
        need = {}

        def add(tok):
            k = id(tok[0])
            if k not in need or need[k][1] < tok[1]:
                need[k] = tok
        for t in reads:
            for g in self._overlaps(t.region):
                for tok in g.last_w.values():
                    add(tok)
                if g.space == "ps":
                    for k, tok in g.readers.items():
                        if k != own:
                            add(tok)
        for t in writes:
            for g in self._overlaps(t.region):
                for tok in g.last_w.values():
                    add(tok)
                for tok in g.readers.values():
                    add(tok)
        return need

    def _wait(self, e, need):
        for k, (sem, val) in need.items():
            if sem is e.sem and e.name == "pe":
                continue
            if e.known.get(k, 0) >= val:
                continue
            e.h.wait_ge(sem, val)
            e.known[k] = val
            self.n_wait += 1

    def _commit(self, tok, reads, writes):
        k = id(tok[0])
        for t in reads:
            r = t.region
            if k not in r.readers or r.readers[k][1] < tok[1]:
                r.readers[k] = tok
        for t in writes:
            r = t.region
            for g in self._overlaps(r):
                if g is not r and g.lo >= r.lo and g.hi <= r.hi:
                    g.last_w = {}
                    g.readers = {}
            r.last_w = {k: tok}
            r.readers = {}

    @staticmethod
    def _split(args):
        tiles, aps = [], []
        for a in args:
            if isinstance(a, View):
                tiles.append(a.tile)
                aps.append(a.ap)
            else:
                aps.append(a)
        return tiles, aps

    def op(self, engname, fn, outs, ins, sig=None, extra_reads=(), extra_writes=()):
        e = self.eng[engname]
        wt, wa = self._split(outs)
        rt, ra = self._split(ins)
        rt = rt + [v.tile for v in extra_reads]
        wt = wt + [v.tile for v in extra_writes]
        self._wait(e, self._deps(rt, wt, id(e.sem)))
        inst = fn(e.h, *wa, *ra)
        self.n_inst += 1
        if sig is None:
            sig = e.every
        if sig:
            e.count += 1
            inst.then_inc(e.sem, 1)
            tok = (e.sem, e.count)
        else:
            tok = (e.sem, e.count + 1)
        self._commit(tok, rt, wt)
        return inst

    def dma(self, qname, out, in_, sem, **kw):
        e = self.eng[qname]
        wt, wa = self._split([out])
        rt, ra = self._split([in_])
        self._wait(e, self._deps(rt, wt))
        inst = e.h.dma_start(out=wa[0], in_=ra[0], **kw)
        inst.then_inc(sem, 16)
        self.n_inst += 1
        k = id(sem)
        self.dma_cum[k] = self.dma_cum.get(k, 0) + 16
        tok = (sem, self.dma_cum[k])
        self._commit(tok, rt, wt)
        return tok

    def retoken(self, tiles, sem):
        tok = (sem, self.dma_cum[id(sem)])
        for t in tiles:
            t.region.last_w = {id(sem): tok}

    def wait_all(self, engname, toks):
        e = self.eng[engname]
        need = {}
        for tok in toks:
            k = id(tok[0])
            if k not in need or need[k][1] < tok[1]:
                need[k] = tok
        self._wait(e, need)

    def mm(self, out, lhsT, rhs, start, stop, sig=None, **kw):
        return self.op("pe", lambda h, o, l, r: h.matmul(o, l, r, start=start, stop=stop, **kw),
                       [out], [lhsT, rhs], sig=stop if sig is None else sig)

    def tr(self, out, in_, ident, sig=False):
        return self.op("pe", lambda h, o, i, d: h.transpose(o, i, d), [out], [in_, ident], sig=sig)

    def act(self, out, in_, func, bias=None, scale=None, accum_out=None, eng="act"):
        ins = [in_]
        kw = {}
        order = []
        if bias is not None:
            if isinstance(bias, View):
                ins.append(bias); order.append("bias")
            else:
                kw["bias"] = bias
        if scale is not None:
            if isinstance(scale, View):
                ins.append(scale); order.append("scale")
            else:
                kw["scale"] = scale
        outs = [out]
        if accum_out is not None:
            outs.append(accum_out)

        def fn(h, *a):
            o = a[0]
            i0 = 1
            acc = None
            if accum_out is not None:
                acc = a[1]; i0 = 2
            k2 = dict(kw)
            for j, nm in enumerate(order):
                k2[nm] = a[i0 + 1 + j]
            if acc is not None:
                k2["accum_out"] = acc
            return h.activation(o, a[i0], func, **k2)
        return self.op(eng, fn, outs, ins)

    def tt(self, out, in0, in1, op, eng="dve"):
        return self.op(eng, lambda h, o, a, b: h.tensor_tensor(o, a, b, op), [out], [in0, in1])

    def ts(self, out, in0, s1, s2, op0, op1=None, accum_out=None, eng="dve"):
        ins = [in0]
        idx = {}
        if isinstance(s1, View):
            idx["s1"] = len(ins); ins.append(s1)
        if isinstance(s2, View):
            idx["s2"] = len(ins); ins.append(s2)
        outs = [out] + ([accum_out] if accum_out is not None else [])
        no = len(outs)

        def fn(h, *a):
            o = a[0]
            i = a[no:]
            v1 = i[idx["s1"]] if "s1" in idx else s1
            v2 = i[idx["s2"]] if "s2" in idx else s2
            kw = {}
            if op1 is not None:
                kw["op1"] = op1
            if accum_out is not None:
                kw["accum_out"] = a[1]
            return h.tensor_scalar(o, i[0], v1, v2, op0, **kw)
        return self.op(eng, fn, outs, ins)

    def stt(self, out, in0, scalar, in1, op0, op1, eng="dve"):
        ins = [in0, in1]
        if isinstance(scalar, View):
            ins.append(scalar)

        def fn(h, o, a, b, *s):
            return h.scalar_tensor_tensor(o, a, s[0] if s else scalar, b, op0, op1)
        return self.op(eng, fn, [out], ins)

    def copy(self, out, in_, eng="dve"):
        return self.op(eng, lambda h, o, i: h.tensor_copy(o, i), [out], [in_])

    def memset(self, out, val, eng="dve"):
        return self.op(eng, lambda h, o: h.memset(o, val), [out], [])

    def reduce(self, out, in_, op, axis=AX.X, eng="dve"):
        return self.op(eng, lambda h, o, i: h.tensor_reduce(o, i, axis, op), [out], [in_])

    def recip(self, out, in_):
        return self.op("dve", lambda h, o, i: h.reciprocal(o, i), [out], [in_])


from contextlib import ExitStack
from concourse.bass_utils import run_bass_kernel_spmd

D = 2048
S = 2048
T = 512
NPASS = S // T
TT = T // 128
MEM = 256
DFF = 5632
NFC = DFF // 512
EPS = 1e-6
C_IN = 8960
G0 = 2816
NSLAB = 7
SLAB_EL = 4096
PER_PASS = 117
NSLAB_TOTAL = 4 + PER_PASS

DEBUG = {}


def build(npass=NPASS, dbg=None):
    nc = bass.Bass("TRN2", target_bir_lowering=False)

    def din(name, shape, dt=F32):
        return nc.dram_tensor(name, list(shape), dt, kind="ExternalInput").ap()

    x_d = din("x", [S, D])
    mem_d = din("mem", [MEM, D])
    pos_d = din("pos_t", [128, 16], I32)
    gmix_d = din("gmixT", [128, 16])
    gffn_d = din("gffnT", [128, 16])
    gmem_d = din("gmemT", [128, 16])
    wpack_d = din("wpack", [NSLAB_TOTAL, 128, SLAB_EL])
    gav_d = din("gavT", [128, 4])
    wsT_d = din("wsT", [128, 4, 128])
    bsp_d = din("bsp", [128, 512])
    gbq_d = din("gbq", [128, 64])
    gbk_d = din("gbk", [128, 64])
    sinks_d = din("sinks", [128, 16])
    gcq_d = din("gcq", [128, 1])
    gck_d = din("gck", [128, 128])
    cw_d = din("convw", [128, 3, 88])
    cb_d = din("convb", [128, 88])
    out_d = nc.dram_tensor("out", [S, D], F32, kind="ExternalOutput").ap()
    dbg_d = None
    if dbg is not None:
        dbg_d = nc.dram_tensor("dbg", [128, dbg], F32, kind="ExternalOutput").ap()

    K = MK(nc, 206 * 1024)
    sem_c = K.new_sem("const")
    sem_x = [K.new_sem(f"x{i}") for i in range(TT)]
    sem_slab = [K.new_sem(f"slab{i}") for i in range(NSLAB)]
    sem_slabx = [K.new_sem(f"slabx{i}") for i in range(2)]
    sem_xs = [K.new_sem(f"xs{i}") for i in range(TT)]
    sem_dbg = K.new_sem("dbg")

    X1 = [K.sb_tile([128, D], F32, f"x1_{i}") for i in range(TT)]
    X1m = [[K.sb_tile([128, 512], F32, f"x1_{i}_{m}", at=X1[i].region.lo + m * 2048) for m in range(4)]
           for i in range(TT)]
    hT = [K.sb_tile([128, T], BF16, f"hT{c}") for c in range(16)]
    yTa = K.sb_tile([128, 4, T], BF16, "yTa")
    yTb = K.sb_tile([128, 8, T], BF16, "yTb")
    yTc = K.sb_tile([128, 4, T], BF16, "yTc")
    mT = [K.sb_tile([128, T], BF16, f"mT{c}") for c in range(16)]
    slabs = [K.sb_tile([128, SLAB_EL], BF16, f"slab{i}") for i in range(NSLAB)]
    xstage = [K.sb_tile([128, D], F32, f"xstage{i}", at=yTa.region.lo + i * 8192) for i in range(TT)]
    assert xstage[-1].region.hi <= mT[15].region.hi
    slabs_x = [K.sb_tile([128, SLAB_EL], BF16, f"slabx{i}", at=mT[8 * i].region.lo) for i in range(2)]
    ident = K.sb_tile([128, 128], BF16, "ident")
    gmixT = K.sb_tile([128, 16], F32, "gmixT")
    gffnT = K.sb_tile([128, 16], F32, "gffnT")
    gmemT = K.sb_tile([128, 16], F32, "gmemT")
    gavT = K.sb_tile([128, 4], F32, "gavT")
    gcq = K.sb_tile([128, 1], F32, "gcq")
    convw = K.sb_tile([128, 3, 88], F32, "convw")
    convb = K.sb_tile([128, 88], F32, "convb")
    hist = K.sb_tile([128, 88, 2], F32, "hist")
    wsT = K.sb_tile([128, 4, 128], BF16, "wsT")
    bsp = K.sb_tile([128, 4, 128], F32, "bsp")
    gbq = K.sb_tile([128, 64], F32, "gbq")
    gbk = K.sb_tile([128, 64], F32, "gbk")
    gck = K.sb_tile([128, 128], F32, "gck")
    sinkexp = K.sb_tile([128, 16], F32, "sinkexp")
    cosT = K.sb_tile([128, 16, 8], F32, "cosT")
    sinT = K.sb_tile([128, 16, 8], F32, "sinT")
    mask_cur = K.sb_tile([128, 4, 128], BF16, "mask_cur")
    mask_prev = K.sb_tile([128, 4, 128], BF16, "mask_prev")
    ones_bf = K.sb_tile([128, 128], BF16, "ones_bf")
    kTmem = K.sb_tile([128, 4, MEM], BF16, "kTmem")
    vmem = K.sb_tile([128, 2, 512], BF16, "vmem")
    KT = [K.sb_tile([128, 4, 128], BF16, f"KT{i}") for i in range(3)]
    vaug = [K.sb_tile([128, 2, 65], BF16, f"vaug{i}") for i in range(3)]
    kpad = K.sb_tile([128, 4, 128], BF16, "kpad")
    ss = K.sb_tile([128, 32], F32, "ss")
    rstd = K.sb_tile([128, 32], F32, "rstd")
    ssA = K.sb_tile([128, 8], F32, "ssA")
    ssB = K.sb_tile([128, 32], F32, "ssB")
    rstdB = K.sb_tile([128, 32], F32, "rstdB")

    base = K.sb_top
    xn = [K.sb_tile([128, D], BF16, f"xn{i}") for i in range(2)]
    junk = xn[1]
    top_norm = K.sb_top
    xs = [K.sb_tile([128, 516], F32, f"xs{i}") for i in range(2)]
    acc = [K.sb_tile([128, 512], F32, f"acc{i}") for i in range(2)]
    sa = [K.sb_tile([128, 512], BF16, f"sa{c}") for c in range(4)]
    actT = [[K.sb_tile([128, 512], BF16, f"actT{j}_{c}") for c in range(4)] for j in range(2)]
    top_f = K.sb_top
    K.sb_top = top_norm
    sg = [K.sb_tile([128, 512], BF16, f"sg{i}") for i in range(6)]
    tm = [K.sb_tile([128, 512], F32, f"tm{i}") for i in range(4)]
    top_m = K.sb_top
    K.sb_top = base
    uT = [K.sb_tile([128, T], BF16, f"uT{c}") for c in range(4)]
    vn = [K.sb_tile([128, 512], BF16, f"vn{i}") for i in range(TT)]
    vg = [K.sb_tile([128, 512], F32, f"vg{i}") for i in range(2)]
    tmpA = K.sb_tile([128, 512], F32, "tmpA")
    sq = K.sb_tile([128, 1024 + 128], F32, "sq")
    qn = K.sb_tile([128, 16, 64], F32, "qn", at=sq.region.lo)
    kn = K.sb_tile([128, 2, 64], F32, "kn")
    rt = [K.sb_tile([128, 16, 8], F32, f"rt{i}") for i in range(4)]
    qr = K.sb_tile([128, 1024], BF16, "qr")
    yb = K.sb_tile([128, 1024], BF16, "yb", at=qr.region.lo)
    qT = [K.sb_tile([128, 1024], BF16, f"qT{i}") for i in range(2)]
    PT = [K.sb_tile([128, 512], BF16, f"PT{i}") for i in range(8)]
    den = K.sb_tile([128, 16], F32, "den")
    sqc = K.sb_tile([128, 512], BF16, "sqc")
    msc = K.sb_tile([128, 512], F32, "msc")
    qcT = K.sb_tile([128, 512], BF16, "qcT")
    PTc = [K.sb_tile([128, 512], BF16, f"PTc{i}") for i in range(2)]
    rdc = K.sb_tile([128, 512], F32, "rdc")
    top_abc = K.sb_top
    K.sb_top = max(top_f, top_m, top_abc)
    print("SBUF bytes used per partition:", K.sb_top)

    banks = [K.ps_tile(b, 1, [128, 512], F32, f"bank{b}") for b in range(8)]
    bank_i = [0]

    def bank():
        b = banks[bank_i[0] % 7]
        bank_i[0] += 1
        return b

    def bf(bk, shape):
        v = bk.bitcast(BF16)
        n = int(np.prod(shape[1:]))
        v = v[:, 0:n]
        if len(shape) == 3:
            v = v.rearrange("p (a b) -> p a b", a=shape[1])
        return v

    slab_i = [0]

    plan = []
    free_slots = list(range(NSLAB))

    def load_slab(parts, xslot=None):
        if xslot is None:
            assert free_slots, "slab ring exhausted"
            idx = free_slots.pop(0)
        n_mem = 4
        sidx = slab_i[0] if slab_i[0] < n_mem else n_mem + (slab_i[0] - n_mem) % PER_PASS
        if slab_i[0] < n_mem + PER_PASS:
            plan.append(list(parts))
        slab_i[0] += 1
        if xslot is not None:
            sl = slabs_x[xslot]
            K.dma("pool", sl, wpack_d[sidx], sem_slabx[xslot])
            return sl
        sl = slabs[idx]
        K.dma("pool", sl, wpack_d[sidx], sem_slab[idx])
        return sl

    def release(*views):
        for v in views:
            if v.tile in slabs_x:
                continue
            idx = slabs.index(v.tile)
            assert idx not in free_slots
            free_slots.append(idx)

    w_in_v, wmkv_v, wo_v, wup_v, wdn_v = "w_in", "w_mem_kv", "w_out", "w_up", "w_down"
    wba_v, wbb_v, wbc_v = "w_branch_a", "w_branch_b", "w_branch_c"

    def slab_k8(view, kh, c0, xslot=None):
        sl = load_slab([(0, view, kh * 8, (kh + 1) * 8, c0, c0 + 512)], xslot=xslot)
        return sl.rearrange("p (a b) -> p a b", a=8)

    def slab_k16(view, c0):
        sl = load_slab([(0, view, 0, 16, c0, c0 + 256)])
        return sl.rearrange("p (a b) -> p a b", a=16)

    dbg_off = [0]
    dbg_st = []
    dbg_n = [0]

    def dump(v, n):
        if dbg_d is None:
            return
        if not dbg_st:
            dbg_st.extend(K.sb_tile([128, 512], F32, f"dbgst{j}") for j in range(2))
        for c0 in range(0, n, 512):
            st = dbg_st[dbg_n[0] % 2]
            dbg_n[0] += 1
            K.copy(st, v[:, c0:c0 + 512])
            K.dma("sp", dbg_d[:, dbg_off[0]:dbg_off[0] + 512], st, sem_dbg)
            dbg_off[0] += 512

    for mt in range(2):
        K.dma("sp", xstage[mt], mem_d[mt * 128:(mt + 1) * 128, :], sem_xs[mt])
    cl = []
    for t, src in ((gmixT, gmix_d), (gffnT, gffn_d), (gmemT, gmem_d), (gavT, gav_d), (gcq, gcq_d),
                   (convw, cw_d), (convb, cb_d)):
        K.dma("sp", t, src, sem_c)
        cl.append(t)
    wsT_f = K.sb_tile([128, 4, 128], F32, "wsT_f", at=top_norm + 11264)
    pos_i = K.sb_tile([128, 16], I32, "pos_i", at=top_norm + 13312)
    sinks_f = K.sb_tile([128, 16], F32, "sinks_f", at=top_norm + 13376)
    K.dma("sp", wsT_f, wsT_d, sem_c)
    K.dma("sp", pos_i, pos_d, sem_c)
    K.dma("sp", bsp.rearrange("p a b -> p (a b)"), bsp_d, sem_c)
    K.dma("sp", gbq, gbq_d, sem_c)
    K.dma("sp", gbk, gbk_d, sem_c)
    K.dma("sp", gck, gck_d, sem_c)
    K.dma("sp", sinks_f, sinks_d, sem_c)
    K.retoken(cl + [wsT_f, pos_i, bsp, gbq, gbk, gck, sinks_f], sem_c)

    for i in range(TT):
        K.dma("sp", X1[i], x_d[i * 128:(i + 1) * 128, :], sem_x[i])
    K.memset(ident, 1.0, eng="pool")
    K.op("pool", lambda h, o, i: h.affine_select(o, i, [[-1, 128]], ALU.is_equal, 0.0, base=0,
                                                channel_multiplier=1), [ident], [ident])
    wk = [[slab_k8(wmkv_v, kh, n * 512) for kh in range(2)] for n in range(2)]
    def norm_sumsq(j, src):
        if j % 2 == 0:
            K.act(junk, src, AF.Square, accum_out=ss[:, j:j + 1])
        else:
            K.op("dve", lambda h, o, acc_, a_, b_: h.scalar_tensor_tensor(o, a_, 1.0, b_, ALU.mult, ALU.mult,
                                                                          accum_out=acc_),
                 [xn[0], ss[:, j:j + 1]], [src, src])

    def norm_rstd(n):
        K.ts(ss[:, 8:8 + n], ss[:, 0:n], 1.0 / D, EPS, ALU.mult, ALU.add)
        K.act(ss[:, 16:16 + n], ss[:, 8:8 + n], AF.Sqrt)
        K.recip(rstd[:, 0:n], ss[:, 16:16 + n])

    def norm_scale(j, src):
        if j % 2 == 0:
            K.act(xn[j % 2], src, AF.Identity, scale=rstd[:, j:j + 1])
        else:
            K.ts(xn[j % 2], src, rstd[:, j:j + 1], None, ALU.mult)

    def norm_transpose(j, gT, dstT):
        x_ = xn[j % 2]
        for hb in range(2):
            bk = bank()
            bv = bf(bk, [128, 8, 128])
            for jj in range(8):
                c = hb * 8 + jj
                K.tr(bv[:, jj, :], x_[:, c * 128:(c + 1) * 128], ident, sig=(jj == 7))
            for jj in range(8):
                c = hb * 8 + jj
                if hb == 0:
                    K.ts(dstT(c)[:, j * 128:(j + 1) * 128], bv[:, jj, :], gT[:, c:c + 1], None, ALU.mult)
                else:
                    K.act(dstT(c)[:, j * 128:(j + 1) * 128], bv[:, jj, :], AF.Identity, scale=gT[:, c:c + 1])

    def norm_tiles(srcs, gT, dstT, have_stats=False, prescaled=0):
        n = len(srcs)
        if not have_stats:
            K.memset(ss[:, 0:n], 0.0)
            for j, src in enumerate(srcs):
                norm_sumsq(j, src)
            norm_rstd(n)
        for j, src in enumerate(srcs):
            if j >= prescaled:
                norm_scale(j, src)
            norm_transpose(j, gT, dstT)

    memT = K.sb_tile([128, 16, MEM], BF16, "memT", at=top_norm)
    norm_tiles([xstage[0], xstage[1]], gmemT, lambda c: memT[:, c, :])
    kfm = K.sb_tile([128, 512], F32, "kfm", at=top_norm + 8192)
    kbm = K.sb_tile([128, 512], BF16, "kbm", at=top_norm + 10240)
    for mt in range(2):
        bk_k, bk_v = bank(), bank()
        for n, bk in ((0, bk_k), (1, bk_v)):
            for k in range(16):
                K.mm(bk, memT[:, k, mt * 128:(mt + 1) * 128], wk[n][k // 8][:, k % 8, :], k == 0, k == 15)
        K.copy(vmem[:, mt, :], bk_v)
        K.act(kfm, bk_k, AF.Square)
        K.reduce(ss[:, 4:8], kfm.rearrange("p (h d) -> p h d", h=4), ALU.add)
        K.ts(ss[:, 8:12], ss[:, 4:8], 1.0 / 128, EPS, ALU.mult, ALU.add)
        K.act(ss[:, 12:16], ss[:, 8:12], AF.Sqrt)
        K.recip(rstd[:, 4:8], ss[:, 12:16])
        K.tt(kfm.rearrange("p (h d) -> p h d", h=4), bk_k.rearrange("p (h d) -> p h d", h=4),
             rstd[:, 4:8].unsqueeze(2).broadcast_to([128, 4, 128]), ALU.mult)
        K.tt(kbm.rearrange("p (h d) -> p h d", h=4), kfm.rearrange("p (h d) -> p h d", h=4),
             gck.unsqueeze(1).broadcast_to([128, 4, 128]), ALU.mult)
        bk = bank()
        bv = bf(bk, [128, 4, 128])
        for h in range(4):
            K.tr(bv[:, h, :], kbm[:, h * 128:(h + 1) * 128], ident, sig=(h == 3))
        K.copy(kTmem[:, :, mt * 128:(mt + 1) * 128], bv)
    release(wk[0][0], wk[0][1], wk[1][0], wk[1][1])

    K.memset(mask_cur, 0.0, eng="pool")
    K.op("pool", lambda h, o, i: h.affine_select(o, i, [[0, 4], [1, 128]], ALU.is_ge, -30000.0, base=0,
                                                channel_multiplier=-1), [mask_cur], [mask_cur])
    K.memset(mask_prev, 0.0, eng="pool")
    K.op("pool", lambda h, o, i: h.affine_select(o, i, [[0, 4], [-1, 128]], ALU.is_gt, -30000.0, base=0,
                                                channel_multiplier=1), [mask_prev], [mask_prev])
    K.memset(ones_bf, 1.0, eng="pool")
    K.op("pool", lambda h, o, i: h.affine_select(o, i, [[0, 4], [1, 128]], ALU.is_ge, 0.0, base=0,
                                                channel_multiplier=-1), [wsT_f], [wsT_f])
    K.copy(wsT, wsT_f)
    K.memset(hist, 0.0)
    K.memset(kpad, 0.0)
    for i in range(3):
        K.memset(vaug[i], 1.0)
        K.memset(KT[i], 0.0)
    K.ts(gcq, gcq, 128.0 ** -0.5, None, ALU.mult)
    K.act(sinkexp, sinks_f, AF.Exp)

    ang = K.sb_tile([128, 16, 8], F32, "ang", at=top_norm + 13440)
    posf = K.sb_tile([128, 16], F32, "posf", at=top_norm + 13952)
    rr = K.sb_tile([128, 16, 8], F32, "rr", at=top_norm + 14464)
    rf = K.sb_tile([128, 16, 8], F32, "rf", at=top_norm + 14976)
    ri = K.sb_tile([128, 16, 8], I32, "ri", at=top_norm + 15488)
    mk_ = K.sb_tile([128, 16, 8], F32, "mk_", at=top_norm + 16000)
    K.copy(posf, pos_i)
    half = 8
    inv = (np.float32(500000.0) ** (-(np.arange(half, dtype=np.float32) / np.float32(half)))).astype(np.float32)
    for j in range(half):
        K.ts(ang[:, :, j], posf, float(inv[j]), None, ALU.mult)
    TWO_PI = 6.283185307179586
    for dst, shift in ((sinT, 0.0), (cosT, 0.25)):
        K.ts(rr, ang, 1.0 / TWO_PI, shift, ALU.mult, ALU.add)
        K.copy(ri, rr)
        K.copy(rf, ri)
        K.tt(rr, rr, rf, ALU.subtract)
        K.ts(mk_, rr, 0.5, None, ALU.is_gt)
        K.tt(rr, rr, mk_, ALU.subtract)
        K.ts(mk_, rr, -0.5, None, ALU.is_lt)
        K.tt(rr, rr, mk_, ALU.add)
        K.act(dst, rr, AF.Sin, scale=TWO_PI)


    for p in range(npass):
        t0 = p * T
        if p == 0:
            norm_tiles(X1, gmixT, lambda c: hT[c])
        else:
            norm_tiles(xstage, gmixT, lambda c: hT[c], have_stats=True, prescaled=2)
        if p == 0 and dbg_d is not None and "hT" in DEBUG:
            for c in range(16):
                dump(hT[c], T)

        def gen_a():
            Wu = [slab_k8(w_in_v, kh, 0) for kh in range(2)]
            yield
            for c in range(4):
                bk = bank()
                for k in range(16):
                    K.mm(bk, Wu[k // 8][:, k % 8, c * 128:(c + 1) * 128], hT[k], k == 0, k == 15)
                K.act(uT[c], bk, AF.Gelu_apprx_tanh)
                if c == 3:
                    release(*Wu)
                    Wv = [slab_k8(w_in_v, kh, 512) for kh in range(2)]
                yield
            for i in range(TT):
                bk = bank()
                for k in range(16):
                    K.mm(bk, hT[k][:, i * 128:(i + 1) * 128], Wv[k // 8][:, k % 8, :], k == 0, k == 15)
                g_ = vg[i % 2]
                K.act(g_, bk, AF.Gelu_apprx_tanh)
                K.memset(ssA[:, 0:1], 0.0)
                K.act(tmpA.bitcast(BF16)[:, 0:512], g_, AF.Square, accum_out=ssA[:, 0:1])
                K.ts(ssA[:, 1:2], ssA[:, 0:1], 1.0 / 512, EPS, ALU.mult, ALU.add)
                K.act(ssA[:, 2:3], ssA[:, 1:2], AF.Sqrt)
                K.recip(ssA[:, 3:4], ssA[:, 2:3])
                K.ts(vn[i], g_, ssA[:, 3:4], None, ALU.mult)
                if i == TT - 1:
                    release(*Wv)
                yield
            for g in range(4):
                bk = bank()
                for i in range(TT):
                    K.mm(bk[:, i * 128:(i + 1) * 128], vn[i][:, g * 128:(g + 1) * 128], wsT[:, g, :], True, True,
                         sig=(i == TT - 1))
                K.stt(tmpA.rearrange("p (a b) -> p a b", a=4), bk.rearrange("p (a b) -> p a b", a=4),
                      gavT[:, g:g + 1], bsp[:, g:g + 1, :].broadcast_to([128, 4, 128]), ALU.mult, ALU.add)
                K.tt(yTa[:, g, :], tmpA, uT[g], ALU.mult)
                yield

        def gen_b():
            Wq = [[slab_k8(w_in_v, kh, 1024 + n * 512) for kh in range(2)] for n in range(2)]
            Wkv = slab_k16(w_in_v, 2048)
            yield

            def stage_x(i):
                gi = p * TT + i
                cur = gi % 3
                bq = [bank(), bank()]
                bkv = bank()
                for n in range(2):
                    for k in range(16):
                        K.mm(bq[n], hT[k][:, i * 128:(i + 1) * 128], Wq[n][k // 8][:, k % 8, :], k == 0, k == 15)
                for k in range(16):
                    K.mm(bkv[:, 0:256], hT[k][:, i * 128:(i + 1) * 128], Wkv[:, k, :], k == 0, k == 15)
                for n in range(2):
                    K.act(sq[:, n * 512:(n + 1) * 512], bq[n], AF.Square)
                K.act(sq[:, 1024:1152], bkv[:, 0:128], AF.Square)
                K.act(vaug[cur][:, :, 0:64], bkv[:, 128:256].rearrange("p (g d) -> p g d", g=2), AF.Copy)
                K.reduce(ssB[:, 0:18], sq.rearrange("p (h d) -> p h d", h=18), ALU.add)
                K.ts(ssB[:, 0:18], ssB[:, 0:18], 1.0 / 64, EPS, ALU.mult, ALU.add)
                K.act(rstdB[:, 0:18], ssB[:, 0:18], AF.Sqrt)
                K.recip(rstdB[:, 0:18], rstdB[:, 0:18])
                K.ts(rstdB[:, 0:16], rstdB[:, 0:16], 0.125, None, ALU.mult)
                for n in range(2):
                    K.tt(qn[:, n * 8:(n + 1) * 8, :], bq[n].rearrange("p (h d) -> p h d", h=8),
                         rstdB[:, n * 8:(n + 1) * 8].unsqueeze(2).broadcast_to([128, 8, 64]), ALU.mult)
                K.tt(kn, bkv[:, 0:128].rearrange("p (g d) -> p g d", g=2),
                     rstdB[:, 16:18].unsqueeze(2).broadcast_to([128, 2, 64]), ALU.mult)
                yield
                PE_ = "dve"
                K.tt(qn, qn, gbq.unsqueeze(1).broadcast_to([128, 16, 64]), ALU.mult, eng=PE_)
                K.tt(kn, kn, gbk.unsqueeze(1).broadcast_to([128, 2, 64]), ALU.mult, eng=PE_)
                cs = cosT[:, gi, :].unsqueeze(1)
                sn = sinT[:, gi, :].unsqueeze(1)
                qr3 = qr.rearrange("p (h d) -> p h d", h=16)
                K.copy(qr3[:, :, 16:64], qn[:, :, 16:64], eng=PE_)
                K.tt(rt[0], qn[:, :, 0:8], cs.broadcast_to([128, 16, 8]), ALU.mult, eng=PE_)
                K.tt(rt[1], qn[:, :, 8:16], sn.broadcast_to([128, 16, 8]), ALU.mult, eng=PE_)
                K.tt(qr3[:, :, 0:8], rt[0], rt[1], ALU.subtract, eng=PE_)
                K.tt(rt[2], qn[:, :, 8:16], cs.broadcast_to([128, 16, 8]), ALU.mult, eng=PE_)
                K.tt(rt[3], qn[:, :, 0:8], sn.broadcast_to([128, 16, 8]), ALU.mult, eng=PE_)
                K.tt(qr3[:, :, 8:16], rt[2], rt[3], ALU.add, eng=PE_)
                kd = kpad.rearrange("p (g q) d -> p g q d", g=2)
                d0 = kd[:, :, 0, 0:64]
                K.copy(d0[:, :, 16:64], kn[:, :, 16:64], eng=PE_)
                K.tt(rt[0][:, 0:2, :], kn[:, :, 0:8], cs.broadcast_to([128, 2, 8]), ALU.mult, eng=PE_)
                K.tt(rt[1][:, 0:2, :], kn[:, :, 8:16], sn.broadcast_to([128, 2, 8]), ALU.mult, eng=PE_)
                K.tt(d0[:, :, 0:8], rt[0][:, 0:2, :], rt[1][:, 0:2, :], ALU.subtract, eng=PE_)
                K.tt(rt[2][:, 0:2, :], kn[:, :, 8:16], cs.broadcast_to([128, 2, 8]), ALU.mult, eng=PE_)
                K.tt(rt[3][:, 0:2, :], kn[:, :, 0:8], sn.broadcast_to([128, 2, 8]), ALU.mult, eng=PE_)
                K.tt(d0[:, :, 8:16], rt[2][:, 0:2, :], rt[3][:, 0:2, :], ALU.add, eng=PE_)
                K.copy(kd[:, :, 1, 64:128], d0, eng=PE_)
                yield
                bk = bank()
                bv = bf(bk, [128, 4, 128])
                for j in range(4):
                    K.tr(bv[:, j, :], kpad[:, j, :], ident, sig=(j == 3))
                K.copy(KT[cur], bv)
                bk = bank()
                bv = bf(bk, [128, 8, 128])
                for j in range(8):
                    K.tr(bv[:, j, :], qr[:, j * 128:(j + 1) * 128], ident, sig=(j == 7))
                K.act(qT[i % 2].rearrange("p (a b) -> p a b", a=8), bv, AF.Copy)
                yield

            def stage_y(i):
                gi = p * TT + i
                cur, prv = gi % 3, (gi - 1) % 3
                qT_ = qT[i % 2]
                blocks = ([(prv, mask_prev)] if gi > 0 else []) + [(cur, mask_cur)]
                pt_of = {}
                pi = 0
                for g in range(2):
                    for (kb, msk) in blocks:
                        for par in range(2):
                            bs = bank()
                            K.mm(bs, KT[kb][:, 2 * g + par, :], qT_[:, g * 512:(g + 1) * 512], True, False)
                            K.mm(bs, ident, msk.rearrange("p a b -> p (a b)"), False, True)
                            ptile = PT[pi]
                            pi += 1
                            K.act(ptile, bs, AF.Exp)
                            pt_of[(g, kb, par)] = ptile
                    yield
                ob = [bank() for _ in range(4)]
                for h in range(16):
                    g, jj, par = h // 8, (h % 8) // 2, h % 2
                    o = ob[h // 4].rearrange("p (a b) -> p a b", a=4)[:, h % 4, 0:65]
                    for bi, (kb, msk) in enumerate(blocks):
                        K.mm(o, pt_of[(g, kb, par)][:, jj * 128:(jj + 1) * 128], vaug[kb][:, g, :],
                             bi == 0, bi == len(blocks) - 1, sig=(bi == len(blocks) - 1 and h % 4 == 3))
                for b4 in range(4):
                    o3 = ob[b4].rearrange("p (a b) -> p a b", a=4)
                    K.tt(den[:, b4 * 4:(b4 + 1) * 4], o3[:, :, 64], sinkexp[:, b4 * 4:(b4 + 1) * 4], ALU.add)
                K.recip(den, den)
                for b4 in range(4):
                    o3 = ob[b4].rearrange("p (a b) -> p a b", a=4)
                    K.tt(yb[:, b4 * 256:(b4 + 1) * 256].rearrange("p (a b) -> p a b", a=4), o3[:, :, 0:64],
                         den[:, b4 * 4:(b4 + 1) * 4].unsqueeze(2).broadcast_to([128, 4, 64]), ALU.mult)
                yield
                bk = bank()
                bv = bf(bk, [128, 8, 128])
                for j in range(8):
                    K.tr(bv[:, j, :], yb[:, j * 128:(j + 1) * 128], ident, sig=(j == 7))
                K.act(yTb[:, :, i * 128:(i + 1) * 128], bv, AF.Copy)
                yield

            for kind, i in (("x", 0), ("x", 1), ("y", 0), ("x", 2), ("y", 1), ("x", 3), ("y", 2), ("y", 3)):
                yield from (stage_x(i) if kind == "x" else stage_y(i))
                if (kind, i) == ("x", 3):
                    release(Wq[0][0], Wq[0][1], Wq[1][0], Wq[1][1], Wkv)

        def gen_c():
            Wqc = [slab_k8(w_in_v, kh, 2304, xslot=kh) for kh in range(2)]
            yield
            for h in range(4):
                bqc = banks[7]
                for k in range(16):
                    K.mm(bqc, Wqc[k // 8][:, k % 8, h * 128:(h + 1) * 128], hT[k], k == 0, k == 15)
                K.act(sqc, bqc, AF.Square)
                yield
                bss = bank()
                K.mm(bss, ones_bf, sqc, True, True)
                K.ts(msc, bss, 1.0 / 128, EPS, ALU.mult, ALU.add)
                K.act(msc, msc, AF.Sqrt)
                K.recip(msc, msc)
                K.stt(qcT, bqc, gcq[:, 0:1], msc, ALU.mult, ALU.mult)
                yield
                for mt in range(2):
                    bs = bank()
                    K.mm(bs, kTmem[:, h, mt * 128:(mt + 1) * 128], qcT, True, True)
                    K.act(PTc[mt], bs, AF.Exp)
                yield
                bo, bd = bank(), bank()
                for mt in range(2):
                    K.mm(bo, vmem[:, mt, h * 128:(h + 1) * 128], PTc[mt], mt == 0, mt == 1)
                for mt in range(2):
                    K.mm(bd, ones_bf, PTc[mt], mt == 0, mt == 1)
                K.recip(rdc, bd)
                K.tt(yTc[:, h, :], bo, rdc, ALU.mult)
                if h == 3:
                    release(*Wqc)
                yield

        ga, gb, gc = gen_a(), gen_b(), gen_c()
        gens = [gb, gc, ga]
        c_started = True
        while gens:
            for g_ in list(gens):
                try:
                    next(g_)
                except StopIteration:
                    gens.remove(g_)
                    if g_ is ga and not c_started:
                        gens.append(gc)
                        c_started = True
        if p == 0 and dbg_d is not None:
            if "yTa" in DEBUG:
                for c in range(4):
                    dump(yTa[:, c, :], T)
            if "yTb" in DEBUG:
                for c in range(8):
                    dump(yTb[:, c, :], T)
            if "yTc" in DEBUG:
                for c in range(4):
                    dump(yTc[:, c, :], T)

        for np_ in range(8):
            Gs = [slab_k16(w_in_v, G0 + gidx * D + np_ * 256) for gidx in range(3)]
            c0 = np_ * 256
            BR = load_slab([(0, wba_v, 0, 4, c0, c0 + 256),
                            (1024, wbb_v, 0, 8, c0, c0 + 256),
                            (3072, wbc_v, 0, 4, c0, c0 + 256)]).rearrange("p (a b) -> p a b", a=16)
            ysrc = [(yTa, 0, 4), (yTb, 4, 8), (yTc, 12, 4)]
            for nn in range(2):
                n = np_ * 2 + nn
                cols = slice(nn * 128, (nn + 1) * 128)
                tms = []
                for gidx in range(3):
                    bg = bank()
                    for k in range(16):
                        K.mm(bg, Gs[gidx][:, k, cols], hT[k], k == 0, k == 15)
                    s_ = sg[(n * 3 + gidx) % 6]
                    K.act(s_, bg, AF.Sigmoid)
                    yt, boff, nk = ysrc[gidx]
                    bb = bank()
                    for kc in range(nk):
                        K.mm(bb, BR[:, boff + kc, cols], yt[:, kc, :], kc == 0, kc == nk - 1)
                    t_ = tm[gidx]
                    K.tt(t_, bb, s_, ALU.mult)
                    tms.append(t_)
                K.tt(tm[3], tms[0], tms[1], ALU.add)
                K.tt(mT[n], tm[3], tms[2], ALU.add)
            release(Gs[0], Gs[1], Gs[2], BR)
        if p == 0 and dbg_d is not None and "mT" in DEBUG:
            for c in range(16):
                dump(mT[c], T)

        for m in range(4):
            Wo = [slab_k8(wo_v, kh, m * 512) for kh in range(2)]
            for i in range(TT):
                bk = bank()
                for k in range(16):
                    K.mm(bk, mT[k][:, i * 128:(i + 1) * 128], Wo[k // 8][:, k % 8, :], k == 0, k == 15)
                K.tt(X1m[i][m], X1m[i][m], bk, ALU.add)
                if m == 3:
                    if i == 0:
                        K.memset(ss[:, 0:TT], 0.0)
                    norm_sumsq(i, X1[i])
            release(*Wo)
        if p == 0 and dbg_d is not None and "x1" in DEBUG:
            for i in range(TT):
                dump(X1[i], D)

        norm_rstd(TT)
        norm_tiles(X1, gffnT, lambda c: hT[c], have_stats=True)

        def f_up(fc, ab):
            aT = actT[fc % 2]
            U = [slab_k8(wup_v, kh, ab * DFF + fc * 512) for kh in range(2)]
            for c in range(4):
                cg = ab * 44 + fc * 4 + c
                bk = bank()
                for k in range(16):
                    K.mm(bk, U[k // 8][:, k % 8, c * 128:(c + 1) * 128], hT[k], k == 0, k == 15)
                x_ = xs[c % 2]
                a_ = acc[c % 2]
                K.copy(x_[:, 0:2], hist[:, cg, :])
                K.act(x_[:, 2:514], bk, AF.Copy)
                K.copy(hist[:, cg, :], x_[:, 512:514])
                K.act(a_, bk, AF.Identity, bias=convb[:, cg:cg + 1], scale=convw[:, 2, cg:cg + 1])
                K.stt(a_, x_[:, 1:513], convw[:, 1, cg:cg + 1], a_, ALU.mult, ALU.add)
                K.stt(a_, x_[:, 0:512], convw[:, 0, cg:cg + 1], a_, ALU.mult, ALU.add)
                if ab == 0:
                    K.act(sa[c], a_, AF.Silu)
                else:
                    K.tt(aT[c], sa[c], a_, ALU.mult)
            release(*U)

        def f_down(fc):
            aT = actT[fc % 2]
            Dn = [load_slab([(0, wdn_v, fc * 4 + 2 * j, fc * 4 + 2 * j + 2, 0, 2048)]).rearrange(
                "p (a b) -> p a b", a=2) for j in range(2)]
            for i in range(TT):
                for m in range(4):
                    bk = bank()
                    for c in range(4):
                        K.mm(bk, aT[c][:, i * 128:(i + 1) * 128], Dn[c // 2][:, c % 2, m * 512:(m + 1) * 512],
                             c == 0, c == 3)
                    K.tt(X1m[i][m], X1m[i][m], bk, ALU.add)
            release(*Dn)

        f_up(0, 0)
        f_up(0, 1)
        if p + 1 < npass:
            for i in range(TT):
                K.dma("sp", xstage[i], x_d[t0 + T + i * 128:t0 + T + (i + 1) * 128, :], sem_xs[i])
        for fc in range(NFC):
            if fc + 1 < NFC:
                f_up(fc + 1, 0)
            if fc == NFC - 1 and p + 1 < npass:
                K.memset(ss[:, 0:TT], 0.0)
                for j in range(TT):
                    norm_sumsq(j, xstage[j])
                norm_rstd(TT)
                norm_scale(0, xstage[0])
                norm_scale(1, xstage[1])
            f_down(fc)
            if fc + 1 < NFC:
                f_up(fc + 1, 1)
        for i in range(TT):
            K.dma("sp", out_d[t0 + i * 128:t0 + (i + 1) * 128, :], X1[i], sem_x[i])
            if p + 1 < npass:
                K.dma("sp", X1[i], xstage[i], sem_x[i])

    toks = [(sem_x[i], K.dma_cum[id(sem_x[i])]) for i in range(TT)]
    if dbg_d is not None and id(sem_dbg) in K.dma_cum:
        toks.append((sem_dbg, K.dma_cum[id(sem_dbg)]))
    K.wait_all("sp", toks)
    print("instructions", K.n_inst, "waits", K.n_wait, "slabs", slab_i[0], "dbg cols", dbg_off[0])
    assert len(plan) == NSLAB_TOTAL, len(plan)
    nc.mk_plan = plan
    return nc


_PLAN = []


def pack_weights(inputs, plan):
    wp = np.zeros((len(plan), 128, SLAB_EL), dtype=np.float32)
    views = {}
    for si, parts in enumerate(plan):
        for off, name, k0, k1, c0, c1 in parts:
            if name not in views:
                w = np.asarray(inputs[name][0], dtype=np.float32)
                views[name] = w.reshape(w.shape[0] // 128, 128, w.shape[1])
            blk = views[name][k0:k1, :, c0:c1]
            n = (k1 - k0) * (c1 - c0)
            wp[si, :, off:off + n] = np.transpose(blk, (1, 0, 2)).reshape(128, n)
    return wp


def prep_inputs(inputs, plan):
    f = lambda a: np.ascontiguousarray(np.asarray(a, dtype=np.float32))

    def rep(v, n):
        return np.ascontiguousarray(np.broadcast_to(f(v).reshape(1, n), (128, n)))

    def colT(v, nchunk):
        return np.ascontiguousarray(f(v).reshape(nchunk, 128).T)
    shared = {
        "gmixT": colT(inputs["g_mix"][0], 16),
        "gffnT": colT(inputs["g_ffn"][0], 16),
        "gmemT": colT(inputs["g_mem"][0], 16),
        "gavT": colT(inputs["g_a_v"][0], 4),
        "wsT": np.ascontiguousarray(np.transpose(f(inputs["w_spatial"][0]), (2, 0, 1))),
        "bsp": rep(inputs["b_spatial"][0], 512),
        "gbq": rep(inputs["g_b_q"][0], 64),
        "gbk": rep(inputs["g_b_k"][0], 64),
        "sinks": rep(inputs["sinks"][0], 16),
        "gcq": f(inputs["g_c_q"][0]).reshape(128, 1),
        "gck": rep(inputs["g_c_k"][0], 128),
        "convw": np.ascontiguousarray(np.transpose(f(inputs["conv_w"][0]).reshape(3, 88, 128), (2, 0, 1))),
        "convb": colT(inputs["conv_b"][0], 88),
    }
    shared["wpack"] = pack_weights(inputs, plan)
    x = np.asarray(inputs["x"], dtype=np.float32)
    mem = np.asarray(inputs["mem"], dtype=np.float32)
    pos = np.asarray(inputs["positions"], dtype=np.int32)
    maps = []
    for b in range(x.shape[0]):
        m = dict(shared)
        m["x"] = np.ascontiguousarray(x[b])
        m["mem"] = np.ascontiguousarray(mem[b])
        m["pos_t"] = np.ascontiguousarray(pos[b].reshape(16, 128).T)
        maps.append(m)
    return maps


def kernel(**inputs):
    nc = build()
    maps = prep_inputs(inputs, nc.mk_plan)
    res = run_bass_kernel_spmd(nc, maps, core_ids=list(range(len(maps))))
    return np.stack([np.asarray(r["out"], dtype=np.float32) for r in res.results], axis=0)
```

```python
import numpy as np
import concourse.bass as bass
import concourse.mybir as mybir

F32 = mybir.dt.float32
BF16 = mybir.dt.bfloat16
I32 = mybir.dt.int32
AF = mybir.ActivationFunctionType
ALU = mybir.AluOpType
AX = mybir.AxisListType
DSZ = {F32: 4, BF16: 2, I32: 4}
PAGE = 256


class Region:
    __slots__ = ("space", "lo", "hi", "last_w", "readers", "name")

    def __init__(self, space, lo, hi, name=""):
        self.space, self.lo, self.hi, self.name = space, lo, hi, name
        self.last_w = {}
        self.readers = {}


class View:
    __slots__ = ("tile", "ap")

    def __init__(self, tile, ap):
        self.tile, self.ap = tile, ap

    def __getitem__(self, key):
        return View(self.tile, self.ap[key])

    def bitcast(self, dt):
        return View(self.tile, self.ap.bitcast(dt))

    def rearrange(self, s, **kw):
        return View(self.tile, self.ap.rearrange(s, **kw))

    def broadcast_to(self, shape):
        return View(self.tile, self.ap.broadcast_to(list(shape)))

    def unsqueeze(self, axis):
        return View(self.tile, self.ap.unsqueeze(axis))


class Tile(View):
    __slots__ = ("region", "shape", "dtype")

    def __init__(self, K, space, base_ap, lo, hi, shape, dtype, name=""):
        self.tile = self
        self.ap = base_ap
        self.region = Region(space, lo, hi, name)
        self.shape, self.dtype = shape, dtype
        K._register(self.region)


class Eng:
    def __init__(self, name, h, sem, every):
        self.name, self.h, self.sem, self.every = name, h, sem, every
        self.count = 0
        self.known = {}


class MK:
    def __init__(self, nc, sb_bytes=200 * 1024):
        self.nc = nc
        self.sb_bytes = sb_bytes
        self.sb = nc.alloc_sbuf_tensor("mk_sb", [128, sb_bytes // 4], F32)
        self.ps = nc.alloc_psum_tensor("mk_ps", [128, 8 * 512], F32)
        self.sb_ap = self.sb.ap() if hasattr(self.sb, "ap") else self.sb[:]
        self.ps_ap = self.ps.ap() if hasattr(self.ps, "ap") else self.ps[:]
        self.pages = {"sb": {}, "ps": {}}
        self._sems = []
        self.eng = {}
        for name, h, every in (("pe", nc.tensor, False), ("act", nc.scalar, True),
                               ("dve", nc.vector, True), ("pool", nc.gpsimd, True),
                               ("sp", nc.sync, True)):
            self.eng[name] = Eng(name, h, self.new_sem("c_" + name), every)
        self.dma_cum = {}
        self.sb_top = 0
        self.n_wait = 0
        self.n_inst = 0

    def new_sem(self, name):
        s = self.nc.alloc_semaphore(name)
        self._sems.append(s)
        return s

    def sb_tile(self, shape, dtype, name="", at=None):
        n = int(np.prod(shape[1:])) * DSZ[dtype]
        n = (n + 31) // 32 * 32
        if at is None:
            at = self.sb_top
            self.sb_top += n
        assert at + n <= self.sb_bytes, f"SBUF overflow {name} {at + n}"
        ap = self.sb_ap[0:shape[0], at // 4:(at + n) // 4]
        if dtype != F32:
            ap = ap.bitcast(dtype)
        nel = int(np.prod(shape[1:]))
        ap = ap[:, 0:nel]
        if len(shape) > 2:
            names = " ".join(f"d{i}" for i in range(len(shape) - 1))
            ap = ap.rearrange(f"p ({names}) -> p {names}",
                              **{f"d{i}": shape[i + 1] for i in range(len(shape) - 2)})
        return Tile(self, "sb", ap, at, at + n, list(shape), dtype, name)

    def ps_tile(self, bank, nbanks=1, shape=None, dtype=F32, name=""):
        lo = bank * 2048
        hi = (bank + nbanks) * 2048
        ap = self.ps_ap[:, bank * 512:(bank + nbanks) * 512]
        if dtype != F32:
            ap = ap.bitcast(dtype)
        if shape is not None:
            nel = int(np.prod(shape[1:]))
            ap = ap[0:shape[0], 0:nel]
            if len(shape) > 2:
                names = " ".join(f"d{i}" for i in range(len(shape) - 1))
                ap = ap.rearrange(f"p ({names}) -> p {names}",
                                  **{f"d{i}": shape[i + 1] for i in range(len(shape) - 2)})
        return Tile(self, "ps", ap, lo, hi, shape, dtype, name)

    def _register(self, r):
        pg = self.pages[r.space]
        for p in range(r.lo // PAGE, (r.hi - 1) // PAGE + 1):
            pg.setdefault(p, []).append(r)

    def _overlaps(self, r):
        pg = self.pages[r.space]
        seen = {id(r): r}
        for p in range(r.lo // PAGE, (r.hi - 1) // PAGE + 1):
            for g in pg.get(p, ()):
                if id(g) not in seen and g.lo < r.hi and r.lo < g.hi:
                    seen[id(g)] = g
        return seen.values()

    def _deps(self, reads, writes, own=None):
        need = {}

        def add(tok):
            k = id(tok[0])
            if k not in need or need[k][1] < tok[1]:
                need[k] = tok
        for t in reads:
            for g in self._overlaps(t.region):
                for tok in g.last_w.values():
                    add(tok)
                if g.space == "ps":
                    for k, tok in g.readers.items():
                        if k != own:
                            add(tok)
        for t in writes:
            for g in self._overlaps(t.region):
                for tok in g.last_w.values():
                    add(tok)
                for tok in g.readers.values():
                    add(tok)
        return need

    def _wait(self, e, need):
        for k, (sem, val) in need.items():
            if sem is e.sem and e.name == "pe":
                continue
            if e.known.get(k, 0) >= val:
                continue
            e.h.wait_ge(sem, val)
            e.known[k] = val
            self.n_wait += 1

    def _commit(self, tok, reads, writes):
        k = id(tok[0])
        for t in reads:
            r = t.region
            if k not in r.readers or r.readers[k][1] < tok[1]:
                r.readers[k] = tok
        for t in writes:
            r = t.region
            for g in self._overlaps(r):
                if g is not r and g.lo >= r.lo and g.hi <= r.hi:
                    g.last_w = {}
                    g.readers = {}
            r.last_w = {k: tok}
            r.readers = {}

    @staticmethod
    def _split(args):
        tiles, aps = [], []
        for a in args:
            if isinstance(a, View):
                tiles.append(a.tile)
                aps.append(a.ap)
            else:
                aps.append(a)
        return tiles, aps

    def op(self, engname, fn, outs, ins, sig=None, extra_reads=(), extra_writes=()):
        e = self.eng[engname]
        wt, wa = self._split(outs)
        rt, ra = self._split(ins)
        rt = rt + [v.tile for v in extra_reads]
        wt = wt + [v.tile for v in extra_writes]
        self._wait(e, self._deps(rt, wt, id(e.sem)))
        inst = fn(e.h, *wa, *ra)
        self.n_inst += 1
        if sig is None:
            sig = e.every
        if sig:
            e.count += 1
            inst.then_inc(e.sem, 1)
            tok = (e.sem, e.count)
        else:
            tok = (e.sem, e.count + 1)
        self._commit(tok, rt, wt)
        return inst

    def dma(self, qname, out, in_, sem, **kw):
        e = self.eng[qname]
        wt, wa = self._split([out])
        rt, ra = self._split([in_])
        self._wait(e, self._deps(rt, wt))
        inst = e.h.dma_start(out=wa[0], in_=ra[0], **kw)
        inst.then_inc(sem, 16)
        self.n_inst += 1
        k = id(sem)
        self.dma_cum[k] = self.dma_cum.get(k, 0) + 16
        tok = (sem, self.dma_cum[k])
        self._commit(tok, rt, wt)
        return tok

    def retoken(self, tiles, sem):
        tok = (sem, self.dma_cum[id(sem)])
        for t in tiles:
            t.region.last_w = {id(sem): tok}

    def wait_all(self, engname, toks):
        e = self.eng[engname]
        need = {}
        for tok in toks:
            k = id(tok[0])
            if k not in need or need[k][1] < tok[1]:
                need[k] = tok
        self._wait(e, need)

    def mm(self, out, lhsT, rhs, start, stop, sig=None, **kw):
        return self.op("pe", lambda h, o, l, r: h.matmul(o, l, r, start=start, stop=stop, **kw),
                       [out], [lhsT, rhs], sig=stop if sig is None else sig)

    def tr(self, out, in_, ident, sig=False):
        return self.op("pe", lambda h, o, i, d: h.transpose(o, i, d), [out], [in_, ident], sig=sig)

    def act(self, out, in_, func, bias=None, scale=None, accum_out=None, eng="act"):
        ins = [in_]
        kw = {}
        order = []
        if bias is not None:
            if isinstance(bias, View):
                ins.append(bias); order.append("bias")
            else:
                kw["bias"] = bias
        if scale is not None:
            if isinstance(scale, View):
                ins.append(scale); order.append("scale")
            else:
                kw["scale"] = scale
        outs = [out]
        if accum_out is not None:
            outs.append(accum_out)

        def fn(h, *a):
            o = a[0]
            i0 = 1
            acc = None
            if accum_out is not None:
                acc = a[1]; i0 = 2
            k2 = dict(kw)
            for j, nm in enumerate(order):
                k2[nm] = a[i0 + 1 + j]
            if acc is not None:
                k2["accum_out"] = acc
            return h.activation(o, a[i0], func, **k2)
        return self.op(eng, fn, outs, ins)

    def tt(self, out, in0, in1, op, eng="dve"):
        return self.op(eng, lambda h, o, a, b: h.tensor_tensor(o, a, b, op), [out], [in0, in1])

    def ts(self, out, in0, s1, s2, op0, op1=None, accum_out=None, eng="dve"):
        ins = [in0]
        idx = {}
        if isinstance(s1, View):
            idx["s1"] = len(ins); ins.append(s1)
        if isinstance(s2, View):
            idx["s2"] = len(ins); ins.append(s2)
        outs = [out] + ([accum_out] if accum_out is not None else [])
        no = len(outs)

        def fn(h, *a):
            o = a[0]
            i = a[no:]
            v1 = i[idx["s1"]] if "s1" in idx else s1
            v2 = i[idx["s2"]] if "s2" in idx else s2
            kw = {}
            if op1 is not None:
                kw["op1"] = op1
            if accum_out is not None:
                kw["accum_out"] = a[1]
            return h.tensor_scalar(o, i[0], v1, v2, op0, **kw)
        return self.op(eng, fn, outs, ins)

    def stt(self, out, in0, scalar, in1, op0, op1, eng="dve"):
        ins = [in0, in1]
        if isinstance(scalar, View):
            ins.append(scalar)

        def fn(h, o, a, b, *s):
            return h.scalar_tensor_tensor(o, a, s[0] if s else scalar, b, op0, op1)
        return self.op(eng, fn, [out], ins)

    def copy(self, out, in_, eng="dve"):
        return self.op(eng, lambda h, o, i: h.tensor_copy(o, i), [out], [in_])

    def memset(self, out, val, eng="dve"):
        return self.op(eng, lambda h, o: h.memset(o, val), [out], [])

    def reduce(self, out, in_, op, axis=AX.X, eng="dve"):
        return self.op(eng, lambda h, o, i: h.tensor_reduce(o, i, axis, op), [out], [in_])

    def recip(self, out, in_):
        return self.op("dve", lambda h, o, i: h.reciprocal(o, i), [out], [in_])


from contextlib import ExitStack
from concourse.bass_utils import run_bass_kernel_spmd

D = 2048
S = 2048
T = 512
NPASS = S // T
TT = T // 128
MEM = 256
DFF = 5632
NFC = DFF // 512
EPS = 1e-6
C_IN = 8960
G0 = 2816
NSLAB = 7
SLAB_EL = 4096
PER_PASS = 117
NSLAB_TOTAL = 4 + PER_PASS

DEBUG = {}


def build(npass=NPASS, dbg=None):
    nc = bass.Bass("TRN2", target_bir_lowering=False)

    def din(name, shape, dt=F32):
        return nc.dram_tensor(name, list(shape), dt, kind="ExternalInput").ap()

    x_d = din("x", [S, D])
    mem_d = din("mem", [MEM, D])
    pos_d = din("pos_t", [128, 16], I32)
    gmix_d = din("gmixT", [128, 16])
    gffn_d = din("gffnT", [128, 16])
    gmem_d = din("gmemT", [128, 16])
    wpack_d = din("wpack", [NSLAB_TOTAL, 128, SLAB_EL])
    gav_d = din("gavT", [128, 4])
    wsT_d = din("wsT", [128, 4, 128])
    bsp_d = din("bsp", [128, 512])
    gbq_d = din("gbq", [128, 64])
    gbk_d = din("gbk", [128, 64])
    sinks_d = din("sinks", [128, 16])
    gcq_d = din("gcq", [128, 1])
    gck_d = din("gck", [128, 128])
    cw_d = din("convw", [128, 3, 88])
    cb_d = din("convb", [128, 88])
    out_d = nc.dram_tensor("out", [S, D], F32, kind="ExternalOutput").ap()
    dbg_d = None
    if dbg is not None:
        dbg_d = nc.dram_tensor("dbg", [128, dbg], F32, kind="ExternalOutput").ap()

    K = MK(nc, 206 * 1024)
    sem_c = K.new_sem("const")
    sem_x = [K.new_sem(f"x{i}") for i in range(TT)]
    sem_slab = [K.new_sem(f"slab{i}") for i in range(NSLAB)]
    sem_slabx = [K.new_sem(f"slabx{i}") for i in range(2)]
    sem_xs = [K.new_sem(f"xs{i}") for i in range(TT)]
    sem_dbg = K.new_sem("dbg")

    X1 = [K.sb_tile([128, D], F32, f"x1_{i}") for i in range(TT)]
    X1m = [[K.sb_tile([128, 512], F32, f"x1_{i}_{m}", at=X1[i].region.lo + m * 2048) for m in range(4)]
           for i in range(TT)]
    hT = [K.sb_tile([128, T], BF16, f"hT{c}") for c in range(16)]
    yTa = K.sb_tile([128, 4, T], BF16, "yTa")
    yTb = K.sb_tile([128, 8, T], BF16, "yTb")
    yTc = K.sb_tile([128, 4, T], BF16, "yTc")
    mT = [K.sb_tile([128, T], BF16, f"mT{c}") for c in range(16)]
    slabs = [K.sb_tile([128, SLAB_EL], BF16, f"slab{i}") for i in range(NSLAB)]
    xstage = [K.sb_tile([128, D], F32, f"xstage{i}", at=yTa.region.lo + i * 8192) for i in range(TT)]
    assert xstage[-1].region.hi <= mT[15].region.hi
    slabs_x = [K.sb_tile([128, SLAB_EL], BF16, f"slabx{i}", at=mT[8 * i].region.lo) for i in range(2)]
    ident = K.sb_tile([128, 128], BF16, "ident")
    gmixT = K.sb_tile([128, 16], F32, "gmixT")
    gffnT = K.sb_tile([128, 16], F32, "gffnT")
    gmemT = K.sb_tile([128, 16], F32, "gmemT")
    gavT = K.sb_tile([128, 4], F32, "gavT")
    gcq = K.sb_tile([128, 1], F32, "gcq")
    convw = K.sb_tile([128, 3, 88], F32, "convw")
    convb = K.sb_tile([128, 88], F32, "convb")
    hist = K.sb_tile([128, 88, 2], F32, "hist")
    wsT = K.sb_tile([128, 4, 128], BF16, "wsT")
    bsp = K.sb_tile([128, 4, 128], F32, "bsp")
    gbq = K.sb_tile([128, 64], F32, "gbq")
    gbk = K.sb_tile([128, 64], F32, "gbk")
    gck = K.sb_tile([128, 128], F32, "gck")
    sinkexp = K.sb_tile([128, 16], F32, "sinkexp")
    cosT = K.sb_tile([128, 16, 8], F32, "cosT")
    sinT = K.sb_tile([128, 16, 8], F32, "sinT")
    mask_cur = K.sb_tile([128, 4, 128], BF16, "mask_cur")
    mask_prev = K.sb_tile([128, 4, 128], BF16, "mask_prev")
    ones_bf = K.sb_tile([128, 128], BF16, "ones_bf")
    kTmem = K.sb_tile([128, 4, MEM], BF16, "kTmem")
    vmem = K.sb_tile([128, 2, 512], BF16, "vmem")
    KT = [K.sb_tile([128, 4, 128], BF16, f"KT{i}") for i in range(3)]
    vaug = [K.sb_tile([128, 2, 65], BF16, f"vaug{i}") for i in range(3)]
    kpad = K.sb_tile([128, 4, 128], BF16, "kpad")
    ss = K.sb_tile([128, 32], F32, "ss")
    rstd = K.sb_tile([128, 32], F32, "rstd")
    ssA = K.sb_tile([128, 8], F32, "ssA")
    ssB = K.sb_tile([128, 32], F32, "ssB")
    rstdB = K.sb_tile([128, 32], F32, "rstdB")

    base = K.sb_top
    xn = [K.sb_tile([128, D], BF16, f"xn{i}") for i in range(2)]
    junk = xn[1]
    top_norm = K.sb_top
    xs = [K.sb_tile([128, 516], F32, f"xs{i}") for i in range(2)]
    acc = [K.sb_tile([128, 512], F32, f"acc{i}") for i in range(2)]
    sa = [K.sb_tile([128, 512], BF16, f"sa{c}") for c in range(4)]
    actT = [[K.sb_tile([128, 512], BF16, f"actT{j}_{c}") for c in range(4)] for j in range(2)]
    top_f = K.sb_top
    K.sb_top = top_norm
    sg = [K.sb_tile([128, 512], BF16, f"sg{i}") for i in range(6)]
    tm = [K.sb_tile([128, 512], F32, f"tm{i}") for i in range(4)]
    top_m = K.sb_top
    K.sb_top = base
    uT = [K.sb_tile([128, T], BF16, f"uT{c}") for c in range(4)]
    vn = [K.sb_tile([128, 512], BF16, f"vn{i}") for i in range(TT)]
    vg = [K.sb_tile([128, 512], F32, f"vg{i}") for i in range(2)]
    tmpA = K.sb_tile([128, 512], F32, "tmpA")
    sq = K.sb_tile([128, 1024 + 128], F32, "sq")
    qn = K.sb_tile([128, 16, 64], F32, "qn", at=sq.region.lo)
    kn = K.sb_tile([128, 2, 64], F32, "kn")
    rt = [K.sb_tile([128, 16, 8], F32, f"rt{i}") for i in range(4)]
    qr = K.sb_tile([128, 1024], BF16, "qr")
    yb = K.sb_tile([128, 1024], BF16, "yb", at=qr.region.lo)
    qT = [K.sb_tile([128, 1024], BF16, f"qT{i}") for i in range(2)]
    PT = [K.sb_tile([128, 512], BF16, f"PT{i}") for i in range(8)]
    den = K.sb_tile([128, 16], F32, "den")
    sqc = K.sb_tile([128, 512], BF16, "sqc")
    msc = K.sb_tile([128, 512], F32, "msc")
    qcT = K.sb_tile([128, 512], BF16, "qcT")
    PTc = [K.sb_tile([128, 512], BF16, f"PTc{i}") for i in range(2)]
    rdc = K.sb_tile([128, 512], F32, "rdc")
    top_abc = K.sb_top
    K.sb_top = max(top_f, top_m, top_abc)
    print("SBUF bytes used per partition:", K.sb_top)

    banks = [K.ps_tile(b, 1, [128, 512], F32, f"bank{b}") for b in range(8)]
    bank_i = [0]

    def bank():
        b = banks[bank_i[0] % 7]
        bank_i[0] += 1
        return b

    def bf(bk, shape):
        v = bk.bitcast(BF16)
        n = int(np.prod(shape[1:]))
        v = v[:, 0:n]
        if len(shape) == 3:
            v = v.rearrange("p (a b) -> p a b", a=shape[1])
        return v

    slab_i = [0]

    plan = []
    plan_idx = {}
    free_slots = list(range(NSLAB))

    def load_slab(parts, xslot=None):
        if xslot is None:
            assert free_slots, "slab ring exhausted"
            idx = free_slots.pop(0)
        key = tuple(tuple(p_) for p_ in parts)
        if key not in plan_idx:
            plan_idx[key] = len(plan)
            plan.append(list(parts))
        sidx = plan_idx[key]
        slab_i[0] += 1
        if xslot is not None:
            sl = slabs_x[xslot]
            K.dma("pool", sl, wpack_d[sidx], sem_slabx[xslot])
            return sl
        sl = slabs[idx]
        K.dma("pool", sl, wpack_d[sidx], sem_slab[idx])
        return sl

    def release(*views):
        for v in views:
            if v.tile in slabs_x:
                continue
            idx = slabs.index(v.tile)
            assert idx not in free_slots
            free_slots.append(idx)

    w_in_v, wmkv_v, wo_v, wup_v, wdn_v = "w_in", "w_mem_kv", "w_out", "w_up", "w_down"
    wba_v, wbb_v, wbc_v = "w_branch_a", "w_branch_b", "w_branch_c"

    def slab_k8(view, kh, c0, xslot=None):
        sl = load_slab([(0, view, kh * 8, (kh + 1) * 8, c0, c0 + 512)], xslot=xslot)
        return sl.rearrange("p (a b) -> p a b", a=8)

    def slab_k16(view, c0):
        sl = load_slab([(0, view, 0, 16, c0, c0 + 256)])
        return sl.rearrange("p (a b) -> p a b", a=16)

    dbg_off = [0]
    dbg_st = []
    dbg_n = [0]

    def dump(v, n):
        if dbg_d is None:
            return
        if not dbg_st:
            dbg_st.extend(K.sb_tile([128, 512], F32, f"dbgst{j}") for j in range(2))
        for c0 in range(0, n, 512):
            st = dbg_st[dbg_n[0] % 2]
            dbg_n[0] += 1
            K.copy(st, v[:, c0:c0 + 512])
            K.dma("sp", dbg_d[:, dbg_off[0]:dbg_off[0] + 512], st, sem_dbg)
            dbg_off[0] += 512

    for mt in range(2):
        K.dma("sp", xstage[mt], mem_d[mt * 128:(mt + 1) * 128, :], sem_xs[mt])
    cl = []
    for t, src in ((gmixT, gmix_d), (gffnT, gffn_d), (gmemT, gmem_d), (gavT, gav_d), (gcq, gcq_d),
                   (convw, cw_d), (convb, cb_d)):
        K.dma("sp", t, src, sem_c)
        cl.append(t)
    wsT_f = K.sb_tile([128, 4, 128], F32, "wsT_f", at=top_norm + 11264)
    pos_i = K.sb_tile([128, 16], I32, "pos_i", at=top_norm + 13312)
    sinks_f = K.sb_tile([128, 16], F32, "sinks_f", at=top_norm + 13376)
    K.dma("sp", wsT_f, wsT_d, sem_c)
    K.dma("sp", pos_i, pos_d, sem_c)
    K.dma("sp", bsp.rearrange("p a b -> p (a b)"), bsp_d, sem_c)
    K.dma("sp", gbq, gbq_d, sem_c)
    K.dma("sp", gbk, gbk_d, sem_c)
    K.dma("sp", gck, gck_d, sem_c)
    K.dma("sp", sinks_f, sinks_d, sem_c)
    K.retoken(cl + [wsT_f, pos_i, bsp, gbq, gbk, gck, sinks_f], sem_c)

    for i in range(TT):
        K.dma("sp", X1[i], x_d[i * 128:(i + 1) * 128, :], sem_x[i])
    K.memset(ident, 1.0, eng="pool")
    K.op("pool", lambda h, o, i: h.affine_select(o, i, [[-1, 128]], ALU.is_equal, 0.0, base=0,
                                                channel_multiplier=1), [ident], [ident])
    wk = [[slab_k8(wmkv_v, kh, n * 512) for kh in range(2)] for n in range(2)]
    def norm_sumsq(j, src):
        if j % 2 == 0:
            K.act(junk, src, AF.Square, accum_out=ss[:, j:j + 1])
        else:
            K.op("dve", lambda h, o, acc_, a_, b_: h.scalar_tensor_tensor(o, a_, 1.0, b_, ALU.mult, ALU.mult,
                                                                          accum_out=acc_),
                 [xn[0], ss[:, j:j + 1]], [src, src])

    def norm_rstd(n):
        K.ts(ss[:, 8:8 + n], ss[:, 0:n], 1.0 / D, EPS, ALU.mult, ALU.add)
        K.act(ss[:, 16:16 + n], ss[:, 8:8 + n], AF.Sqrt)
        K.recip(rstd[:, 0:n], ss[:, 16:16 + n])

    def norm_scale(j, src):
        if j % 2 == 0:
            K.act(xn[j % 2], src, AF.Identity, scale=rstd[:, j:j + 1])
        else:
            K.ts(xn[j % 2], src, rstd[:, j:j + 1], None, ALU.mult)

    def norm_transpose(j, gT, dstT):
        x_ = xn[j % 2]
        for hb in range(2):
            bk = bank()
            bv = bf(bk, [128, 8, 128])
            for jj in range(8):
                c = hb * 8 + jj
                K.tr(bv[:, jj, :], x_[:, c * 128:(c + 1) * 128], ident, sig=(jj == 7))
            for jj in range(8):
                c = hb * 8 + jj
                if hb == 0:
                    K.ts(dstT(c)[:, j * 128:(j + 1) * 128], bv[:, jj, :], gT[:, c:c + 1], None, ALU.mult)
                else:
                    K.act(dstT(c)[:, j * 128:(j + 1) * 128], bv[:, jj, :], AF.Identity, scale=gT[:, c:c + 1])

    def norm_tiles(srcs, gT, dstT, have_stats=False, prescaled=0):
        n = len(srcs)
        if not have_stats:
            K.memset(ss[:, 0:n], 0.0)
            for j, src in enumerate(srcs):
                norm_sumsq(j, src)
            norm_rstd(n)
        for j, src in enumerate(srcs):
            if j >= prescaled:
                norm_scale(j, src)
            norm_transpose(j, gT, dstT)

    memT = K.sb_tile([128, 16, MEM], BF16, "memT", at=top_norm)
    norm_tiles([xstage[0], xstage[1]], gmemT, lambda c: memT[:, c, :])
    kfm = K.sb_tile([128, 512], F32, "kfm", at=top_norm + 8192)
    kbm = K.sb_tile([128, 512], BF16, "kbm", at=top_norm + 10240)
    for mt in range(2):
        bk_k, bk_v = bank(), bank()
        for n, bk in ((0, bk_k), (1, bk_v)):
            for k in range(16):
                K.mm(bk, memT[:, k, mt * 128:(mt + 1) * 128], wk[n][k // 8][:, k % 8, :], k == 0, k == 15)
        K.copy(vmem[:, mt, :], bk_v)
        K.act(kfm, bk_k, AF.Square)
        K.reduce(ss[:, 4:8], kfm.rearrange("p (h d) -> p h d", h=4), ALU.add)
        K.ts(ss[:, 8:12], ss[:, 4:8], 1.0 / 128, EPS, ALU.mult, ALU.add)
        K.act(ss[:, 12:16], ss[:, 8:12], AF.Sqrt)
        K.recip(rstd[:, 4:8], ss[:, 12:16])
        K.tt(kfm.rearrange("p (h d) -> p h d", h=4), bk_k.rearrange("p (h d) -> p h d", h=4),
             rstd[:, 4:8].unsqueeze(2).broadcast_to([128, 4, 128]), ALU.mult)
        K.tt(kbm.rearrange("p (h d) -> p h d", h=4), kfm.rearrange("p (h d) -> p h d", h=4),
             gck.unsqueeze(1).broadcast_to([128, 4, 128]), ALU.mult)
        bk = bank()
        bv = bf(bk, [128, 4, 128])
        for h in range(4):
            K.tr(bv[:, h, :], kbm[:, h * 128:(h + 1) * 128], ident, sig=(h == 3))
        K.copy(kTmem[:, :, mt * 128:(mt + 1) * 128], bv)
    release(wk[0][0], wk[0][1], wk[1][0], wk[1][1])

    K.memset(mask_cur, 0.0, eng="pool")
    K.op("pool", lambda h, o, i: h.affine_select(o, i, [[0, 4], [1, 128]], ALU.is_ge, -30000.0, base=0,
                                                channel_multiplier=-1), [mask_cur], [mask_cur])
    K.memset(mask_prev, 0.0, eng="pool")
    K.op("pool", lambda h, o, i: h.affine_select(o, i, [[0, 4], [-1, 128]], ALU.is_gt, -30000.0, base=0,
                                                channel_multiplier=1), [mask_prev], [mask_prev])
    K.memset(ones_bf, 1.0, eng="pool")
    K.op("pool", lambda h, o, i: h.affine_select(o, i, [[0, 4], [1, 128]], ALU.is_ge, 0.0, base=0,
                                                channel_multiplier=-1), [wsT_f], [wsT_f])
    K.copy(wsT, wsT_f)
    K.memset(hist, 0.0)
    K.memset(kpad, 0.0)
    for i in range(3):
        K.memset(vaug[i], 1.0)
        K.memset(KT[i], 0.0)
    K.ts(gcq, gcq, 128.0 ** -0.5, None, ALU.mult)
    K.act(sinkexp, sinks_f, AF.Exp)

    ang = K.sb_tile([128, 16, 8], F32, "ang", at=top_norm + 13440)
    posf = K.sb_tile([128, 16], F32, "posf", at=top_norm + 13952)
    rr = K.sb_tile([128, 16, 8], F32, "rr", at=top_norm + 14464)
    rf = K.sb_tile([128, 16, 8], F32, "rf", at=top_norm + 14976)
    ri = K.sb_tile([128, 16, 8], I32, "ri", at=top_norm + 15488)
    mk_ = K.sb_tile([128, 16, 8], F32, "mk_", at=top_norm + 16000)
    K.copy(posf, pos_i)
    half = 8
    inv = (np.float32(500000.0) ** (-(np.arange(half, dtype=np.float32) / np.float32(half)))).astype(np.float32)
    for j in range(half):
        K.ts(ang[:, :, j], posf, float(inv[j]), None, ALU.mult)
    TWO_PI = 6.283185307179586
    for dst, shift in ((sinT, 0.0), (cosT, 0.25)):
        K.ts(rr, ang, 1.0 / TWO_PI, shift, ALU.mult, ALU.add)
        K.copy(ri, rr)
        K.copy(rf, ri)
        K.tt(rr, rr, rf, ALU.subtract)
        K.ts(mk_, rr, 0.5, None, ALU.is_gt)
        K.tt(rr, rr, mk_, ALU.subtract)
        K.ts(mk_, rr, -0.5, None, ALU.is_lt)
        K.tt(rr, rr, mk_, ALU.add)
        K.act(dst, rr, AF.Sin, scale=TWO_PI)


    pre_b = []
    for p in range(npass):
        t0 = p * T
        if p == 0:
            norm_tiles(X1, gmixT, lambda c: hT[c])
        else:
            norm_tiles(xstage, gmixT, lambda c: hT[c], have_stats=True, prescaled=2)
        if p == 0 and dbg_d is not None and "hT" in DEBUG:
            for c in range(16):
                dump(hT[c], T)

        def gen_a():
            Wu = [slab_k8(w_in_v, kh, 0) for kh in range(2)]
            yield
            for c in range(4):
                bk = bank()
                for k in range(16):
                    K.mm(bk, Wu[k // 8][:, k % 8, c * 128:(c + 1) * 128], hT[k], k == 0, k == 15)
                K.act(uT[c], bk, AF.Gelu_apprx_tanh)
                if c == 3:
                    release(*Wu)
                    Wv = [slab_k8(w_in_v, kh, 512) for kh in range(2)]
                yield
            for i in range(TT):
                bk = bank()
                for k in range(16):
                    K.mm(bk, hT[k][:, i * 128:(i + 1) * 128], Wv[k // 8][:, k % 8, :], k == 0, k == 15)
                g_ = vg[i % 2]
                K.act(g_, bk, AF.Gelu_apprx_tanh)
                K.memset(ssA[:, 0:1], 0.0)
                K.act(tmpA.bitcast(BF16)[:, 0:512], g_, AF.Square, accum_out=ssA[:, 0:1])
                K.ts(ssA[:, 1:2], ssA[:, 0:1], 1.0 / 512, EPS, ALU.mult, ALU.add)
                K.act(ssA[:, 2:3], ssA[:, 1:2], AF.Sqrt)
                K.recip(ssA[:, 3:4], ssA[:, 2:3])
                K.ts(vn[i], g_, ssA[:, 3:4], None, ALU.mult)
                if i == TT - 1:
                    release(*Wv)
                yield
            for g in range(4):
                bk = bank()
                for i in range(TT):
                    K.mm(bk[:, i * 128:(i + 1) * 128], vn[i][:, g * 128:(g + 1) * 128], wsT[:, g, :], True, True,
                         sig=(i == TT - 1))
                K.stt(tmpA.rearrange("p (a b) -> p a b", a=4), bk.rearrange("p (a b) -> p a b", a=4),
                      gavT[:, g:g + 1], bsp[:, g:g + 1, :].broadcast_to([128, 4, 128]), ALU.mult, ALU.add)
                K.tt(yTa[:, g, :], tmpA, uT[g], ALU.mult)
                yield

        def load_b():
            return ([[slab_k8(w_in_v, kh, 1024 + n * 512) for kh in range(2)] for n in range(2)],
                    slab_k16(w_in_v, 2048))

        def gen_b():
            Wq, Wkv = pre_b.pop() if pre_b else load_b()
            yield

            def stage_x(i):
                gi = p * TT + i
                cur = gi % 3
                bq = [bank(), bank()]
                bkv = bank()
                for n in range(2):
                    for k in range(16):
                        K.mm(bq[n], hT[k][:, i * 128:(i + 1) * 128], Wq[n][k // 8][:, k % 8, :], k == 0, k == 15)
                for k in range(16):
                    K.mm(bkv[:, 0:256], hT[k][:, i * 128:(i + 1) * 128], Wkv[:, k, :], k == 0, k == 15)
                for n in range(2):
                    K.act(sq[:, n * 512:(n + 1) * 512], bq[n], AF.Square)
                K.act(sq[:, 1024:1152], bkv[:, 0:128], AF.Square)
                K.act(vaug[cur][:, :, 0:64], bkv[:, 128:256].rearrange("p (g d) -> p g d", g=2), AF.Copy)
                K.reduce(ssB[:, 0:18], sq.rearrange("p (h d) -> p h d", h=18), ALU.add)
                K.ts(ssB[:, 0:18], ssB[:, 0:18], 1.0 / 64, EPS, ALU.mult, ALU.add)
                K.act(rstdB[:, 0:18], ssB[:, 0:18], AF.Sqrt)
                K.recip(rstdB[:, 0:18], rstdB[:, 0:18])
                K.ts(rstdB[:, 0:16], rstdB[:, 0:16], 0.125, None, ALU.mult)
                for n in range(2):
                    K.tt(qn[:, n * 8:(n + 1) * 8, :], bq[n].rearrange("p (h d) -> p h d", h=8),
                         rstdB[:, n * 8:(n + 1) * 8].unsqueeze(2).broadcast_to([128, 8, 64]), ALU.mult)
                K.tt(kn, bkv[:, 0:128].rearrange("p (g d) -> p g d", g=2),
                     rstdB[:, 16:18].unsqueeze(2).broadcast_to([128, 2, 64]), ALU.mult)
                yield
                PE_ = "dve"
                K.tt(qn, qn, gbq.unsqueeze(1).broadcast_to([128, 16, 64]), ALU.mult, eng=PE_)
                K.tt(kn, kn, gbk.unsqueeze(1).broadcast_to([128, 2, 64]), ALU.mult, eng=PE_)
                cs = cosT[:, gi, :].unsqueeze(1)
                sn = sinT[:, gi, :].unsqueeze(1)
                qr3 = qr.rearrange("p (h d) -> p h d", h=16)
                K.copy(qr3[:, :, 16:64], qn[:, :, 16:64], eng=PE_)
                K.tt(rt[0], qn[:, :, 0:8], cs.broadcast_to([128, 16, 8]), ALU.mult, eng=PE_)
                K.tt(rt[1], qn[:, :, 8:16], sn.broadcast_to([128, 16, 8]), ALU.mult, eng=PE_)
                K.tt(qr3[:, :, 0:8], rt[0], rt[1], ALU.subtract, eng=PE_)
                K.tt(rt[2], qn[:, :, 8:16], cs.broadcast_to([128, 16, 8]), ALU.mult, eng=PE_)
                K.tt(rt[3], qn[:, :, 0:8], sn.broadcast_to([128, 16, 8]), ALU.mult, eng=PE_)
                K.tt(qr3[:, :, 8:16], rt[2], rt[3], ALU.add, eng=PE_)
                kd = kpad.rearrange("p (g q) d -> p g q d", g=2)
                d0 = kd[:, :, 0, 0:64]
                K.copy(d0[:, :, 16:64], kn[:, :, 16:64], eng=PE_)
                K.tt(rt[0][:, 0:2, :], kn[:, :, 0:8], cs.broadcast_to([128, 2, 8]), ALU.mult, eng=PE_)
                K.tt(rt[1][:, 0:2, :], kn[:, :, 8:16], sn.broadcast_to([128, 2, 8]), ALU.mult, eng=PE_)
                K.tt(d0[:, :, 0:8], rt[0][:, 0:2, :], rt[1][:, 0:2, :], ALU.subtract, eng=PE_)
                K.tt(rt[2][:, 0:2, :], kn[:, :, 8:16], cs.broadcast_to([128, 2, 8]), ALU.mult, eng=PE_)
                K.tt(rt[3][:, 0:2, :], kn[:, :, 0:8], sn.broadcast_to([128, 2, 8]), ALU.mult, eng=PE_)
                K.tt(d0[:, :, 8:16], rt[2][:, 0:2, :], rt[3][:, 0:2, :], ALU.add, eng=PE_)
                K.copy(kd[:, :, 1, 64:128], d0, eng=PE_)
                yield
                bk = bank()
                bv = bf(bk, [128, 4, 128])
                for j in range(4):
                    K.tr(bv[:, j, :], kpad[:, j, :], ident, sig=(j == 3))
                K.copy(KT[cur], bv)
                bk = bank()
                bv = bf(bk, [128, 8, 128])
                for j in range(8):
                    K.tr(bv[:, j, :], qr[:, j * 128:(j + 1) * 128], ident, sig=(j == 7))
                K.act(qT[i % 2].rearrange("p (a b) -> p a b", a=8), bv, AF.Copy)
                yield

            def stage_y(i):
                gi = p * TT + i
                cur, prv = gi % 3, (gi - 1) % 3
                qT_ = qT[i % 2]
                blocks = ([(prv, mask_prev)] if gi > 0 else []) + [(cur, mask_cur)]
                pt_of = {}
                pi = 0
                for g in range(2):
                    for (kb, msk) in blocks:
                        for par in range(2):
                            bs = bank()
                            K.mm(bs, KT[kb][:, 2 * g + par, :], qT_[:, g * 512:(g + 1) * 512], True, False)
                            K.mm(bs, ident, msk.rearrange("p a b -> p (a b)"), False, True)
                            ptile = PT[pi]
                            pi += 1
                            K.act(ptile, bs, AF.Exp)
                            pt_of[(g, kb, par)] = ptile
                    yield
                ob = [bank() for _ in range(4)]
                for h in range(16):
                    g, jj, par = h // 8, (h % 8) // 2, h % 2
                    o = ob[h // 4].rearrange("p (a b) -> p a b", a=4)[:, h % 4, 0:65]
                    for bi, (kb, msk) in enumerate(blocks):
                        K.mm(o, pt_of[(g, kb, par)][:, jj * 128:(jj + 1) * 128], vaug[kb][:, g, :],
                             bi == 0, bi == len(blocks) - 1, sig=(bi == len(blocks) - 1 and h % 4 == 3))
                for b4 in range(4):
                    o3 = ob[b4].rearrange("p (a b) -> p a b", a=4)
                    K.tt(den[:, b4 * 4:(b4 + 1) * 4], o3[:, :, 64], sinkexp[:, b4 * 4:(b4 + 1) * 4], ALU.add)
                K.recip(den, den)
                for b4 in range(4):
                    o3 = ob[b4].rearrange("p (a b) -> p a b", a=4)
                    K.tt(yb[:, b4 * 256:(b4 + 1) * 256].rearrange("p (a b) -> p a b", a=4), o3[:, :, 0:64],
                         den[:, b4 * 4:(b4 + 1) * 4].unsqueeze(2).broadcast_to([128, 4, 64]), ALU.mult)
                yield
                bk = bank()
                bv = bf(bk, [128, 8, 128])
                for j in range(8):
                    K.tr(bv[:, j, :], yb[:, j * 128:(j + 1) * 128], ident, sig=(j == 7))
                K.act(yTb[:, :, i * 128:(i + 1) * 128], bv, AF.Copy)
                yield

            for kind, i in (("x", 0), ("x", 1), ("y", 0), ("x", 2), ("y", 1), ("x", 3), ("y", 2), ("y", 3)):
                yield from (stage_x(i) if kind == "x" else stage_y(i))
                if (kind, i) == ("x", 3):
                    release(Wq[0][0], Wq[0][1], Wq[1][0], Wq[1][1], Wkv)

        def gen_c():
            Wqc = [slab_k8(w_in_v, kh, 2304, xslot=kh) for kh in range(2)]
            yield
            for h in range(4):
                bqc = banks[7]
                for k in range(16):
                    K.mm(bqc, Wqc[k // 8][:, k % 8, h * 128:(h + 1) * 128], hT[k], k == 0, k == 15)
                K.act(sqc, bqc, AF.Square)
                yield
                bss = bank()
                K.mm(bss, ones_bf, sqc, True, True)
                K.ts(msc, bss, 1.0 / 128, EPS, ALU.mult, ALU.add)
                K.act(msc, msc, AF.Sqrt)
                K.recip(msc, msc)
                K.stt(qcT, bqc, gcq[:, 0:1], msc, ALU.mult, ALU.mult)
                yield
                for mt in range(2):
                    bs = bank()
                    K.mm(bs, kTmem[:, h, mt * 128:(mt + 1) * 128], qcT, True, True)
                    K.act(PTc[mt], bs, AF.Exp)
                yield
                bo, bd = bank(), bank()
                for mt in range(2):
                    K.mm(bo, vmem[:, mt, h * 128:(h + 1) * 128], PTc[mt], mt == 0, mt == 1)
                for mt in range(2):
                    K.mm(bd, ones_bf, PTc[mt], mt == 0, mt == 1)
                K.recip(rdc, bd)
                K.tt(yTc[:, h, :], bo, rdc, ALU.mult)
                if h == 3:
                    release(*Wqc)
                yield

        ga, gb, gc = gen_a(), gen_b(), gen_c()
        gens = [gb, ga, gc]
        c_started = True
        while gens:
            for g_ in list(gens):
                try:
                    next(g_)
                except StopIteration:
                    gens.remove(g_)
                    if g_ is ga and not c_started:
                        gens.append(gc)
                        c_started = True
        if p == 0 and dbg_d is not None:
            if "yTa" in DEBUG:
                for c in range(4):
                    dump(yTa[:, c, :], T)
            if "yTb" in DEBUG:
                for c in range(8):
                    dump(yTb[:, c, :], T)
            if "yTc" in DEBUG:
                for c in range(4):
                    dump(yTc[:, c, :], T)

        for np_ in range(8):
            Gs = [slab_k16(w_in_v, G0 + gidx * D + np_ * 256) for gidx in range(3)]
            c0 = np_ * 256
            BR = load_slab([(0, wba_v, 0, 4, c0, c0 + 256),
                            (1024, wbb_v, 0, 8, c0, c0 + 256),
                            (3072, wbc_v, 0, 4, c0, c0 + 256)]).rearrange("p (a b) -> p a b", a=16)
            ysrc = [(yTa, 0, 4), (yTb, 4, 8), (yTc, 12, 4)]
            for nn in range(2):
                n = np_ * 2 + nn
                cols = slice(nn * 128, (nn + 1) * 128)
                tms = []
                for gidx in range(3):
                    bg = bank()
                    for k in range(16):
                        K.mm(bg, Gs[gidx][:, k, cols], hT[k], k == 0, k == 15)
                    s_ = sg[(n * 3 + gidx) % 6]
                    K.act(s_, bg, AF.Sigmoid)
                    yt, boff, nk = ysrc[gidx]
                    bb = bank()
                    for kc in range(nk):
                        K.mm(bb, BR[:, boff + kc, cols], yt[:, kc, :], kc == 0, kc == nk - 1)
                    t_ = tm[gidx]
                    K.tt(t_, bb, s_, ALU.mult)
                    tms.append(t_)
                K.tt(tm[3], tms[0], tms[1], ALU.add)
                K.tt(mT[n], tm[3], tms[2], ALU.add)
            release(Gs[0], Gs[1], Gs[2], BR)
        if p == 0 and dbg_d is not None and "mT" in DEBUG:
            for c in range(16):
                dump(mT[c], T)

        for m in range(4):
            Wo = [slab_k8(wo_v, kh, m * 512) for kh in range(2)]
            for i in range(TT):
                bk = bank()
                for k in range(16):
                    K.mm(bk, mT[k][:, i * 128:(i + 1) * 128], Wo[k // 8][:, k % 8, :], k == 0, k == 15)
                K.tt(X1m[i][m], X1m[i][m], bk, ALU.add)
                if m == 3:
                    if i == 0:
                        K.memset(ss[:, 0:TT], 0.0)
                    norm_sumsq(i, X1[i])
            release(*Wo)
        if p == 0 and dbg_d is not None and "x1" in DEBUG:
            for i in range(TT):
                dump(X1[i], D)

        norm_rstd(TT)
        norm_tiles(X1, gffnT, lambda c: hT[c], have_stats=True)

        def f_up(fc, ab):
            aT = actT[fc % 2]
            U = [slab_k8(wup_v, kh, ab * DFF + fc * 512) for kh in range(2)]
            for c in range(4):
                cg = ab * 44 + fc * 4 + c
                bk = bank()
                for k in range(16):
                    K.mm(bk, U[k // 8][:, k % 8, c * 128:(c + 1) * 128], hT[k], k == 0, k == 15)
                x_ = xs[c % 2]
                a_ = acc[c % 2]
                K.copy(x_[:, 0:2], hist[:, cg, :])
                K.act(x_[:, 2:514], bk, AF.Copy)
                K.copy(hist[:, cg, :], x_[:, 512:514])
                K.act(a_, bk, AF.Identity, bias=convb[:, cg:cg + 1], scale=convw[:, 2, cg:cg + 1])
                K.stt(a_, x_[:, 1:513], convw[:, 1, cg:cg + 1], a_, ALU.mult, ALU.add)
                K.stt(a_, x_[:, 0:512], convw[:, 0, cg:cg + 1], a_, ALU.mult, ALU.add)
                if ab == 0:
                    K.act(sa[c], a_, AF.Silu)
                else:
                    K.tt(aT[c], sa[c], a_, ALU.mult)
            release(*U)

        def load_dn(fc):
            return [load_slab([(0, wdn_v, fc * 4 + 2 * j, fc * 4 + 2 * j + 2, 0, 2048)]).rearrange(
                "p (a b) -> p a b", a=2) for j in range(2)]

        def f_down(fc, Dn=None):
            aT = actT[fc % 2]
            if Dn is None:
                Dn = load_dn(fc)
            for i in range(TT):
                for m in range(4):
                    bk = bank()
                    for c in range(4):
                        K.mm(bk, aT[c][:, i * 128:(i + 1) * 128], Dn[c // 2][:, c % 2, m * 512:(m + 1) * 512],
                             c == 0, c == 3)
                    K.tt(X1m[i][m], X1m[i][m], bk, ALU.add)
            release(*Dn)

        f_up(0, 0)
        f_up(0, 1)
        if p + 1 < npass:
            for i in range(TT):
                K.dma("sp", xstage[i], x_d[t0 + T + i * 128:t0 + T + (i + 1) * 128, :], sem_xs[i])
        for fc in range(NFC):
            if fc + 1 < NFC:
                f_up(fc + 1, 0)
            if fc == NFC - 1 and p + 1 < npass:
                K.memset(ss[:, 0:TT], 0.0)
                for j in range(TT):
                    norm_sumsq(j, xstage[j])
                norm_rstd(TT)
                norm_scale(0, xstage[0])
                norm_scale(1, xstage[1])
                dn_last = load_dn(fc)
                pre_b.append(load_b())
                f_down(fc, dn_last)
                continue
            f_down(fc)
            if fc + 1 < NFC:
                f_up(fc + 1, 1)
        for i in range(TT):
            K.dma("sp", out_d[t0 + i * 128:t0 + (i + 1) * 128, :], X1[i], sem_x[i])
        if p + 1 < npass:
            for i in (2, 3, 0, 1):
                K.dma("sp", X1[i], xstage[i], sem_x[i])

    toks = [(sem_x[i], K.dma_cum[id(sem_x[i])]) for i in range(TT)]
    if dbg_d is not None and id(sem_dbg) in K.dma_cum:
        toks.append((sem_dbg, K.dma_cum[id(sem_dbg)]))
    K.wait_all("sp", toks)
    print("instructions", K.n_inst, "waits", K.n_wait, "slabs", slab_i[0], "dbg cols", dbg_off[0])
    assert len(plan) == NSLAB_TOTAL, len(plan)
    nc.mk_plan = plan
    return nc


_PLAN = []


def pack_weights(inputs, plan):
    wp = np.zeros((len(plan), 128, SLAB_EL), dtype=np.float32)
    views = {}
    for si, parts in enumerate(plan):
        for off, name, k0, k1, c0, c1 in parts:
            if name not in views:
                w = np.asarray(inputs[name][0], dtype=np.float32)
                views[name] = w.reshape(w.shape[0] // 128, 128, w.shape[1])
            blk = views[name][k0:k1, :, c0:c1]
            n = (k1 - k0) * (c1 - c0)
            wp[si, :, off:off + n] = np.transpose(blk, (1, 0, 2)).reshape(128, n)
    return wp


def prep_inputs(inputs, plan):
    f = lambda a: np.ascontiguousarray(np.asarray(a, dtype=np.float32))

    def rep(v, n):
        return np.ascontiguousarray(np.broadcast_to(f(v).reshape(1, n), (128, n)))

    def colT(v, nchunk):
        return np.ascontiguousarray(f(v).reshape(nchunk, 128).T)
    shared = {
        "gmixT": colT(inputs["g_mix"][0], 16),
        "gffnT": colT(inputs["g_ffn"][0], 16),
        "gmemT": colT(inputs["g_mem"][0], 16),
        "gavT": colT(inputs["g_a_v"][0], 4),
        "wsT": np.ascontiguousarray(np.transpose(f(inputs["w_spatial"][0]), (2, 0, 1))),
        "bsp": rep(inputs["b_spatial"][0], 512),
        "gbq": rep(inputs["g_b_q"][0], 64),
        "gbk": rep(inputs["g_b_k"][0], 64),
        "sinks": rep(inputs["sinks"][0], 16),
        "gcq": f(inputs["g_c_q"][0]).reshape(128, 1),
        "gck": rep(inputs["g_c_k"][0], 128),
        "convw": np.ascontiguousarray(np.transpose(f(inputs["conv_w"][0]).reshape(3, 88, 128), (2, 0, 1))),
        "convb": colT(inputs["conv_b"][0], 88),
    }
    shared["wpack"] = pack_weights(inputs, plan)
    x = np.asarray(inputs["x"], dtype=np.float32)
    mem = np.asarray(inputs["mem"], dtype=np.float32)
    pos = np.asarray(inputs["positions"], dtype=np.int32)
    maps = []
    for b in range(x.shape[0]):
        m = dict(shared)
        m["x"] = np.ascontiguousarray(x[b])
        m["mem"] = np.ascontiguousarray(mem[b])
        m["pos_t"] = np.ascontiguousarray(pos[b].reshape(16, 128).T)
        maps.append(m)
    return maps


def kernel(**inputs):
    nc = build()
    maps = prep_inputs(inputs, nc.mk_plan)
    res = run_bass_kernel_spmd(nc, maps, core_ids=list(range(len(maps))))
    return np.stack([np.asarray(r["out"], dtype=np.float32) for r in res.results], axis=0)
```

```python
import numpy as np
import concourse.bass as bass
import concourse.mybir as mybir

F32 = mybir.dt.float32
BF16 = mybir.dt.bfloat16
I32 = mybir.dt.int32
AF = mybir.ActivationFunctionType
ALU = mybir.AluOpType
AX = mybir.AxisListType
DSZ = {F32: 4, BF16: 2, I32: 4}
PAGE = 256


class Region:
    __slots__ = ("space", "lo", "hi", "last_w", "readers", "name")

    def __init__(self, space, lo, hi, name=""):
        self.space, self.lo, self.hi, self.name = space, lo, hi, name
        self.last_w = {}
        self.readers = {}


class View:
    __slots__ = ("tile", "ap")

    def __init__(self, tile, ap):
        self.tile, self.ap = tile, ap

    def __getitem__(self, key):
        return View(self.tile, self.ap[key])

    def bitcast(self, dt):
        return View(self.tile, self.ap.bitcast(dt))

    def rearrange(self, s, **kw):
        return View(self.tile, self.ap.rearrange(s, **kw))

    def broadcast_to(self, shape):
        return View(self.tile, self.ap.broadcast_to(list(shape)))

    def unsqueeze(self, axis):
        return View(self.tile, self.ap.unsqueeze(axis))


class Tile(View):
    __slots__ = ("region", "shape", "dtype")

    def __init__(self, K, space, base_ap, lo, hi, shape, dtype, name=""):
        self.tile = self
        self.ap = base_ap
        self.region = Region(space, lo, hi, name)
        self.shape, self.dtype = shape, dtype
        K._register(self.region)


class Eng:
    def __init__(self, name, h, sem, every):
        self.name, self.h, self.sem, self.every = name, h, sem, every
        self.count = 0
        self.known = {}


class MK:
    def __init__(self, nc, sb_bytes=200 * 1024):
        self.nc = nc
        self.sb_bytes = sb_bytes
        self.sb = nc.alloc_sbuf_tensor("mk_sb", [128, sb_bytes // 4], F32)
        self.ps = nc.alloc_psum_tensor("mk_ps", [128, 8 * 512], F32)
        self.sb_ap = self.sb.ap() if hasattr(self.sb, "ap") else self.sb[:]
        self.ps_ap = self.ps.ap() if hasattr(self.ps, "ap") else self.ps[:]
        self.pages = {"sb": {}, "ps": {}}
        self._sems = []
        self.eng = {}
        for name, h, every in (("pe", nc.tensor, False), ("act", nc.scalar, True),
                               ("dve", nc.vector, True), ("pool", nc.gpsimd, True),
                               ("sp", nc.sync, True)):
            self.eng[name] = Eng(name, h, self.new_sem("c_" + name), every)
        self.dma_cum = {}
        self.sb_top = 0
        self.n_wait = 0
        self.n_inst = 0

    def new_sem(self, name):
        s = self.nc.alloc_semaphore(name)
        self._sems.append(s)
        return s

    def sb_tile(self, shape, dtype, name="", at=None):
        n = int(np.prod(shape[1:])) * DSZ[dtype]
        n = (n + 31) // 32 * 32
        if at is None:
            at = self.sb_top
            self.sb_top += n
        assert at + n <= self.sb_bytes, f"SBUF overflow {name} {at + n}"
        ap = self.sb_ap[0:shape[0], at // 4:(at + n) // 4]
        if dtype != F32:
            ap = ap.bitcast(dtype)
        nel = int(np.prod(shape[1:]))
        ap = ap[:, 0:nel]
        if len(shape) > 2:
            names = " ".join(f"d{i}" for i in range(len(shape) - 1))
            ap = ap.rearrange(f"p ({names}) -> p {names}",
                              **{f"d{i}": shape[i + 1] for i in range(len(shape) - 2)})
        return Tile(self, "sb", ap, at, at + n, list(shape), dtype, name)

    def ps_tile(self, bank, nbanks=1, shape=None, dtype=F32, name=""):
        lo = bank * 2048
        hi = (bank + nbanks) * 2048
        ap = self.ps_ap[:, bank * 512:(bank + nbanks) * 512]
        if dtype != F32:
            ap = ap.bitcast(dtype)
        if shape is not None:
            nel = int(np.prod(shape[1:]))
            ap = ap[0:shape[0], 0:nel]
            if len(shape) > 2:
                names = " ".join(f"d{i}" for i in range(len(shape) - 1))
                ap = ap.rearrange(f"p ({names}) -> p {names}",
                                  **{f"d{i}": shape[i + 1] for i in range(len(shape) - 2)})
        return Tile(self, "ps", ap, lo, hi, shape, dtype, name)

    def _register(self, r):
        pg = self.pages[r.space]
        for p in range(r.lo // PAGE, (r.hi - 1) // PAGE + 1):
            pg.setdefault(p, []).append(r)

    def _overlaps(self, r):
        pg = self.pages[r.space]
        seen = {id(r): r}
        for p in range(r.lo // PAGE, (r.hi - 1) // PAGE + 1):
            for g in pg.get(p, ()):
                if id(g) not in seen and g.lo < r.hi and r.lo < g.hi:
                    seen[id(g)] = g
        return seen.values()

    def _deps(self, reads, writes, own=None):
        need = {}

        def add(tok):
            k = id(tok[0])
            if k not in need or need[k][1] < tok[1]:
                need[k] = tok
        for t in reads:
            for g in self._overlaps(t.region):
                for tok in g.last_w.values():
                    add(tok)
                if g.space == "ps":
                    for k, tok in g.readers.items():
                        if k != own:
                            add(tok)
        for t in writes:
            for g in self._overlaps(t.region):
                for tok in g.last_w.values():
                    add(tok)
                for tok in g.readers.values():
                    add(tok)
        return need

    def _wait(self, e, need):
        for k, (sem, val) in need.items():
            if sem is e.sem and e.name == "pe":
                continue
            if e.known.get(k, 0) >= val:
                continue
            e.h.wait_ge(sem, val)
            e.known[k] = val
            self.n_wait += 1

    def _commit(self, tok, reads, writes):
        k = id(tok[0])
        for t in reads:
            r = t.region
            if k not in r.readers or r.readers[k][1] < tok[1]:
                r.readers[k] = tok
        for t in writes:
            r = t.region
            for g in self._overlaps(r):
                if g is not r and g.lo >= r.lo and g.hi <= r.hi:
                    g.last_w = {}
                    g.readers = {}
            r.last_w = {k: tok}
            r.readers = {}

    @staticmethod
    def _split(args):
        tiles, aps = [], []
        for a in args:
            if isinstance(a, View):
                tiles.append(a.tile)
                aps.append(a.ap)
            else:
                aps.append(a)
        return tiles, aps

    def op(self, engname, fn, outs, ins, sig=None, extra_reads=(), extra_writes=()):
        e = self.eng[engname]
        wt, wa = self._split(outs)
        rt, ra = self._split(ins)
        rt = rt + [v.tile for v in extra_reads]
        wt = wt + [v.tile for v in extra_writes]
        self._wait(e, self._deps(rt, wt, id(e.sem)))
        inst = fn(e.h, *wa, *ra)
        self.n_inst += 1
        if sig is None:
            sig = e.every
        if sig:
            e.count += 1
            inst.then_inc(e.sem, 1)
            tok = (e.sem, e.count)
        else:
            tok = (e.sem, e.count + 1)
        self._commit(tok, rt, wt)
        return inst

    def dma(self, qname, out, in_, sem, **kw):
        e = self.eng[qname]
        wt, wa = self._split([out])
        rt, ra = self._split([in_])
        self._wait(e, self._deps(rt, wt))
        inst = e.h.dma_start(out=wa[0], in_=ra[0], **kw)
        inst.then_inc(sem, 16)
        self.n_inst += 1
        k = id(sem)
        self.dma_cum[k] = self.dma_cum.get(k, 0) + 16
        tok = (sem, self.dma_cum[k])
        self._commit(tok, rt, wt)
        return tok

    def retoken(self, tiles, sem):
        tok = (sem, self.dma_cum[id(sem)])
        for t in tiles:
            t.region.last_w = {id(sem): tok}

    def wait_all(self, engname, toks):
        e = self.eng[engname]
        need = {}
        for tok in toks:
            k = id(tok[0])
            if k not in need or need[k][1] < tok[1]:
                need[k] = tok
        self._wait(e, need)

    def mm(self, out, lhsT, rhs, start, stop, sig=None, **kw):
        return self.op("pe", lambda h, o, l, r: h.matmul(o, l, r, start=start, stop=stop, **kw),
                       [out], [lhsT, rhs], sig=stop if sig is None else sig)

    def tr(self, out, in_, ident, sig=False):
        return self.op("pe", lambda h, o, i, d: h.transpose(o, i, d), [out], [in_, ident], sig=sig)

    def act(self, out, in_, func, bias=None, scale=None, accum_out=None, eng="act"):
        ins = [in_]
        kw = {}
        order = []
        if bias is not None:
            if isinstance(bias, View):
                ins.append(bias); order.append("bias")
            else:
                kw["bias"] = bias
        if scale is not None:
            if isinstance(scale, View):
                ins.append(scale); order.append("scale")
            else:
                kw["scale"] = scale
        outs = [out]
        if accum_out is not None:
            outs.append(accum_out)

        def fn(h, *a):
            o = a[0]
            i0 = 1
            acc = None
            if accum_out is not None:
                acc = a[1]; i0 = 2
            k2 = dict(kw)
            for j, nm in enumerate(order):
                k2[nm] = a[i0 + 1 + j]
            if acc is not None:
                k2["accum_out"] = acc
            return h.activation(o, a[i0], func, **k2)
        return self.op(eng, fn, outs, ins)

    def tt(self, out, in0, in1, op, eng="dve"):
        return self.op(eng, lambda h, o, a, b: h.tensor_tensor(o, a, b, op), [out], [in0, in1])

    def ts(self, out, in0, s1, s2, op0, op1=None, accum_out=None, eng="dve"):
        ins = [in0]
        idx = {}
        if isinstance(s1, View):
            idx["s1"] = len(ins); ins.append(s1)
        if isinstance(s2, View):
            idx["s2"] = len(ins); ins.append(s2)
        outs = [out] + ([accum_out] if accum_out is not None else [])
        no = len(outs)

        def fn(h, *a):
            o = a[0]
            i = a[no:]
            v1 = i[idx["s1"]] if "s1" in idx else s1
            v2 = i[idx["s2"]] if "s2" in idx else s2
            kw = {}
            if op1 is not None:
                kw["op1"] = op1
            if accum_out is not None:
                kw["accum_out"] = a[1]
            return h.tensor_scalar(o, i[0], v1, v2, op0, **kw)
        return self.op(eng, fn, outs, ins)

    def stt(self, out, in0, scalar, in1, op0, op1, eng="dve"):
        ins = [in0, in1]
        if isinstance(scalar, View):
            ins.append(scalar)

        def fn(h, o, a, b, *s):
            return h.scalar_tensor_tensor(o, a, s[0] if s else scalar, b, op0, op1)
        return self.op(eng, fn, [out], ins)

    def copy(self, out, in_, eng="dve"):
        return self.op(eng, lambda h, o, i: h.tensor_copy(o, i), [out], [in_])

    def memset(self, out, val, eng="dve"):
        return self.op(eng, lambda h, o: h.memset(o, val), [out], [])

    def reduce(self, out, in_, op, axis=AX.X, eng="dve"):
        return self.op(eng, lambda h, o, i: h.tensor_reduce(o, i, axis, op), [out], [in_])

    def recip(self, out, in_):
        return self.op("dve", lambda h, o, i: h.reciprocal(o, i), [out], [in_])


from contextlib import ExitStack
from concourse.bass_utils import run_bass_kernel_spmd

D = 2048
S = 2048
T = 512
NPASS = S // T
TT = T // 128
MEM = 256
DFF = 5632
NFC = DFF // 512
EPS = 1e-6
C_IN = 8960
G0 = 2816
NSLAB = 7
SLAB_EL = 4096
PER_PASS = 117
NSLAB_TOTAL = 4 + PER_PASS

DEBUG = {}


def build(npass=NPASS, dbg=None):
    nc = bass.Bass("TRN2", target_bir_lowering=False)

    def din(name, shape, dt=F32):
        return nc.dram_tensor(name, list(shape), dt, kind="ExternalInput").ap()

    x_d = din("x", [S, D])
    mem_d = din("mem", [MEM, D])
    pos_d = din("pos_t", [128, 16], I32)
    gmix_d = din("gmixT", [128, 16])
    gffn_d = din("gffnT", [128, 16])
    gmem_d = din("gmemT", [128, 16])
    wpack_d = din("wpack", [NSLAB_TOTAL, 128, SLAB_EL])
    gav_d = din("gavT", [128, 4])
    wsT_d = din("wsT", [128, 4, 128])
    bsp_d = din("bsp", [128, 512])
    gbq_d = din("gbq", [128, 64])
    gbk_d = din("gbk", [128, 64])
    sinks_d = din("sinks", [128, 16])
    gcq_d = din("gcq", [128, 1])
    gck_d = din("gck", [128, 128])
    cw_d = din("convw", [128, 3, 88])
    cb_d = din("convb", [128, 88])
    out_d = nc.dram_tensor("out", [S, D], F32, kind="ExternalOutput").ap()
    dbg_d = None
    if dbg is not None:
        dbg_d = nc.dram_tensor("dbg", [128, dbg], F32, kind="ExternalOutput").ap()

    K = MK(nc, 206 * 1024)
    sem_c = K.new_sem("const")
    sem_x = [K.new_sem(f"x{i}") for i in range(TT)]
    sem_slab = [K.new_sem(f"slab{i}") for i in range(NSLAB)]
    sem_slabx = [K.new_sem(f"slabx{i}") for i in range(2)]
    sem_xs = [K.new_sem(f"xs{i}") for i in range(TT)]
    sem_dbg = K.new_sem("dbg")

    X1 = [K.sb_tile([128, D], F32, f"x1_{i}") for i in range(TT)]
    X1m = [[K.sb_tile([128, 512], F32, f"x1_{i}_{m}", at=X1[i].region.lo + m * 2048) for m in range(4)]
           for i in range(TT)]
    hT = [K.sb_tile([128, T], BF16, f"hT{c}") for c in range(16)]
    yTa = K.sb_tile([128, 4, T], BF16, "yTa")
    yTb = K.sb_tile([128, 8, T], BF16, "yTb")
    yTc = K.sb_tile([128, 4, T], BF16, "yTc")
    mT = [K.sb_tile([128, T], BF16, f"mT{c}") for c in range(16)]
    slabs = [K.sb_tile([128, SLAB_EL], BF16, f"slab{i}") for i in range(NSLAB)]
    xstage = [K.sb_tile([128, D], F32, f"xstage{i}", at=yTa.region.lo + i * 8192) for i in range(TT)]
    assert xstage[-1].region.hi <= mT[15].region.hi
    slabs_x = [K.sb_tile([128, SLAB_EL], BF16, f"slabx{i}", at=mT[8 * i].region.lo) for i in range(2)]
    ident = K.sb_tile([128, 128], BF16, "ident")
    gmixT = K.sb_tile([128, 16], F32, "gmixT")
    gffnT = K.sb_tile([128, 16], F32, "gffnT")
    gmemT = K.sb_tile([128, 16], F32, "gmemT")
    gavT = K.sb_tile([128, 4], F32, "gavT")
    gcq = K.sb_tile([128, 1], F32, "gcq")
    convw = K.sb_tile([128, 3, 88], F32, "convw")
    convb = K.sb_tile([128, 88], F32, "convb")
    hist = K.sb_tile([128, 88, 2], F32, "hist")
    wsT = K.sb_tile([128, 4, 128], BF16, "wsT")
    bsp = K.sb_tile([128, 4, 128], F32, "bsp")
    gbq = K.sb_tile([128, 64], F32, "gbq")
    gbk = K.sb_tile([128, 64], F32, "gbk")
    gck = K.sb_tile([128, 128], F32, "gck")
    sinkexp = K.sb_tile([128, 16], F32, "sinkexp")
    cosT = K.sb_tile([128, 16, 8], F32, "cosT")
    sinT = K.sb_tile([128, 16, 8], F32, "sinT")
    mask_cur = K.sb_tile([128, 4, 128], BF16, "mask_cur")
    mask_prev = K.sb_tile([128, 4, 128], BF16, "mask_prev")
    ones_bf = K.sb_tile([128, 128], BF16, "ones_bf")
    kTmem = K.sb_tile([128, 4, MEM], BF16, "kTmem")
    vmem = K.sb_tile([128, 2, 512], BF16, "vmem")
    KT = [K.sb_tile([128, 4, 128], BF16, f"KT{i}") for i in range(3)]
    vaug = [K.sb_tile([128, 2, 65], BF16, f"vaug{i}") for i in range(3)]
    kpad = K.sb_tile([128, 4, 128], BF16, "kpad")
    ss = K.sb_tile([128, 32], F32, "ss")
    rstd = K.sb_tile([128, 32], F32, "rstd")
    ssA = K.sb_tile([128, 8], F32, "ssA")
    ssB = K.sb_tile([128, 32], F32, "ssB")
    rstdB = K.sb_tile([128, 32], F32, "rstdB")

    base = K.sb_top
    xn = [K.sb_tile([128, D], BF16, f"xn{i}") for i in range(2)]
    junk = xn[1]
    top_norm = K.sb_top
    xs = [K.sb_tile([128, 516], F32, f"xs{i}") for i in range(2)]
    acc = [K.sb_tile([128, 512], F32, f"acc{i}") for i in range(2)]
    sa = [K.sb_tile([128, 512], BF16, f"sa{c}") for c in range(4)]
    actT = [[K.sb_tile([128, 512], BF16, f"actT{j}_{c}") for c in range(4)] for j in range(2)]
    top_f = K.sb_top
    K.sb_top = top_norm
    sg = [K.sb_tile([128, 512], BF16, f"sg{i}") for i in range(6)]
    tm = [K.sb_tile([128, 512], F32, f"tm{i}") for i in range(4)]
    top_m = K.sb_top
    K.sb_top = base
    uT = [K.sb_tile([128, T], BF16, f"uT{c}") for c in range(4)]
    vn = [K.sb_tile([128, 512], BF16, f"vn{i}") for i in range(TT)]
    vg = [K.sb_tile([128, 512], F32, f"vg{i}") for i in range(2)]
    tmpA = K.sb_tile([128, 512], F32, "tmpA")
    sq = K.sb_tile([128, 1024 + 128], F32, "sq")
    qn = K.sb_tile([128, 16, 64], F32, "qn", at=sq.region.lo)
    kn = K.sb_tile([128, 2, 64], F32, "kn")
    rt = [K.sb_tile([128, 16, 8], F32, f"rt{i}") for i in range(4)]
    qr = K.sb_tile([128, 1024], BF16, "qr")
    yb = K.sb_tile([128, 1024], BF16, "yb")
    qT = [K.sb_tile([128, 1024], BF16, f"qT{i}") for i in range(2)]
    PT = [K.sb_tile([128, 512], BF16, f"PT{i}") for i in range(8)]
    den = K.sb_tile([128, 16], F32, "den")
    sqc = K.sb_tile([128, 512], BF16, "sqc")
    msc = K.sb_tile([128, 512], F32, "msc")
    qcT = K.sb_tile([128, 512], BF16, "qcT")
    PTc = [K.sb_tile([128, 512], BF16, f"PTc{i}") for i in range(2)]
    rdc = K.sb_tile([128, 512], F32, "rdc")
    top_abc = K.sb_top
    K.sb_top = max(top_f, top_m, top_abc)
    print("SBUF bytes used per partition:", K.sb_top)

    banks = [K.ps_tile(b, 1, [128, 512], F32, f"bank{b}") for b in range(8)]
    bank_i = [0]

    def bank():
        b = banks[bank_i[0] % 7]
        bank_i[0] += 1
        return b

    def bf(bk, shape):
        v = bk.bitcast(BF16)
        n = int(np.prod(shape[1:]))
        v = v[:, 0:n]
        if len(shape) == 3:
            v = v.rearrange("p (a b) -> p a b", a=shape[1])
        return v

    slab_i = [0]

    plan = []
    plan_idx = {}
    free_slots = list(range(NSLAB))

    def load_slab(parts, xslot=None):
        if xslot is None:
            assert free_slots, "slab ring exhausted"
            idx = free_slots.pop(0)
        key = tuple(tuple(p_) for p_ in parts)
        if key not in plan_idx:
            plan_idx[key] = len(plan)
            plan.append(list(parts))
        sidx = plan_idx[key]
        slab_i[0] += 1
        if xslot is not None:
            sl = slabs_x[xslot]
            K.dma("pool", sl, wpack_d[sidx], sem_slabx[xslot])
            return sl
        sl = slabs[idx]
        K.dma("pool", sl, wpack_d[sidx], sem_slab[idx])
        return sl

    def release(*views):
        for v in views:
            if v.tile in slabs_x:
                continue
            idx = slabs.index(v.tile)
            assert idx not in free_slots
            free_slots.append(idx)

    w_in_v, wmkv_v, wo_v, wup_v, wdn_v = "w_in", "w_mem_kv", "w_out", "w_up", "w_down"
    wba_v, wbb_v, wbc_v = "w_branch_a", "w_branch_b", "w_branch_c"

    def slab_k8(view, kh, c0, xslot=None):
        sl = load_slab([(0, view, kh * 8, (kh + 1) * 8, c0, c0 + 512)], xslot=xslot)
        return sl.rearrange("p (a b) -> p a b", a=8)

    def slab_k16(view, c0):
        sl = load_slab([(0, view, 0, 16, c0, c0 + 256)])
        return sl.rearrange("p (a b) -> p a b", a=16)

    dbg_off = [0]
    dbg_st = []
    dbg_n = [0]

    def dump(v, n):
        if dbg_d is None:
            return
        if not dbg_st:
            dbg_st.extend(K.sb_tile([128, 512], F32, f"dbgst{j}") for j in range(2))
        for c0 in range(0, n, 512):
            st = dbg_st[dbg_n[0] % 2]
            dbg_n[0] += 1
            K.copy(st, v[:, c0:c0 + 512])
            K.dma("sp", dbg_d[:, dbg_off[0]:dbg_off[0] + 512], st, sem_dbg)
            dbg_off[0] += 512

    for mt in range(2):
        K.dma("sp", xstage[mt], mem_d[mt * 128:(mt + 1) * 128, :], sem_xs[mt])
    cl = []
    for t, src in ((gmixT, gmix_d), (gffnT, gffn_d), (gmemT, gmem_d), (gavT, gav_d), (gcq, gcq_d),
                   (convw, cw_d), (convb, cb_d)):
        K.dma("sp", t, src, sem_c)
        cl.append(t)
    wsT_f = K.sb_tile([128, 4, 128], F32, "wsT_f", at=top_norm + 11264)
    pos_i = K.sb_tile([128, 16], I32, "pos_i", at=top_norm + 13312)
    sinks_f = K.sb_tile([128, 16], F32, "sinks_f", at=top_norm + 13376)
    K.dma("sp", wsT_f, wsT_d, sem_c)
    K.dma("sp", pos_i, pos_d, sem_c)
    K.dma("sp", bsp.rearrange("p a b -> p (a b)"), bsp_d, sem_c)
    K.dma("sp", gbq, gbq_d, sem_c)
    K.dma("sp", gbk, gbk_d, sem_c)
    K.dma("sp", gck, gck_d, sem_c)
    K.dma("sp", sinks_f, sinks_d, sem_c)
    K.retoken(cl + [wsT_f, pos_i, bsp, gbq, gbk, gck, sinks_f], sem_c)

    for i in range(TT):
        K.dma("sp", X1[i], x_d[i * 128:(i + 1) * 128, :], sem_x[i])
    K.memset(ident, 1.0, eng="pool")
    K.op("pool", lambda h, o, i: h.affine_select(o, i, [[-1, 128]], ALU.is_equal, 0.0, base=0,
                                                channel_multiplier=1), [ident], [ident])
    wk = [[slab_k8(wmkv_v, kh, n * 512) for kh in range(2)] for n in range(2)]
    def norm_sumsq(j, src):
        if j % 2 == 0:
            K.act(junk, src, AF.Square, accum_out=ss[:, j:j + 1])
        else:
            K.op("dve", lambda h, o, acc_, a_, b_: h.scalar_tensor_tensor(o, a_, 1.0, b_, ALU.mult, ALU.mult,
                                                                          accum_out=acc_),
                 [xn[0], ss[:, j:j + 1]], [src, src])

    def norm_rstd(n):
        K.ts(ss[:, 8:8 + n], ss[:, 0:n], 1.0 / D, EPS, ALU.mult, ALU.add)
        K.act(ss[:, 16:16 + n], ss[:, 8:8 + n], AF.Sqrt)
        K.recip(rstd[:, 0:n], ss[:, 16:16 + n])

    def norm_scale(j, src):
        if j % 2 == 0:
            K.act(xn[j % 2], src, AF.Identity, scale=rstd[:, j:j + 1])
        else:
            K.ts(xn[j % 2], src, rstd[:, j:j + 1], None, ALU.mult)

    def norm_transpose(j, gT, dstT):
        x_ = xn[j % 2]
        for hb in range(2):
            bk = bank()
            bv = bf(bk, [128, 8, 128])
            for jj in range(8):
                c = hb * 8 + jj
                K.tr(bv[:, jj, :], x_[:, c * 128:(c + 1) * 128], ident, sig=(jj == 7))
            for jj in range(8):
                c = hb * 8 + jj
                if hb == 0:
                    K.ts(dstT(c)[:, j * 128:(j + 1) * 128], bv[:, jj, :], gT[:, c:c + 1], None, ALU.mult)
                else:
                    K.act(dstT(c)[:, j * 128:(j + 1) * 128], bv[:, jj, :], AF.Identity, scale=gT[:, c:c + 1])

    def norm_tiles(srcs, gT, dstT, have_stats=False, prescaled=0):
        n = len(srcs)
        if not have_stats:
            K.memset(ss[:, 0:n], 0.0)
            for j, src in enumerate(srcs):
                norm_sumsq(j, src)
            norm_rstd(n)
        for j, src in enumerate(srcs):
            if j >= prescaled:
                norm_scale(j, src)
            norm_transpose(j, gT, dstT)

    memT = K.sb_tile([128, 16, MEM], BF16, "memT", at=top_norm)
    norm_tiles([xstage[0], xstage[1]], gmemT, lambda c: memT[:, c, :])
    kfm = K.sb_tile([128, 512], F32, "kfm", at=top_norm + 8192)
    kbm = K.sb_tile([128, 512], BF16, "kbm", at=top_norm + 10240)
    for mt in range(2):
        bk_k, bk_v = bank(), bank()
        for n, bk in ((0, bk_k), (1, bk_v)):
            for k in range(16):
                K.mm(bk, memT[:, k, mt * 128:(mt + 1) * 128], wk[n][k // 8][:, k % 8, :], k == 0, k == 15)
        K.copy(vmem[:, mt, :], bk_v)
        K.act(kfm, bk_k, AF.Square)
        K.reduce(ss[:, 4:8], kfm.rearrange("p (h d) -> p h d", h=4), ALU.add)
        K.ts(ss[:, 8:12], ss[:, 4:8], 1.0 / 128, EPS, ALU.mult, ALU.add)
        K.act(ss[:, 12:16], ss[:, 8:12], AF.Sqrt)
        K.recip(rstd[:, 4:8], ss[:, 12:16])
        K.tt(kfm.rearrange("p (h d) -> p h d", h=4), bk_k.rearrange("p (h d) -> p h d", h=4),
             rstd[:, 4:8].unsqueeze(2).broadcast_to([128, 4, 128]), ALU.mult)
        K.tt(kbm.rearrange("p (h d) -> p h d", h=4), kfm.rearrange("p (h d) -> p h d", h=4),
             gck.unsqueeze(1).broadcast_to([128, 4, 128]), ALU.mult)
        bk = bank()
        bv = bf(bk, [128, 4, 128])
        for h in range(4):
            K.tr(bv[:, h, :], kbm[:, h * 128:(h + 1) * 128], ident, sig=(h == 3))
        K.copy(kTmem[:, :, mt * 128:(mt + 1) * 128], bv)
    release(wk[0][0], wk[0][1], wk[1][0], wk[1][1])

    K.memset(mask_cur, 0.0, eng="pool")
    K.op("pool", lambda h, o, i: h.affine_select(o, i, [[0, 4], [1, 128]], ALU.is_ge, -30000.0, base=0,
                                                channel_multiplier=-1), [mask_cur], [mask_cur])
    K.memset(mask_prev, 0.0, eng="pool")
    K.op("pool", lambda h, o, i: h.affine_select(o, i, [[0, 4], [-1, 128]], ALU.is_gt, -30000.0, base=0,
                                                channel_multiplier=1), [mask_prev], [mask_prev])
    K.memset(ones_bf, 1.0, eng="pool")
    K.op("pool", lambda h, o, i: h.affine_select(o, i, [[0, 4], [1, 128]], ALU.is_ge, 0.0, base=0,
                                                channel_multiplier=-1), [wsT_f], [wsT_f])
    K.copy(wsT, wsT_f)
    K.memset(hist, 0.0)
    K.memset(kpad, 0.0)
    for i in range(3):
        K.memset(vaug[i], 1.0)
        K.memset(KT[i], 0.0)
    K.ts(gcq, gcq, 128.0 ** -0.5, None, ALU.mult)
    K.act(sinkexp, sinks_f, AF.Exp)

    ang = K.sb_tile([128, 16, 8], F32, "ang", at=top_norm + 13440)
    posf = K.sb_tile([128, 16], F32, "posf", at=top_norm + 13952)
    rr = K.sb_tile([128, 16, 8], F32, "rr", at=top_norm + 14464)
    rf = K.sb_tile([128, 16, 8], F32, "rf", at=top_norm + 14976)
    ri = K.sb_tile([128, 16, 8], I32, "ri", at=top_norm + 15488)
    mk_ = K.sb_tile([128, 16, 8], F32, "mk_", at=top_norm + 16000)
    K.copy(posf, pos_i)
    half = 8
    inv = (np.float32(500000.0) ** (-(np.arange(half, dtype=np.float32) / np.float32(half)))).astype(np.float32)
    for j in range(half):
        K.ts(ang[:, :, j], posf, float(inv[j]), None, ALU.mult)
    TWO_PI = 6.283185307179586
    for dst, shift in ((sinT, 0.0), (cosT, 0.25)):
        K.ts(rr, ang, 1.0 / TWO_PI, shift, ALU.mult, ALU.add)
        K.copy(ri, rr)
        K.copy(rf, ri)
        K.tt(rr, rr, rf, ALU.subtract)
        K.ts(mk_, rr, 0.5, None, ALU.is_gt)
        K.tt(rr, rr, mk_, ALU.subtract)
        K.ts(mk_, rr, -0.5, None, ALU.is_lt)
        K.tt(rr, rr, mk_, ALU.add)
        K.act(dst, rr, AF.Sin, scale=TWO_PI)


    pre_b = []
    for p in range(npass):
        t0 = p * T
        if p == 0:
            norm_tiles(X1, gmixT, lambda c: hT[c])
        else:
            norm_tiles(xstage, gmixT, lambda c: hT[c], have_stats=True, prescaled=2)
        if p == 0 and dbg_d is not None and "hT" in DEBUG:
            for c in range(16):
                dump(hT[c], T)

        def gen_a():
            Wu = [slab_k8(w_in_v, kh, 0) for kh in range(2)]
            yield
            for c in range(4):
                bk = bank()
                for k in range(16):
                    K.mm(bk, Wu[k // 8][:, k % 8, c * 128:(c + 1) * 128], hT[k], k == 0, k == 15)
                K.act(uT[c], bk, AF.Gelu_apprx_tanh)
                if c == 3:
                    release(*Wu)
                    Wv = [slab_k8(w_in_v, kh, 512) for kh in range(2)]
                yield
            for i in range(TT):
                bk = bank()
                for k in range(16):
                    K.mm(bk, hT[k][:, i * 128:(i + 1) * 128], Wv[k // 8][:, k % 8, :], k == 0, k == 15)
                g_ = vg[i % 2]
                K.act(g_, bk, AF.Gelu_apprx_tanh)
                K.memset(ssA[:, 0:1], 0.0)
                K.act(tmpA.bitcast(BF16)[:, 0:512], g_, AF.Square, accum_out=ssA[:, 0:1])
                K.ts(ssA[:, 1:2], ssA[:, 0:1], 1.0 / 512, EPS, ALU.mult, ALU.add)
                K.act(ssA[:, 2:3], ssA[:, 1:2], AF.Sqrt)
                K.recip(ssA[:, 3:4], ssA[:, 2:3])
                K.ts(vn[i], g_, ssA[:, 3:4], None, ALU.mult)
                if i == TT - 1:
                    release(*Wv)
                yield
            for g in range(4):
                bk = bank()
                for i in range(TT):
                    K.mm(bk[:, i * 128:(i + 1) * 128], vn[i][:, g * 128:(g + 1) * 128], wsT[:, g, :], True, True,
                         sig=(i == TT - 1))
                K.stt(tmpA.rearrange("p (a b) -> p a b", a=4), bk.rearrange("p (a b) -> p a b", a=4),
                      gavT[:, g:g + 1], bsp[:, g:g + 1, :].broadcast_to([128, 4, 128]), ALU.mult, ALU.add)
                K.tt(yTa[:, g, :], tmpA, uT[g], ALU.mult)
                yield

        def load_b():
            return ([[slab_k8(w_in_v, kh, 1024 + n * 512) for kh in range(2)] for n in range(2)],
                    slab_k16(w_in_v, 2048))

        def gen_b():
            Wq, Wkv = pre_b.pop() if pre_b else load_b()
            yield

            def stage_x(i):
                gi = p * TT + i
                cur = gi % 3
                bq = [bank(), bank()]
                bkv = bank()
                for n in range(2):
                    for k in range(16):
                        K.mm(bq[n], hT[k][:, i * 128:(i + 1) * 128], Wq[n][k // 8][:, k % 8, :], k == 0, k == 15)
                for k in range(16):
                    K.mm(bkv[:, 0:256], hT[k][:, i * 128:(i + 1) * 128], Wkv[:, k, :], k == 0, k == 15)
                for n in range(2):
                    K.act(sq[:, n * 512:(n + 1) * 512], bq[n], AF.Square)
                K.act(sq[:, 1024:1152], bkv[:, 0:128], AF.Square)
                K.act(vaug[cur][:, :, 0:64], bkv[:, 128:256].rearrange("p (g d) -> p g d", g=2), AF.Copy)
                K.reduce(ssB[:, 0:18], sq.rearrange("p (h d) -> p h d", h=18), ALU.add)
                K.ts(ssB[:, 0:18], ssB[:, 0:18], 1.0 / 64, EPS, ALU.mult, ALU.add)
                K.act(rstdB[:, 0:18], ssB[:, 0:18], AF.Sqrt)
                K.recip(rstdB[:, 0:18], rstdB[:, 0:18])
                K.ts(rstdB[:, 0:16], rstdB[:, 0:16], 0.125, None, ALU.mult)
                for n in range(2):
                    K.tt(qn[:, n * 8:(n + 1) * 8, :], bq[n].rearrange("p (h d) -> p h d", h=8),
                         rstdB[:, n * 8:(n + 1) * 8].unsqueeze(2).broadcast_to([128, 8, 64]), ALU.mult)
                K.tt(kn, bkv[:, 0:128].rearrange("p (g d) -> p g d", g=2),
                     rstdB[:, 16:18].unsqueeze(2).broadcast_to([128, 2, 64]), ALU.mult)
                yield
                PE_ = "dve"
                K.tt(qn, qn, gbq.unsqueeze(1).broadcast_to([128, 16, 64]), ALU.mult, eng=PE_)
                K.tt(kn, kn, gbk.unsqueeze(1).broadcast_to([128, 2, 64]), ALU.mult, eng=PE_)
                cs = cosT[:, gi, :].unsqueeze(1)
                sn = sinT[:, gi, :].unsqueeze(1)
                qr3 = qr.rearrange("p (h d) -> p h d", h=16)
                K.copy(qr3[:, :, 16:64], qn[:, :, 16:64], eng=PE_)
                K.tt(rt[0], qn[:, :, 0:8], cs.broadcast_to([128, 16, 8]), ALU.mult, eng=PE_)
                K.tt(rt[1], qn[:, :, 8:16], sn.broadcast_to([128, 16, 8]), ALU.mult, eng=PE_)
                K.tt(qr3[:, :, 0:8], rt[0], rt[1], ALU.subtract, eng=PE_)
                K.tt(rt[2], qn[:, :, 8:16], cs.broadcast_to([128, 16, 8]), ALU.mult, eng=PE_)
                K.tt(rt[3], qn[:, :, 0:8], sn.broadcast_to([128, 16, 8]), ALU.mult, eng=PE_)
                K.tt(qr3[:, :, 8:16], rt[2], rt[3], ALU.add, eng=PE_)
                kd = kpad.rearrange("p (g q) d -> p g q d", g=2)
                d0 = kd[:, :, 0, 0:64]
                K.copy(d0[:, :, 16:64], kn[:, :, 16:64], eng=PE_)
                K.tt(rt[0][:, 0:2, :], kn[:, :, 0:8], cs.broadcast_to([128, 2, 8]), ALU.mult, eng=PE_)
                K.tt(rt[1][:, 0:2, :], kn[:, :, 8:16], sn.broadcast_to([128, 2, 8]), ALU.mult, eng=PE_)
                K.tt(d0[:, :, 0:8], rt[0][:, 0:2, :], rt[1][:, 0:2, :], ALU.subtract, eng=PE_)
                K.tt(rt[2][:, 0:2, :], kn[:, :, 8:16], cs.broadcast_to([128, 2, 8]), ALU.mult, eng=PE_)
                K.tt(rt[3][:, 0:2, :], kn[:, :, 0:8], sn.broadcast_to([128, 2, 8]), ALU.mult, eng=PE_)
                K.tt(d0[:, :, 8:16], rt[2][:, 0:2, :], rt[3][:, 0:2, :], ALU.add, eng=PE_)
                K.copy(kd[:, :, 1, 64:128], d0, eng=PE_)
                yield
                bk = bank()
                bv = bf(bk, [128, 4, 128])
                for j in range(4):
                    K.tr(bv[:, j, :], kpad[:, j, :], ident, sig=(j == 3))
                K.copy(KT[cur], bv)
                bk = bank()
                bv = bf(bk, [128, 8, 128])
                for j in range(8):
                    K.tr(bv[:, j, :], qr[:, j * 128:(j + 1) * 128], ident, sig=(j == 7))
                K.act(qT[i % 2].rearrange("p (a b) -> p a b", a=8), bv, AF.Copy)
                yield

            def stage_y(i):
                gi = p * TT + i
                cur, prv = gi % 3, (gi - 1) % 3
                qT_ = qT[i % 2]
                blocks = ([(prv, mask_prev)] if gi > 0 else []) + [(cur, mask_cur)]
                pt_of = {}
                pi = 0
                for g in range(2):
                    for (kb, msk) in blocks:
                        for par in range(2):
                            bs = bank()
                            K.mm(bs, KT[kb][:, 2 * g + par, :], qT_[:, g * 512:(g + 1) * 512], True, False)
                            K.mm(bs, ident, msk.rearrange("p a b -> p (a b)"), False, True)
                            ptile = PT[pi]
                            pi += 1
                            K.act(ptile, bs, AF.Exp)
                            pt_of[(g, kb, par)] = ptile
                    yield
                ob = [bank() for _ in range(4)]
                for h in range(16):
                    g, jj, par = h // 8, (h % 8) // 2, h % 2
                    o = ob[h // 4].rearrange("p (a b) -> p a b", a=4)[:, h % 4, 0:65]
                    for bi, (kb, msk) in enumerate(blocks):
                        K.mm(o, pt_of[(g, kb, par)][:, jj * 128:(jj + 1) * 128], vaug[kb][:, g, :],
                             bi == 0, bi == len(blocks) - 1, sig=(bi == len(blocks) - 1 and h % 4 == 3))
                for b4 in range(4):
                    o3 = ob[b4].rearrange("p (a b) -> p a b", a=4)
                    K.tt(den[:, b4 * 4:(b4 + 1) * 4], o3[:, :, 64], sinkexp[:, b4 * 4:(b4 + 1) * 4], ALU.add)
                K.recip(den, den)
                for b4 in range(4):
                    o3 = ob[b4].rearrange("p (a b) -> p a b", a=4)
                    K.tt(yb[:, b4 * 256:(b4 + 1) * 256].rearrange("p (a b) -> p a b", a=4), o3[:, :, 0:64],
                         den[:, b4 * 4:(b4 + 1) * 4].unsqueeze(2).broadcast_to([128, 4, 64]), ALU.mult)
                yield
                bk = bank()
                bv = bf(bk, [128, 8, 128])
                for j in range(8):
                    K.tr(bv[:, j, :], yb[:, j * 128:(j + 1) * 128], ident, sig=(j == 7))
                K.act(yTb[:, :, i * 128:(i + 1) * 128], bv, AF.Copy)
                yield

            def chain_x():
                for i in range(TT):
                    while y_done[0] < i - 1:
                        yield
                    yield from stage_x(i)
                    x_done[0] = i + 1
                release(Wq[0][0], Wq[0][1], Wq[1][0], Wq[1][1], Wkv)

            def chain_y():
                for i in range(TT):
                    while x_done[0] < i + 1:
                        yield
                    yield from stage_y(i)
                    y_done[0] = i + 1

            b_chains.extend([chain_x(), chain_y()])

        def gen_c():
            Wqc = [slab_k8(w_in_v, kh, 2304, xslot=kh) for kh in range(2)]
            yield
            for h in range(4):
                bqc = banks[7]
                for k in range(16):
                    K.mm(bqc, Wqc[k // 8][:, k % 8, h * 128:(h + 1) * 128], hT[k], k == 0, k == 15)
                K.act(sqc, bqc, AF.Square)
                yield
                bss = bank()
                K.mm(bss, ones_bf, sqc, True, True)
                K.ts(msc, bss, 1.0 / 128, EPS, ALU.mult, ALU.add)
                K.act(msc, msc, AF.Sqrt)
                K.recip(msc, msc)
                K.stt(qcT, bqc, gcq[:, 0:1], msc, ALU.mult, ALU.mult)
                yield
                for mt in range(2):
                    bs = bank()
                    K.mm(bs, kTmem[:, h, mt * 128:(mt + 1) * 128], qcT, True, True)
                    K.act(PTc[mt], bs, AF.Exp)
                yield
                bo, bd = bank(), bank()
                for mt in range(2):
                    K.mm(bo, vmem[:, mt, h * 128:(h + 1) * 128], PTc[mt], mt == 0, mt == 1)
                for mt in range(2):
                    K.mm(bd, ones_bf, PTc[mt], mt == 0, mt == 1)
                K.recip(rdc, bd)
                K.tt(yTc[:, h, :], bo, rdc, ALU.mult)
                if h == 3:
                    release(*Wqc)
                yield

        x_done, y_done, b_chains = [0], [0], []
        ga, gb, gc = gen_a(), gen_b(), gen_c()
        next(gb)
        for _ in gb:
            pass
        gens = [b_chains[0], b_chains[1], ga, gc]
        c_started = True
        while gens:
            for g_ in list(gens):
                try:
                    next(g_)
                except StopIteration:
                    gens.remove(g_)
                    if g_ is ga and not c_started:
                        gens.append(gc)
                        c_started = True
        if p == 0 and dbg_d is not None:
            if "yTa" in DEBUG:
                for c in range(4):
                    dump(yTa[:, c, :], T)
            if "yTb" in DEBUG:
                for c in range(8):
                    dump(yTb[:, c, :], T)
            if "yTc" in DEBUG:
                for c in range(4):
                    dump(yTc[:, c, :], T)

        for np_ in range(8):
            Gs = [slab_k16(w_in_v, G0 + gidx * D + np_ * 256) for gidx in range(3)]
            c0 = np_ * 256
            BR = load_slab([(0, wba_v, 0, 4, c0, c0 + 256),
                            (1024, wbb_v, 0, 8, c0, c0 + 256),
                            (3072, wbc_v, 0, 4, c0, c0 + 256)]).rearrange("p (a b) -> p a b", a=16)
            ysrc = [(yTa, 0, 4), (yTb, 4, 8), (yTc, 12, 4)]
            for nn in range(2):
                n = np_ * 2 + nn
                cols = slice(nn * 128, (nn + 1) * 128)
                tms = []
                for gidx in range(3):
                    bg = bank()
                    for k in range(16):
                        K.mm(bg, Gs[gidx][:, k, cols], hT[k], k == 0, k == 15)
                    s_ = sg[(n * 3 + gidx) % 6]
                    K.act(s_, bg, AF.Sigmoid)
                    yt, boff, nk = ysrc[gidx]
                    bb = bank()
                    for kc in range(nk):
                        K.mm(bb, BR[:, boff + kc, cols], yt[:, kc, :], kc == 0, kc == nk - 1)
                    t_ = tm[gidx]
                    K.tt(t_, bb, s_, ALU.mult)
                    tms.append(t_)
                K.tt(tm[3], tms[0], tms[1], ALU.add)
                K.tt(mT[n], tm[3], tms[2], ALU.add)
            release(Gs[0], Gs[1], Gs[2], BR)
        if p == 0 and dbg_d is not None and "mT" in DEBUG:
            for c in range(16):
                dump(mT[c], T)

        for m in range(4):
            Wo = [slab_k8(wo_v, kh, m * 512) for kh in range(2)]
            for i in range(TT):
                bk = bank()
                for k in range(16):
                    K.mm(bk, mT[k][:, i * 128:(i + 1) * 128], Wo[k // 8][:, k % 8, :], k == 0, k == 15)
                K.tt(X1m[i][m], X1m[i][m], bk, ALU.add)
                if m == 3:
                    if i == 0:
                        K.memset(ss[:, 0:TT], 0.0)
                    norm_sumsq(i, X1[i])
            release(*Wo)
        if p == 0 and dbg_d is not None and "x1" in DEBUG:
            for i in range(TT):
                dump(X1[i], D)

        norm_rstd(TT)
        norm_tiles(X1, gffnT, lambda c: hT[c], have_stats=True)

        def f_up(fc, ab):
            aT = actT[fc % 2]
            U = [slab_k8(wup_v, kh, ab * DFF + fc * 512) for kh in range(2)]
            for c in range(4):
                cg = ab * 44 + fc * 4 + c
                bk = bank()
                for k in range(16):
                    K.mm(bk, U[k // 8][:, k % 8, c * 128:(c + 1) * 128], hT[k], k == 0, k == 15)
                x_ = xs[c % 2]
                a_ = acc[c % 2]
                K.copy(x_[:, 0:2], hist[:, cg, :])
                K.act(x_[:, 2:514], bk, AF.Copy)
                K.copy(hist[:, cg, :], x_[:, 512:514])
                K.act(a_, bk, AF.Identity, bias=convb[:, cg:cg + 1], scale=convw[:, 2, cg:cg + 1])
                K.stt(a_, x_[:, 1:513], convw[:, 1, cg:cg + 1], a_, ALU.mult, ALU.add)
                K.stt(a_, x_[:, 0:512], convw[:, 0, cg:cg + 1], a_, ALU.mult, ALU.add)
                if ab == 0:
                    K.act(sa[c], a_, AF.Silu)
                else:
                    K.tt(aT[c], sa[c], a_, ALU.mult)
            release(*U)

        def load_dn(fc):
            return [load_slab([(0, wdn_v, fc * 4 + 2 * j, fc * 4 + 2 * j + 2, 0, 2048)]).rearrange(
                "p (a b) -> p a b", a=2) for j in range(2)]

        def f_down(fc, Dn=None):
            aT = actT[fc % 2]
            if Dn is None:
                Dn = load_dn(fc)
            for i in range(TT):
                for m in range(4):
                    bk = bank()
                    for c in range(4):
                        K.mm(bk, aT[c][:, i * 128:(i + 1) * 128], Dn[c // 2][:, c % 2, m * 512:(m + 1) * 512],
                             c == 0, c == 3)
                    K.tt(X1m[i][m], X1m[i][m], bk, ALU.add)
            release(*Dn)

        f_up(0, 0)
        f_up(0, 1)
        if p + 1 < npass:
            for i in range(TT):
                K.dma("sp", xstage[i], x_d[t0 + T + i * 128:t0 + T + (i + 1) * 128, :], sem_xs[i])
        for fc in range(NFC):
            if fc + 1 < NFC:
                f_up(fc + 1, 0)
            if fc == NFC - 1 and p + 1 < npass:
                K.memset(ss[:, 0:TT], 0.0)
                for j in range(TT):
                    norm_sumsq(j, xstage[j])
                norm_rstd(TT)
                norm_scale(0, xstage[0])
                norm_scale(1, xstage[1])
                dn_last = load_dn(fc)
                pre_b.append(load_b())
                f_down(fc, dn_last)
                continue
            f_down(fc)
            if fc + 1 < NFC:
                f_up(fc + 1, 1)
        for i in range(TT):
            K.dma("sp", out_d[t0 + i * 128:t0 + (i + 1) * 128, :], X1[i], sem_x[i])
        if p + 1 < npass:
            for i in (2, 3, 0, 1):
                K.dma("sp", X1[i], xstage[i], sem_x[i])

    toks = [(sem_x[i], K.dma_cum[id(sem_x[i])]) for i in range(TT)]
    if dbg_d is not None and id(sem_dbg) in K.dma_cum:
        toks.append((sem_dbg, K.dma_cum[id(sem_dbg)]))
    K.wait_all("sp", toks)
    print("instructions", K.n_inst, "waits", K.n_wait, "slabs", slab_i[0], "dbg cols", dbg_off[0])
    assert len(plan) == NSLAB_TOTAL, len(plan)
    nc.mk_plan = plan
    return nc


_PLAN = []


def pack_weights(inputs, plan):
    wp = np.zeros((len(plan), 128, SLAB_EL), dtype=np.float32)
    views = {}
    for si, parts in enumerate(plan):
        for off, name, k0, k1, c0, c1 in parts:
            if name not in views:
                w = np.asarray(inputs[name][0], dtype=np.float32)
                views[name] = w.reshape(w.shape[0] // 128, 128, w.shape[1])
            blk = views[name][k0:k1, :, c0:c1]
            n = (k1 - k0) * (c1 - c0)
            wp[si, :, off:off + n] = np.transpose(blk, (1, 0, 2)).reshape(128, n)
    return wp


def prep_inputs(inputs, plan):
    f = lambda a: np.ascontiguousarray(np.asarray(a, dtype=np.float32))

    def rep(v, n):
        return np.ascontiguousarray(np.broadcast_to(f(v).reshape(1, n), (128, n)))

    def colT(v, nchunk):
        return np.ascontiguousarray(f(v).reshape(nchunk, 128).T)
    shared = {
        "gmixT": colT(inputs["g_mix"][0], 16),
        "gffnT": colT(inputs["g_ffn"][0], 16),
        "gmemT": colT(inputs["g_mem"][0], 16),
        "gavT": colT(inputs["g_a_v"][0], 4),
        "wsT": np.ascontiguousarray(np.transpose(f(inputs["w_spatial"][0]), (2, 0, 1))),
        "bsp": rep(inputs["b_spatial"][0], 512),
        "gbq": rep(inputs["g_b_q"][0], 64),
        "gbk": rep(inputs["g_b_k"][0], 64),
        "sinks": rep(inputs["sinks"][0], 16),
        "gcq": f(inputs["g_c_q"][0]).reshape(128, 1),
        "gck": rep(inputs["g_c_k"][0], 128),
        "convw": np.ascontiguousarray(np.transpose(f(inputs["conv_w"][0]).reshape(3, 88, 128), (2, 0, 1))),
        "convb": colT(inputs["conv_b"][0], 88),
    }
    shared["wpack"] = pack_weights(inputs, plan)
    x = np.asarray(inputs["x"], dtype=np.float32)
    mem = np.asarray(inputs["mem"], dtype=np.float32)
    pos = np.asarray(inputs["positions"], dtype=np.int32)
    maps = []
    for b in range(x.shape[0]):
        m = dict(shared)
        m["x"] = np.ascontiguousarray(x[b])
        m["mem"] = np.ascontiguousarray(mem[b])
        m["pos_t"] = np.ascontiguousarray(pos[b].reshape(16, 128).T)
        maps.append(m)
    return maps


def kernel(**inputs):
    nc = build()
    maps = prep_inputs(inputs, nc.mk_plan)
    res = run_bass_kernel_spmd(nc, maps, core_ids=list(range(len(maps))))
    return np.stack([np.asarray(r["out"], dtype=np.float32) for r in res.results], axis=0)
```

```python
import numpy as np
import concourse.bass as bass
import concourse.mybir as mybir

F32 = mybir.dt.float32
BF16 = mybir.dt.bfloat16
I32 = mybir.dt.int32
AF = mybir.ActivationFunctionType
ALU = mybir.AluOpType
AX = mybir.AxisListType
DSZ = {F32: 4, BF16: 2, I32: 4}
PAGE = 256


class Region:
    __slots__ = ("space", "lo", "hi", "last_w", "readers", "name")

    def __init__(self, space, lo, hi, name=""):
        self.space, self.lo, self.hi, self.name = space, lo, hi, name
        self.last_w = {}
        self.readers = {}


class View:
    __slots__ = ("tile", "ap")

    def __init__(self, tile, ap):
        self.tile, self.ap = tile, ap

    def __getitem__(self, key):
        return View(self.tile, self.ap[key])

    def bitcast(self, dt):
        return View(self.tile, self.ap.bitcast(dt))

    def rearrange(self, s, **kw):
        return View(self.tile, self.ap.rearrange(s, **kw))

    def broadcast_to(self, shape):
        return View(self.tile, self.ap.broadcast_to(list(shape)))

    def unsqueeze(self, axis):
        return View(self.tile, self.ap.unsqueeze(axis))


class Tile(View):
    __slots__ = ("region", "shape", "dtype")

    def __init__(self, K, space, base_ap, lo, hi, shape, dtype, name=""):
        self.tile = self
        self.ap = base_ap
        self.region = Region(space, lo, hi, name)
        self.shape, self.dtype = shape, dtype
        K._register(self.region)


class Eng:
    def __init__(self, name, h, sem, every):
        self.name, self.h, self.sem, self.every = name, h, sem, every
        self.count = 0
        self.known = {}


class MK:
    def __init__(self, nc, sb_bytes=200 * 1024):
        self.nc = nc
        self.sb_bytes = sb_bytes
        self.sb = nc.alloc_sbuf_tensor("mk_sb", [128, sb_bytes // 4], F32)
        self.ps = nc.alloc_psum_tensor("mk_ps", [128, 8 * 512], F32)
        self.sb_ap = self.sb.ap() if hasattr(self.sb, "ap") else self.sb[:]
        self.ps_ap = self.ps.ap() if hasattr(self.ps, "ap") else self.ps[:]
        self.pages = {"sb": {}, "ps": {}}
        self._sems = []
        self.eng = {}
        for name, h, every in (("pe", nc.tensor, False), ("act", nc.scalar, True),
                               ("dve", nc.vector, True), ("pool", nc.gpsimd, True),
                               ("sp", nc.sync, True)):
            self.eng[name] = Eng(name, h, self.new_sem("c_" + name), every)
        self.dma_cum = {}
        self.sb_top = 0
        self.n_wait = 0
        self.n_inst = 0

    def new_sem(self, name):
        s = self.nc.alloc_semaphore(name)
        self._sems.append(s)
        return s

    def sb_tile(self, shape, dtype, name="", at=None):
        n = int(np.prod(shape[1:])) * DSZ[dtype]
        n = (n + 31) // 32 * 32
        if at is None:
            at = self.sb_top
            self.sb_top += n
        assert at + n <= self.sb_bytes, f"SBUF overflow {name} {at + n}"
        ap = self.sb_ap[0:shape[0], at // 4:(at + n) // 4]
        if dtype != F32:
            ap = ap.bitcast(dtype)
        nel = int(np.prod(shape[1:]))
        ap = ap[:, 0:nel]
        if len(shape) > 2:
            names = " ".join(f"d{i}" for i in range(len(shape) - 1))
            ap = ap.rearrange(f"p ({names}) -> p {names}",
                              **{f"d{i}": shape[i + 1] for i in range(len(shape) - 2)})
        return Tile(self, "sb", ap, at, at + n, list(shape), dtype, name)

    def ps_tile(self, bank, nbanks=1, shape=None, dtype=F32, name=""):
        lo = bank * 2048
        hi = (bank + nbanks) * 2048
        ap = self.ps_ap[:, bank * 512:(bank + nbanks) * 512]
        if dtype != F32:
            ap = ap.bitcast(dtype)
        if shape is not None:
            nel = int(np.prod(shape[1:]))
            ap = ap[0:shape[0], 0:nel]
            if len(shape) > 2:
                names = " ".join(f"d{i}" for i in range(len(shape) - 1))
                ap = ap.rearrange(f"p ({names}) -> p {names}",
                                  **{f"d{i}": shape[i + 1] for i in range(len(shape) - 2)})
        return Tile(self, "ps", ap, lo, hi, shape, dtype, name)

    def _register(self, r):
        pg = self.pages[r.space]
        for p in range(r.lo // PAGE, (r.hi - 1) // PAGE + 1):
            pg.setdefault(p, []).append(r)

    def _overlaps(self, r):
        pg = self.pages[r.space]
        seen = {id(r): r}
        for p in range(r.lo // PAGE, (r.hi - 1) // PAGE + 1):
            for g in pg.get(p, ()):
                if id(g) not in seen and g.lo < r.hi and r.lo < g.hi:
                    seen[id(g)] = g
        return seen.values()

    def _deps(self, reads, writes, own=None):
        need = {}

        def add(tok):
            k = id(tok[0])
            if k not in need or need[k][1] < tok[1]:
                need[k] = tok
        for t in reads:
            for g in self._overlaps(t.region):
                for tok in g.last_w.values():
                    add(tok)
                if g.space == "ps":
                    for k, tok in g.readers.items():
                        if k != own:
                            add(tok)
        for t in writes:
            for g in self._overlaps(t.region):
                for tok in g.last_w.values():
                    add(tok)
                for tok in g.readers.values():
                    add(tok)
        return need

    def _wait(self, e, need):
        for k, (sem, val) in need.items():
            if sem is e.sem and e.name == "pe":
                continue
            if e.known.get(k, 0) >= val:
                continue
            e.h.wait_ge(sem, val)
            e.known[k] = val
            self.n_wait += 1

    def _commit(self, tok, reads, writes):
        k = id(tok[0])
        for t in reads:
            r = t.region
            if k not in r.readers or r.readers[k][1] < tok[1]:
                r.readers[k] = tok
        for t in writes:
            r = t.region
            for g in self._overlaps(r):
                if g is not r and g.lo >= r.lo and g.hi <= r.hi:
                    g.last_w = {}
                    g.readers = {}
            r.last_w = {k: tok}
            r.readers = {}

    @staticmethod
    def _split(args):
        tiles, aps = [], []
        for a in args:
            if isinstance(a, View):
                tiles.append(a.tile)
                aps.append(a.ap)
            else:
                aps.append(a)
        return tiles, aps

    def op(self, engname, fn, outs, ins, sig=None, extra_reads=(), extra_writes=()):
        e = self.eng[engname]
        wt, wa = self._split(outs)
        rt, ra = self._split(ins)
        rt = rt + [v.tile for v in extra_reads]
        wt = wt + [v.tile for v in extra_writes]
        self._wait(e, self._deps(rt, wt, id(e.sem)))
        inst = fn(e.h, *wa, *ra)
        self.n_inst += 1
        if sig is None:
            sig = e.every
        if sig:
            e.count += 1
            inst.then_inc(e.sem, 1)
            tok = (e.sem, e.count)
        else:
            tok = (e.sem, e.count + 1)
        self._commit(tok, rt, wt)
        return inst

    def dma(self, qname, out, in_, sem, **kw):
        e = self.eng[qname]
        wt, wa = self._split([out])
        rt, ra = self._split([in_])
        self._wait(e, self._deps(rt, wt))
        inst = e.h.dma_start(out=wa[0], in_=ra[0], **kw)
        inst.then_inc(sem, 16)
        self.n_inst += 1
        k = id(sem)
        self.dma_cum[k] = self.dma_cum.get(k, 0) + 16
        tok = (sem, self.dma_cum[k])
        self._commit(tok, rt, wt)
        return tok

    def retoken(self, tiles, sem):
        tok = (sem, self.dma_cum[id(sem)])
        for t in tiles:
            t.region.last_w = {id(sem): tok}

    def wait_all(self, engname, toks):
        e = self.eng[engname]
        need = {}
        for tok in toks:
            k = id(tok[0])
            if k not in need or need[k][1] < tok[1]:
                need[k] = tok
        self._wait(e, need)

    def mm(self, out, lhsT, rhs, start, stop, sig=None, **kw):
        return self.op("pe", lambda h, o, l, r: h.matmul(o, l, r, start=start, stop=stop, **kw),
                       [out], [lhsT, rhs], sig=stop if sig is None else sig)

    def tr(self, out, in_, ident, sig=False):
        return self.op("pe", lambda h, o, i, d: h.transpose(o, i, d), [out], [in_, ident], sig=sig)

    def act(self, out, in_, func, bias=None, scale=None, accum_out=None, eng="act"):
        ins = [in_]
        kw = {}
        order = []
        if bias is not None:
            if isinstance(bias, View):
                ins.append(bias); order.append("bias")
            else:
                kw["bias"] = bias
        if scale is not None:
            if isinstance(scale, View):
                ins.append(scale); order.append("scale")
            else:
                kw["scale"] = scale
        outs = [out]
        if accum_out is not None:
            outs.append(accum_out)

        def fn(h, *a):
            o = a[0]
            i0 = 1
            acc = None
            if accum_out is not None:
                acc = a[1]; i0 = 2
            k2 = dict(kw)
            for j, nm in enumerate(order):
                k2[nm] = a[i0 + 1 + j]
            if acc is not None:
                k2["accum_out"] = acc
            return h.activation(o, a[i0], func, **k2)
        return self.op(eng, fn, outs, ins)

    def tt(self, out, in0, in1, op, eng="dve"):
        return self.op(eng, lambda h, o, a, b: h.tensor_tensor(o, a, b, op), [out], [in0, in1])

    def ts(self, out, in0, s1, s2, op0, op1=None, accum_out=None, eng="dve"):
        ins = [in0]
        idx = {}
        if isinstance(s1, View):
            idx["s1"] = len(ins); ins.append(s1)
        if isinstance(s2, View):
            idx["s2"] = len(ins); ins.append(s2)
        outs = [out] + ([accum_out] if accum_out is not None else [])
        no = len(outs)

        def fn(h, *a):
            o = a[0]
            i = a[no:]
            v1 = i[idx["s1"]] if "s1" in idx else s1
            v2 = i[idx["s2"]] if "s2" in idx else s2
            kw = {}
            if op1 is not None:
                kw["op1"] = op1
            if accum_out is not None:
                kw["accum_out"] = a[1]
            return h.tensor_scalar(o, i[0], v1, v2, op0, **kw)
        return self.op(eng, fn, outs, ins)

    def stt(self, out, in0, scalar, in1, op0, op1, eng="dve"):
        ins = [in0, in1]
        if isinstance(scalar, View):
            ins.append(scalar)

        def fn(h, o, a, b, *s):
            return h.scalar_tensor_tensor(o, a, s[0] if s else scalar, b, op0, op1)
        return self.op(eng, fn, [out], ins)

    def copy(self, out, in_, eng="dve"):
        return self.op(eng, lambda h, o, i: h.tensor_copy(o, i), [out], [in_])

    def memset(self, out, val, eng="dve"):
        return self.op(eng, lambda h, o: h.memset(o, val), [out], [])

    def reduce(self, out, in_, op, axis=AX.X, eng="dve"):
        return self.op(eng, lambda h, o, i: h.tensor_reduce(o, i, axis, op), [out], [in_])

    def recip(self, out, in_):
        return self.op("dve", lambda h, o, i: h.reciprocal(o, i), [out], [in_])


from contextlib import ExitStack
from concourse.bass_utils import run_bass_kernel_spmd

D = 2048
S = 2048
T = 512
NPASS = S // T
TT = T // 128
MEM = 256
DFF = 5632
NFC = DFF // 512
EPS = 1e-6
C_IN = 8960
G0 = 2816
NSLAB = 7
SLAB_EL = 4096
PER_PASS = 117
NSLAB_TOTAL = 4 + PER_PASS

DEBUG = {}


def build(npass=NPASS, dbg=None):
    nc = bass.Bass("TRN2", target_bir_lowering=False)

    def din(name, shape, dt=F32):
        return nc.dram_tensor(name, list(shape), dt, kind="ExternalInput").ap()

    x_d = din("x", [S, D])
    mem_d = din("mem", [MEM, D])
    pos_d = din("pos_t", [128, 16], I32)
    gmix_d = din("gmixT", [128, 16])
    gffn_d = din("gffnT", [128, 16])
    gmem_d = din("gmemT", [128, 16])
    wpack_d = din("wpack", [NSLAB_TOTAL, 128, SLAB_EL])
    gav_d = din("gavT", [128, 4])
    wsT_d = din("wsT", [128, 4, 128])
    bsp_d = din("bsp", [128, 512])
    gbq_d = din("gbq", [128, 64])
    gbk_d = din("gbk", [128, 64])
    sinks_d = din("sinks", [128, 16])
    gcq_d = din("gcq", [128, 1])
    gck_d = din("gck", [128, 128])
    cw_d = din("convw", [128, 3, 88])
    cb_d = din("convb", [128, 88])
    out_d = nc.dram_tensor("out", [S, D], F32, kind="ExternalOutput").ap()
    dbg_d = None
    if dbg is not None:
        dbg_d = nc.dram_tensor("dbg", [128, dbg], F32, kind="ExternalOutput").ap()

    K = MK(nc, 206 * 1024)
    sem_c = K.new_sem("const")
    sem_x = [K.new_sem(f"x{i}") for i in range(TT)]
    sem_slab = [K.new_sem(f"slab{i}") for i in range(NSLAB)]
    sem_slabx = [K.new_sem(f"slabx{i}") for i in range(2)]
    sem_xs = [K.new_sem(f"xs{i}") for i in range(TT)]
    sem_dbg = K.new_sem("dbg")

    X1 = [K.sb_tile([128, D], F32, f"x1_{i}") for i in range(TT)]
    X1m = [[K.sb_tile([128, 512], F32, f"x1_{i}_{m}", at=X1[i].region.lo + m * 2048) for m in range(4)]
           for i in range(TT)]
    hT = [K.sb_tile([128, T], BF16, f"hT{c}") for c in range(16)]
    yTa = K.sb_tile([128, 4, T], BF16, "yTa")
    yTb = K.sb_tile([128, 8, T], BF16, "yTb")
    yTc = K.sb_tile([128, 4, T], BF16, "yTc")
    mT = [K.sb_tile([128, T], BF16, f"mT{c}") for c in range(16)]
    slabs = [K.sb_tile([128, SLAB_EL], BF16, f"slab{i}") for i in range(NSLAB)]
    xstage = [K.sb_tile([128, D], F32, f"xstage{i}", at=yTa.region.lo + i * 8192) for i in range(TT)]
    assert xstage[-1].region.hi <= mT[15].region.hi
    slabs_x = [K.sb_tile([128, SLAB_EL], BF16, f"slabx{i}", at=mT[8 * i].region.lo) for i in range(2)]
    ident = K.sb_tile([128, 128], BF16, "ident")
    gmixT = K.sb_tile([128, 16], F32, "gmixT")
    gffnT = K.sb_tile([128, 16], F32, "gffnT")
    gmemT = K.sb_tile([128, 16], F32, "gmemT")
    gavT = K.sb_tile([128, 4], F32, "gavT")
    gcq = K.sb_tile([128, 1], F32, "gcq")
    convw = K.sb_tile([128, 3, 88], F32, "convw")
    convb = K.sb_tile([128, 88], F32, "convb")
    hist = K.sb_tile([128, 88, 2], F32, "hist")
    wsT = K.sb_tile([128, 4, 128], BF16, "wsT")
    bsp = K.sb_tile([128, 4, 128], F32, "bsp")
    gbq = K.sb_tile([128, 64], F32, "gbq")
    gbk = K.sb_tile([128, 64], F32, "gbk")
    gck = K.sb_tile([128, 128], F32, "gck")
    sinkexp = K.sb_tile([128, 16], F32, "sinkexp")
    cosT = K.sb_tile([128, 16, 8], F32, "cosT")
    sinT = K.sb_tile([128, 16, 8], F32, "sinT")
    mask_cur = K.sb_tile([128, 4, 128], BF16, "mask_cur")
    mask_prev = K.sb_tile([128, 4, 128], BF16, "mask_prev")
    ones_bf = K.sb_tile([128, 128], BF16, "ones_bf")
    kTmem = K.sb_tile([128, 4, MEM], BF16, "kTmem")
    vmem = K.sb_tile([128, 2, 512], BF16, "vmem")
    KT = [K.sb_tile([128, 4, 128], BF16, f"KT{i}") for i in range(3)]
    vaug = [K.sb_tile([128, 2, 65], BF16, f"vaug{i}") for i in range(3)]
    kpad = K.sb_tile([128, 4, 128], BF16, "kpad")
    ss = K.sb_tile([128, 32], F32, "ss")
    rstd = K.sb_tile([128, 32], F32, "rstd")
    ssA = K.sb_tile([128, 8], F32, "ssA")
    ssB = K.sb_tile([128, 32], F32, "ssB")
    rstdB = K.sb_tile([128, 32], F32, "rstdB")

    base = K.sb_top
    xn = [K.sb_tile([128, D], BF16, f"xn{i}") for i in range(2)]
    junk = xn[1]
    top_norm = K.sb_top
    xs = [K.sb_tile([128, 516], F32, f"xs{i}") for i in range(2)]
    acc = [K.sb_tile([128, 512], F32, f"acc{i}") for i in range(2)]
    sa = [K.sb_tile([128, 512], BF16, f"sa{c}") for c in range(4)]
    actT = [[K.sb_tile([128, 512], BF16, f"actT{j}_{c}") for c in range(4)] for j in range(2)]
    top_f = K.sb_top
    K.sb_top = top_norm
    sg = [K.sb_tile([128, 512], BF16, f"sg{i}") for i in range(6)]
    tm = [K.sb_tile([128, 512], F32, f"tm{i}") for i in range(4)]
    top_m = K.sb_top
    K.sb_top = base
    uT = [K.sb_tile([128, T], BF16, f"uT{c}") for c in range(4)]
    vn = [K.sb_tile([128, 512], BF16, f"vn{i}") for i in range(TT)]
    vg = [K.sb_tile([128, 512], F32, f"vg{i}") for i in range(2)]
    tmpA = K.sb_tile([128, 512], F32, "tmpA")
    sq = K.sb_tile([128, 1024 + 128], F32, "sq")
    qn = K.sb_tile([128, 16, 64], F32, "qn", at=sq.region.lo)
    kn = K.sb_tile([128, 2, 64], F32, "kn")
    rt = [K.sb_tile([128, 16, 8], F32, f"rt{i}") for i in range(4)]
    qr = K.sb_tile([128, 1024], BF16, "qr")
    yb = K.sb_tile([128, 1024], BF16, "yb")
    qT = [K.sb_tile([128, 1024], BF16, f"qT{i}") for i in range(2)]
    PT = [K.sb_tile([128, 512], BF16, f"PT{i}") for i in range(8)]
    den = K.sb_tile([128, 16], F32, "den")
    sqc = K.sb_tile([128, 512], BF16, "sqc")
    msc = K.sb_tile([128, 512], F32, "msc")
    qcT = K.sb_tile([128, 512], BF16, "qcT")
    PTc = [K.sb_tile([128, 512], BF16, f"PTc{i}") for i in range(2)]
    rdc = K.sb_tile([128, 512], F32, "rdc")
    top_abc = K.sb_top
    K.sb_top = max(top_f, top_m, top_abc)
    print("SBUF bytes used per partition:", K.sb_top)

    banks = [K.ps_tile(b, 1, [128, 512], F32, f"bank{b}") for b in range(8)]
    bank_i = [0]

    def bank():
        b = banks[bank_i[0] % 7]
        bank_i[0] += 1
        return b

    def bf(bk, shape):
        v = bk.bitcast(BF16)
        n = int(np.prod(shape[1:]))
        v = v[:, 0:n]
        if len(shape) == 3:
            v = v.rearrange("p (a b) -> p a b", a=shape[1])
        return v

    slab_i = [0]

    plan = []
    plan_idx = {}
    free_slots = list(range(NSLAB))

    def load_slab(parts, xslot=None):
        if xslot is None:
            assert free_slots, "slab ring exhausted"
            idx = free_slots.pop(0)
        key = tuple(tuple(p_) for p_ in parts)
        if key not in plan_idx:
            plan_idx[key] = len(plan)
            plan.append(list(parts))
        sidx = plan_idx[key]
        slab_i[0] += 1
        if xslot is not None:
            sl = slabs_x[xslot]
            K.dma("pool", sl, wpack_d[sidx], sem_slabx[xslot])
            return sl
        sl = slabs[idx]
        K.dma("pool", sl, wpack_d[sidx], sem_slab[idx])
        return sl

    def release(*views):
        for v in views:
            if v.tile in slabs_x:
                continue
            idx = slabs.index(v.tile)
            assert idx not in free_slots
            free_slots.append(idx)

    w_in_v, wmkv_v, wo_v, wup_v, wdn_v = "w_in", "w_mem_kv", "w_out", "w_up", "w_down"
    wba_v, wbb_v, wbc_v = "w_branch_a", "w_branch_b", "w_branch_c"

    def slab_k8(view, kh, c0, xslot=None):
        sl = load_slab([(0, view, kh * 8, (kh + 1) * 8, c0, c0 + 512)], xslot=xslot)
        return sl.rearrange("p (a b) -> p a b", a=8)

    def slab_k16(view, c0):
        sl = load_slab([(0, view, 0, 16, c0, c0 + 256)])
        return sl.rearrange("p (a b) -> p a b", a=16)

    dbg_off = [0]
    dbg_st = []
    dbg_n = [0]

    def dump(v, n):
        if dbg_d is None:
            return
        if not dbg_st:
            dbg_st.extend(K.sb_tile([128, 512], F32, f"dbgst{j}") for j in range(2))
        for c0 in range(0, n, 512):
            st = dbg_st[dbg_n[0] % 2]
            dbg_n[0] += 1
            K.copy(st, v[:, c0:c0 + 512])
            K.dma("sp", dbg_d[:, dbg_off[0]:dbg_off[0] + 512], st, sem_dbg)
            dbg_off[0] += 512

    for mt in range(2):
        K.dma("sp", xstage[mt], mem_d[mt * 128:(mt + 1) * 128, :], sem_xs[mt])
    cl = []
    for t, src in ((gmixT, gmix_d), (gffnT, gffn_d), (gmemT, gmem_d), (gavT, gav_d), (gcq, gcq_d),
                   (convw, cw_d), (convb, cb_d)):
        K.dma("sp", t, src, sem_c)
        cl.append(t)
    wsT_f = K.sb_tile([128, 4, 128], F32, "wsT_f", at=top_norm + 11264)
    pos_i = K.sb_tile([128, 16], I32, "pos_i", at=top_norm + 13312)
    sinks_f = K.sb_tile([128, 16], F32, "sinks_f", at=top_norm + 13376)
    K.dma("sp", wsT_f, wsT_d, sem_c)
    K.dma("sp", pos_i, pos_d, sem_c)
    K.dma("sp", bsp.rearrange("p a b -> p (a b)"), bsp_d, sem_c)
    K.dma("sp", gbq, gbq_d, sem_c)
    K.dma("sp", gbk, gbk_d, sem_c)
    K.dma("sp", gck, gck_d, sem_c)
    K.dma("sp", sinks_f, sinks_d, sem_c)
    K.retoken(cl + [wsT_f, pos_i, bsp, gbq, gbk, gck, sinks_f], sem_c)

    for i in range(TT):
        K.dma("sp", X1[i], x_d[i * 128:(i + 1) * 128, :], sem_x[i])
    K.memset(ident, 1.0, eng="pool")
    K.op("pool", lambda h, o, i: h.affine_select(o, i, [[-1, 128]], ALU.is_equal, 0.0, base=0,
                                                channel_multiplier=1), [ident], [ident])
    wk = [[slab_k8(wmkv_v, kh, n * 512) for kh in range(2)] for n in range(2)]
    junk2 = K.sb_tile([128, D], BF16, "junk2", at=actT[0][0].region.lo)

    def norm_sumsq(j, src, alt_junk=False):
        if j % 2 == 0:
            K.act(junk2 if alt_junk else junk, src, AF.Square, accum_out=ss[:, j:j + 1])
        else:
            K.op("dve", lambda h, o, acc_, a_, b_: h.scalar_tensor_tensor(o, a_, 1.0, b_, ALU.mult, ALU.mult,
                                                                          accum_out=acc_),
                 [junk2 if alt_junk else xn[0], ss[:, j:j + 1]], [src, src])

    def norm_rstd(n):
        K.ts(ss[:, 8:8 + n], ss[:, 0:n], 1.0 / D, EPS, ALU.mult, ALU.add)
        K.act(ss[:, 16:16 + n], ss[:, 8:8 + n], AF.Sqrt)
        K.recip(rstd[:, 0:n], ss[:, 16:16 + n])

    def norm_scale(j, src):
        if j % 2 == 0:
            K.act(xn[j % 2], src, AF.Identity, scale=rstd[:, j:j + 1])
        else:
            K.ts(xn[j % 2], src, rstd[:, j:j + 1], None, ALU.mult)

    def norm_transpose(j, gT, dstT):
        x_ = xn[j % 2]
        for hb in range(2):
            bk = bank()
            bv = bf(bk, [128, 8, 128])
            for jj in range(8):
                c = hb * 8 + jj
                K.tr(bv[:, jj, :], x_[:, c * 128:(c + 1) * 128], ident, sig=(jj == 7))
            for jj in range(8):
                c = hb * 8 + jj
                if hb == 0:
                    K.ts(dstT(c)[:, j * 128:(j + 1) * 128], bv[:, jj, :], gT[:, c:c + 1], None, ALU.mult)
                else:
                    K.act(dstT(c)[:, j * 128:(j + 1) * 128], bv[:, jj, :], AF.Identity, scale=gT[:, c:c + 1])

    def norm_tiles(srcs, gT, dstT, have_stats=False, prescaled=0):
        n = len(srcs)
        if not have_stats:
            K.memset(ss[:, 0:n], 0.0)
            for j, src in enumerate(srcs):
                norm_sumsq(j, src)
            norm_rstd(n)
        for j, src in enumerate(srcs):
            if j >= prescaled:
                norm_scale(j, src)
            norm_transpose(j, gT, dstT)

    memT = K.sb_tile([128, 16, MEM], BF16, "memT", at=top_norm)
    norm_tiles([xstage[0], xstage[1]], gmemT, lambda c: memT[:, c, :])
    kfm = K.sb_tile([128, 512], F32, "kfm", at=top_norm + 8192)
    kbm = K.sb_tile([128, 512], BF16, "kbm", at=top_norm + 10240)
    for mt in range(2):
        bk_k, bk_v = bank(), bank()
        for n, bk in ((0, bk_k), (1, bk_v)):
            for k in range(16):
                K.mm(bk, memT[:, k, mt * 128:(mt + 1) * 128], wk[n][k // 8][:, k % 8, :], k == 0, k == 15)
        K.copy(vmem[:, mt, :], bk_v)
        K.act(kfm, bk_k, AF.Square)
        K.reduce(ss[:, 4:8], kfm.rearrange("p (h d) -> p h d", h=4), ALU.add)
        K.ts(ss[:, 8:12], ss[:, 4:8], 1.0 / 128, EPS, ALU.mult, ALU.add)
        K.act(ss[:, 12:16], ss[:, 8:12], AF.Sqrt)
        K.recip(rstd[:, 4:8], ss[:, 12:16])
        K.tt(kfm.rearrange("p (h d) -> p h d", h=4), bk_k.rearrange("p (h d) -> p h d", h=4),
             rstd[:, 4:8].unsqueeze(2).broadcast_to([128, 4, 128]), ALU.mult)
        K.tt(kbm.rearrange("p (h d) -> p h d", h=4), kfm.rearrange("p (h d) -> p h d", h=4),
             gck.unsqueeze(1).broadcast_to([128, 4, 128]), ALU.mult)
        bk = bank()
        bv = bf(bk, [128, 4, 128])
        for h in range(4):
            K.tr(bv[:, h, :], kbm[:, h * 128:(h + 1) * 128], ident, sig=(h == 3))
        K.copy(kTmem[:, :, mt * 128:(mt + 1) * 128], bv)
    release(wk[0][0], wk[0][1], wk[1][0], wk[1][1])

    K.memset(mask_cur, 0.0, eng="pool")
    K.op("pool", lambda h, o, i: h.affine_select(o, i, [[0, 4], [1, 128]], ALU.is_ge, -30000.0, base=0,
                                                channel_multiplier=-1), [mask_cur], [mask_cur])
    K.memset(mask_prev, 0.0, eng="pool")
    K.op("pool", lambda h, o, i: h.affine_select(o, i, [[0, 4], [-1, 128]], ALU.is_gt, -30000.0, base=0,
                                                channel_multiplier=1), [mask_prev], [mask_prev])
    K.memset(ones_bf, 1.0, eng="pool")
    K.op("pool", lambda h, o, i: h.affine_select(o, i, [[0, 4], [1, 128]], ALU.is_ge, 0.0, base=0,
                                                channel_multiplier=-1), [wsT_f], [wsT_f])
    K.copy(wsT, wsT_f)
    K.memset(hist, 0.0)
    K.memset(kpad, 0.0)
    for i in range(3):
        K.memset(vaug[i], 1.0)
        K.memset(KT[i], 0.0)
    K.ts(gcq, gcq, 128.0 ** -0.5, None, ALU.mult)
    K.act(sinkexp, sinks_f, AF.Exp)

    ang = K.sb_tile([128, 16, 8], F32, "ang", at=top_norm + 13440)
    posf = K.sb_tile([128, 16], F32, "posf", at=top_norm + 13952)
    rr = K.sb_tile([128, 16, 8], F32, "rr", at=top_norm + 14464)
    rf = K.sb_tile([128, 16, 8], F32, "rf", at=top_norm + 14976)
    ri = K.sb_tile([128, 16, 8], I32, "ri", at=top_norm + 15488)
    mk_ = K.sb_tile([128, 16, 8], F32, "mk_", at=top_norm + 16000)
    K.copy(posf, pos_i)
    half = 8
    inv = (np.float32(500000.0) ** (-(np.arange(half, dtype=np.float32) / np.float32(half)))).astype(np.float32)
    for j in range(half):
        K.ts(ang[:, :, j], posf, float(inv[j]), None, ALU.mult)
    TWO_PI = 6.283185307179586
    for dst, shift in ((sinT, 0.0), (cosT, 0.25)):
        K.ts(rr, ang, 1.0 / TWO_PI, shift, ALU.mult, ALU.add)
        K.copy(ri, rr)
        K.copy(rf, ri)
        K.tt(rr, rr, rf, ALU.subtract)
        K.ts(mk_, rr, 0.5, None, ALU.is_gt)
        K.tt(rr, rr, mk_, ALU.subtract)
        K.ts(mk_, rr, -0.5, None, ALU.is_lt)
        K.tt(rr, rr, mk_, ALU.add)
        K.act(dst, rr, AF.Sin, scale=TWO_PI)


    pre_b = []
    for p in range(npass):
        t0 = p * T
        if p == 0:
            norm_tiles(X1, gmixT, lambda c: hT[c])
        else:
            norm_tiles(xstage, gmixT, lambda c: hT[c], have_stats=True, prescaled=2)
        if p == 0 and dbg_d is not None and "hT" in DEBUG:
            for c in range(16):
                dump(hT[c], T)

        def gen_a():
            Wu = [slab_k8(w_in_v, kh, 0) for kh in range(2)]
            yield
            for c in range(4):
                bk = bank()
                for k in range(16):
                    K.mm(bk, Wu[k // 8][:, k % 8, c * 128:(c + 1) * 128], hT[k], k == 0, k == 15)
                K.act(uT[c], bk, AF.Gelu_apprx_tanh)
                if c == 3:
                    release(*Wu)
                    Wv = [slab_k8(w_in_v, kh, 512) for kh in range(2)]
                yield
            for i in range(TT):
                bk = bank()
                for k in range(16):
                    K.mm(bk, hT[k][:, i * 128:(i + 1) * 128], Wv[k // 8][:, k % 8, :], k == 0, k == 15)
                g_ = vg[i % 2]
                K.act(g_, bk, AF.Gelu_apprx_tanh)
                K.memset(ssA[:, 0:1], 0.0)
                K.act(tmpA.bitcast(BF16)[:, 0:512], g_, AF.Square, accum_out=ssA[:, 0:1])
                K.ts(ssA[:, 1:2], ssA[:, 0:1], 1.0 / 512, EPS, ALU.mult, ALU.add)
                K.act(ssA[:, 2:3], ssA[:, 1:2], AF.Sqrt)
                K.recip(ssA[:, 3:4], ssA[:, 2:3])
                K.ts(vn[i], g_, ssA[:, 3:4], None, ALU.mult)
                if i == TT - 1:
                    release(*Wv)
                yield
            for g in range(4):
                bk = bank()
                for i in range(TT):
                    K.mm(bk[:, i * 128:(i + 1) * 128], vn[i][:, g * 128:(g + 1) * 128], wsT[:, g, :], True, True,
                         sig=(i == TT - 1))
                K.stt(tmpA.rearrange("p (a b) -> p a b", a=4), bk.rearrange("p (a b) -> p a b", a=4),
                      gavT[:, g:g + 1], bsp[:, g:g + 1, :].broadcast_to([128, 4, 128]), ALU.mult, ALU.add)
                K.tt(yTa[:, g, :], tmpA, uT[g], ALU.mult)
                yield

        def load_b():
            return ([[slab_k8(w_in_v, kh, 1024 + n * 512) for kh in range(2)] for n in range(2)],
                    slab_k16(w_in_v, 2048))

        def gen_b():
            Wq, Wkv = pre_b.pop() if pre_b else load_b()
            yield

            def stage_x(i):
                gi = p * TT + i
                cur = gi % 3
                bq = [bank(), bank()]
                bkv = bank()
                for n in range(2):
                    for k in range(16):
                        K.mm(bq[n], hT[k][:, i * 128:(i + 1) * 128], Wq[n][k // 8][:, k % 8, :], k == 0, k == 15)
                for k in range(16):
                    K.mm(bkv[:, 0:256], hT[k][:, i * 128:(i + 1) * 128], Wkv[:, k, :], k == 0, k == 15)
                for n in range(2):
                    K.act(sq[:, n * 512:(n + 1) * 512], bq[n], AF.Square)
                K.act(sq[:, 1024:1152], bkv[:, 0:128], AF.Square)
                K.act(vaug[cur][:, :, 0:64], bkv[:, 128:256].rearrange("p (g d) -> p g d", g=2), AF.Copy)
                K.reduce(ssB[:, 0:18], sq.rearrange("p (h d) -> p h d", h=18), ALU.add)
                K.ts(ssB[:, 0:18], ssB[:, 0:18], 1.0 / 64, EPS, ALU.mult, ALU.add)
                K.act(rstdB[:, 0:18], ssB[:, 0:18], AF.Sqrt)
                K.recip(rstdB[:, 0:18], rstdB[:, 0:18])
                K.ts(rstdB[:, 0:16], rstdB[:, 0:16], 0.125, None, ALU.mult)
                for n in range(2):
                    K.tt(qn[:, n * 8:(n + 1) * 8, :], bq[n].rearrange("p (h d) -> p h d", h=8),
                         rstdB[:, n * 8:(n + 1) * 8].unsqueeze(2).broadcast_to([128, 8, 64]), ALU.mult)
                K.tt(kn, bkv[:, 0:128].rearrange("p (g d) -> p g d", g=2),
                     rstdB[:, 16:18].unsqueeze(2).broadcast_to([128, 2, 64]), ALU.mult)
                yield
                PE_ = "dve"
                K.tt(qn, qn, gbq.unsqueeze(1).broadcast_to([128, 16, 64]), ALU.mult, eng=PE_)
                K.tt(kn, kn, gbk.unsqueeze(1).broadcast_to([128, 2, 64]), ALU.mult, eng=PE_)
                cs = cosT[:, gi, :].unsqueeze(1)
                sn = sinT[:, gi, :].unsqueeze(1)
                qr3 = qr.rearrange("p (h d) -> p h d", h=16)
                K.copy(qr3[:, :, 16:64], qn[:, :, 16:64], eng=PE_)
                K.tt(rt[0], qn[:, :, 0:8], cs.broadcast_to([128, 16, 8]), ALU.mult, eng=PE_)
                K.tt(rt[1], qn[:, :, 8:16], sn.broadcast_to([128, 16, 8]), ALU.mult, eng=PE_)
                K.tt(qr3[:, :, 0:8], rt[0], rt[1], ALU.subtract, eng=PE_)
                K.tt(rt[2], qn[:, :, 8:16], cs.broadcast_to([128, 16, 8]), ALU.mult, eng=PE_)
                K.tt(rt[3], qn[:, :, 0:8], sn.broadcast_to([128, 16, 8]), ALU.mult, eng=PE_)
                K.tt(qr3[:, :, 8:16], rt[2], rt[3], ALU.add, eng=PE_)
                kd = kpad.rearrange("p (g q) d -> p g q d", g=2)
                d0 = kd[:, :, 0, 0:64]
                K.copy(d0[:, :, 16:64], kn[:, :, 16:64], eng=PE_)
                K.tt(rt[0][:, 0:2, :], kn[:, :, 0:8], cs.broadcast_to([128, 2, 8]), ALU.mult, eng=PE_)
                K.tt(rt[1][:, 0:2, :], kn[:, :, 8:16], sn.broadcast_to([128, 2, 8]), ALU.mult, eng=PE_)
                K.tt(d0[:, :, 0:8], rt[0][:, 0:2, :], rt[1][:, 0:2, :], ALU.subtract, eng=PE_)
                K.tt(rt[2][:, 0:2, :], kn[:, :, 8:16], cs.broadcast_to([128, 2, 8]), ALU.mult, eng=PE_)
                K.tt(rt[3][:, 0:2, :], kn[:, :, 0:8], sn.broadcast_to([128, 2, 8]), ALU.mult, eng=PE_)
                K.tt(d0[:, :, 8:16], rt[2][:, 0:2, :], rt[3][:, 0:2, :], ALU.add, eng=PE_)
                K.copy(kd[:, :, 1, 64:128], d0, eng=PE_)
                yield
                bk = bank()
                bv = bf(bk, [128, 4, 128])
                for j in range(4):
                    K.tr(bv[:, j, :], kpad[:, j, :], ident, sig=(j == 3))
                K.copy(KT[cur], bv)
                bk = bank()
                bv = bf(bk, [128, 8, 128])
                for j in range(8):
                    K.tr(bv[:, j, :], qr[:, j * 128:(j + 1) * 128], ident, sig=(j == 7))
                K.act(qT[i % 2].rearrange("p (a b) -> p a b", a=8), bv, AF.Copy)
                yield

            def stage_y(i):
                gi = p * TT + i
                cur, prv = gi % 3, (gi - 1) % 3
                qT_ = qT[i % 2]
                blocks = ([(prv, mask_prev)] if gi > 0 else []) + [(cur, mask_cur)]
                pt_of = {}
                pi = 0
                for g in range(2):
                    for (kb, msk) in blocks:
                        for par in range(2):
                            bs = bank()
                            K.mm(bs, KT[kb][:, 2 * g + par, :], qT_[:, g * 512:(g + 1) * 512], True, False)
                            K.mm(bs, ident, msk.rearrange("p a b -> p (a b)"), False, True)
                            ptile = PT[pi]
                            pi += 1
                            K.act(ptile, bs, AF.Exp)
                            pt_of[(g, kb, par)] = ptile
                    yield
                ob = [bank() for _ in range(4)]
                for h in range(16):
                    g, jj, par = h // 8, (h % 8) // 2, h % 2
                    o = ob[h // 4].rearrange("p (a b) -> p a b", a=4)[:, h % 4, 0:65]
                    for bi, (kb, msk) in enumerate(blocks):
                        K.mm(o, pt_of[(g, kb, par)][:, jj * 128:(jj + 1) * 128], vaug[kb][:, g, :],
                             bi == 0, bi == len(blocks) - 1, sig=(bi == len(blocks) - 1 and h % 4 == 3))
                for b4 in range(4):
                    o3 = ob[b4].rearrange("p (a b) -> p a b", a=4)
                    K.tt(den[:, b4 * 4:(b4 + 1) * 4], o3[:, :, 64], sinkexp[:, b4 * 4:(b4 + 1) * 4], ALU.add)
                K.recip(den, den)
                for b4 in range(4):
                    o3 = ob[b4].rearrange("p (a b) -> p a b", a=4)
                    K.tt(yb[:, b4 * 256:(b4 + 1) * 256].rearrange("p (a b) -> p a b", a=4), o3[:, :, 0:64],
                         den[:, b4 * 4:(b4 + 1) * 4].unsqueeze(2).broadcast_to([128, 4, 64]), ALU.mult)
                yield
                bk = bank()
                bv = bf(bk, [128, 8, 128])
                for j in range(8):
                    K.tr(bv[:, j, :], yb[:, j * 128:(j + 1) * 128], ident, sig=(j == 7))
                K.act(yTb[:, :, i * 128:(i + 1) * 128], bv, AF.Copy)
                yield

            def chain_x():
                for i in range(TT):
                    while y_done[0] < i - 1:
                        yield
                    yield from stage_x(i)
                    x_done[0] = i + 1
                release(Wq[0][0], Wq[0][1], Wq[1][0], Wq[1][1], Wkv)

            def chain_y():
                for i in range(TT):
                    while x_done[0] < i + 1:
                        yield
                    yield from stage_y(i)
                    y_done[0] = i + 1

            b_chains.extend([chain_x(), chain_y()])

        def gen_c():
            Wqc = [slab_k8(w_in_v, kh, 2304, xslot=kh) for kh in range(2)]
            yield
            for h in range(4):
                bqc = banks[7]
                for k in range(16):
                    K.mm(bqc, Wqc[k // 8][:, k % 8, h * 128:(h + 1) * 128], hT[k], k == 0, k == 15)
                K.act(sqc, bqc, AF.Square)
                yield
                bss = bank()
                K.mm(bss, ones_bf, sqc, True, True)
                K.ts(msc, bss, 1.0 / 128, EPS, ALU.mult, ALU.add)
                K.act(msc, msc, AF.Sqrt)
                K.recip(msc, msc)
                K.stt(qcT, bqc, gcq[:, 0:1], msc, ALU.mult, ALU.mult)
                yield
                for mt in range(2):
                    bs = bank()
                    K.mm(bs, kTmem[:, h, mt * 128:(mt + 1) * 128], qcT, True, True)
                    K.act(PTc[mt], bs, AF.Exp)
                yield
                bo, bd = bank(), bank()
                for mt in range(2):
                    K.mm(bo, vmem[:, mt, h * 128:(h + 1) * 128], PTc[mt], mt == 0, mt == 1)
                for mt in range(2):
                    K.mm(bd, ones_bf, PTc[mt], mt == 0, mt == 1)
                K.recip(rdc, bd)
                K.tt(yTc[:, h, :], bo, rdc, ALU.mult)
                if h == 3:
                    release(*Wqc)
                yield

        x_done, y_done, b_chains = [0], [0], []
        ga, gb, gc = gen_a(), gen_b(), gen_c()
        next(gb)
        for _ in gb:
            pass
        gens = [b_chains[0], b_chains[1], ga, gc]
        c_started = True
        while gens:
            for g_ in list(gens):
                try:
                    next(g_)
                except StopIteration:
                    gens.remove(g_)
                    if g_ is ga and not c_started:
                        gens.append(gc)
                        c_started = True
        if p == 0 and dbg_d is not None:
            if "yTa" in DEBUG:
                for c in range(4):
                    dump(yTa[:, c, :], T)
            if "yTb" in DEBUG:
                for c in range(8):
                    dump(yTb[:, c, :], T)
            if "yTc" in DEBUG:
                for c in range(4):
                    dump(yTc[:, c, :], T)

        for np_ in range(8):
            Gs = [slab_k16(w_in_v, G0 + gidx * D + np_ * 256) for gidx in range(3)]
            c0 = np_ * 256
            BR = load_slab([(0, wba_v, 0, 4, c0, c0 + 256),
                            (1024, wbb_v, 0, 8, c0, c0 + 256),
                            (3072, wbc_v, 0, 4, c0, c0 + 256)]).rearrange("p (a b) -> p a b", a=16)
            ysrc = [(yTa, 0, 4), (yTb, 4, 8), (yTc, 12, 4)]
            for nn in range(2):
                n = np_ * 2 + nn
                cols = slice(nn * 128, (nn + 1) * 128)
                tms = []
                for gidx in range(3):
                    bg = bank()
                    for k in range(16):
                        K.mm(bg, Gs[gidx][:, k, cols], hT[k], k == 0, k == 15)
                    s_ = sg[(n * 3 + gidx) % 6]
                    K.act(s_, bg, AF.Sigmoid)
                    yt, boff, nk = ysrc[gidx]
                    bb = bank()
                    for kc in range(nk):
                        K.mm(bb, BR[:, boff + kc, cols], yt[:, kc, :], kc == 0, kc == nk - 1)
                    t_ = tm[gidx]
                    K.tt(t_, bb, s_, ALU.mult)
                    tms.append(t_)
                K.tt(tm[3], tms[0], tms[1], ALU.add)
                K.tt(mT[n], tm[3], tms[2], ALU.add)
            release(Gs[0], Gs[1], Gs[2], BR)
        if p == 0 and dbg_d is not None and "mT" in DEBUG:
            for c in range(16):
                dump(mT[c], T)

        for m in range(4):
            Wo = [slab_k8(wo_v, kh, m * 512) for kh in range(2)]
            for i in range(TT):
                bk = bank()
                for k in range(16):
                    K.mm(bk, mT[k][:, i * 128:(i + 1) * 128], Wo[k // 8][:, k % 8, :], k == 0, k == 15)
                K.tt(X1m[i][m], X1m[i][m], bk, ALU.add)
                if m == 3:
                    if i == 0:
                        K.memset(ss[:, 0:TT], 0.0)
                    norm_sumsq(i, X1[i], alt_junk=True)
                    K.ts(ss[:, 8 + i:9 + i], ss[:, i:i + 1], 1.0 / D, EPS, ALU.mult, ALU.add)
                    K.act(ss[:, 16 + i:17 + i], ss[:, 8 + i:9 + i], AF.Sqrt)
                    K.recip(rstd[:, i:i + 1], ss[:, 16 + i:17 + i])
                    if i >= 2:
                        norm_transpose(i - 2, gffnT, lambda c: hT[c])
                    norm_scale(i, X1[i])
            release(*Wo)
        if p == 0 and dbg_d is not None and "x1" in DEBUG:
            for i in range(TT):
                dump(X1[i], D)

        norm_transpose(2, gffnT, lambda c: hT[c])
        norm_transpose(3, gffnT, lambda c: hT[c])

        def f_up(fc, ab):
            aT = actT[fc % 2]
            U = [slab_k8(wup_v, kh, ab * DFF + fc * 512) for kh in range(2)]
            for c in range(4):
                cg = ab * 44 + fc * 4 + c
                bk = bank()
                for k in range(16):
                    K.mm(bk, U[k // 8][:, k % 8, c * 128:(c + 1) * 128], hT[k], k == 0, k == 15)
                x_ = xs[c % 2]
                a_ = acc[c % 2]
                K.copy(x_[:, 0:2], hist[:, cg, :])
                K.act(x_[:, 2:514], bk, AF.Copy)
                K.copy(hist[:, cg, :], x_[:, 512:514])
                K.act(a_, bk, AF.Identity, bias=convb[:, cg:cg + 1], scale=convw[:, 2, cg:cg + 1])
                K.stt(a_, x_[:, 1:513], convw[:, 1, cg:cg + 1], a_, ALU.mult, ALU.add)
                K.stt(a_, x_[:, 0:512], convw[:, 0, cg:cg + 1], a_, ALU.mult, ALU.add)
                if ab == 0:
                    K.act(sa[c], a_, AF.Silu)
                else:
                    K.tt(aT[c], sa[c], a_, ALU.mult)
            release(*U)

        def load_dn(fc):
            return [load_slab([(0, wdn_v, fc * 4 + 2 * j, fc * 4 + 2 * j + 2, 0, 2048)]).rearrange(
                "p (a b) -> p a b", a=2) for j in range(2)]

        def f_down(fc, Dn=None):
            aT = actT[fc % 2]
            if Dn is None:
                Dn = load_dn(fc)
            for i in range(TT):
                for m in range(4):
                    bk = bank()
                    for c in range(4):
                        K.mm(bk, aT[c][:, i * 128:(i + 1) * 128], Dn[c // 2][:, c % 2, m * 512:(m + 1) * 512],
                             c == 0, c == 3)
                    K.tt(X1m[i][m], X1m[i][m], bk, ALU.add)
            release(*Dn)

        f_up(0, 0)
        f_up(0, 1)
        if p + 1 < npass:
            for i in range(TT):
                K.dma("sp", xstage[i], x_d[t0 + T + i * 128:t0 + T + (i + 1) * 128, :], sem_xs[i])
        for fc in range(NFC):
            if fc + 1 < NFC:
                f_up(fc + 1, 0)
            if fc == NFC - 1 and p + 1 < npass:
                K.memset(ss[:, 0:TT], 0.0)
                for j in range(TT):
                    norm_sumsq(j, xstage[j])
                norm_rstd(TT)
                norm_scale(0, xstage[0])
                norm_scale(1, xstage[1])
                dn_last = load_dn(fc)
                pre_b.append(load_b())
                f_down(fc, dn_last)
                continue
            f_down(fc)
            if fc + 1 < NFC:
                f_up(fc + 1, 1)
        for i in range(TT):
            K.dma("sp", out_d[t0 + i * 128:t0 + (i + 1) * 128, :], X1[i], sem_x[i])
        if p + 1 < npass:
            for i in (2, 3, 0, 1):
                K.dma("sp", X1[i], xstage[i], sem_x[i])

    toks = [(sem_x[i], K.dma_cum[id(sem_x[i])]) for i in range(TT)]
    if dbg_d is not None and id(sem_dbg) in K.dma_cum:
        toks.append((sem_dbg, K.dma_cum[id(sem_dbg)]))
    K.wait_all("sp", toks)
    print("instructions", K.n_inst, "waits", K.n_wait, "slabs", slab_i[0], "dbg cols", dbg_off[0])
    assert len(plan) == NSLAB_TOTAL, len(plan)
    nc.mk_plan = plan
    return nc


_PLAN = []


def pack_weights(inputs, plan):
    wp = np.zeros((len(plan), 128, SLAB_EL), dtype=np.float32)
    views = {}
    for si, parts in enumerate(plan):
        for off, name, k0, k1, c0, c1 in parts:
            if name not in views:
                w = np.asarray(inputs[name][0], dtype=np.float32)
                views[name] = w.reshape(w.shape[0] // 128, 128, w.shape[1])
            blk = views[name][k0:k1, :, c0:c1]
            n = (k1 - k0) * (c1 - c0)
            wp[si, :, off:off + n] = np.transpose(blk, (1, 0, 2)).reshape(128, n)
    return wp


def prep_inputs(inputs, plan):
    f = lambda a: np.ascontiguousarray(np.asarray(a, dtype=np.float32))

    def rep(v, n):
        return np.ascontiguousarray(np.broadcast_to(f(v).reshape(1, n), (128, n)))

    def colT(v, nchunk):
        return np.ascontiguousarray(f(v).reshape(nchunk, 128).T)
    shared = {
        "gmixT": colT(inputs["g_mix"][0], 16),
        "gffnT": colT(inputs["g_ffn"][0], 16),
        "gmemT": colT(inputs["g_mem"][0], 16),
        "gavT": colT(inputs["g_a_v"][0], 4),
        "wsT": np.ascontiguousarray(np.transpose(f(inputs["w_spatial"][0]), (2, 0, 1))),
        "bsp": rep(inputs["b_spatial"][0], 512),
        "gbq": rep(inputs["g_b_q"][0], 64),
        "gbk": rep(inputs["g_b_k"][0], 64),
        "sinks": rep(inputs["sinks"][0], 16),
        "gcq": f(inputs["g_c_q"][0]).reshape(128, 1),
        "gck": rep(inputs["g_c_k"][0], 128),
        "convw": np.ascontiguousarray(np.transpose(f(inputs["conv_w"][0]).reshape(3, 88, 128), (2, 0, 1))),
        "convb": colT(inputs["conv_b"][0], 88),
    }
    shared["wpack"] = pack_weights(inputs, plan)
    x = np.asarray(inputs["x"], dtype=np.float32)
    mem = np.asarray(inputs["mem"], dtype=np.float32)
    pos = np.asarray(inputs["positions"], dtype=np.int32)
    maps = []
    for b in range(x.shape[0]):
        m = dict(shared)
        m["x"] = np.ascontiguousarray(x[b])
        m["mem"] = np.ascontiguousarray(mem[b])
        m["pos_t"] = np.ascontiguousarray(pos[b].reshape(16, 128).T)
        maps.append(m)
    return maps


def kernel(**inputs):
    nc = build()
    maps = prep_inputs(inputs, nc.mk_plan)
    res = run_bass_kernel_spmd(nc, maps, core_ids=list(range(len(maps))))
    return np.stack([np.asarray(r["out"], dtype=np.float32) for r in res.results], axis=0)
```

```python
import numpy as np
import concourse.bass as bass
import concourse.mybir as mybir

F32 = mybir.dt.float32
BF16 = mybir.dt.bfloat16
I32 = mybir.dt.int32
AF = mybir.ActivationFunctionType
ALU = mybir.AluOpType
AX = mybir.AxisListType
DSZ = {F32: 4, BF16: 2, I32: 4}
PAGE = 256


class Region:
    __slots__ = ("space", "lo", "hi", "last_w", "readers", "name")

    def __init__(self, space, lo, hi, name=""):
        self.space, self.lo, self.hi, self.name = space, lo, hi, name
        self.last_w = {}
        self.readers = {}


class View:
    __slots__ = ("tile", "ap")

    def __init__(self, tile, ap):
        self.tile, self.ap = tile, ap

    def __getitem__(self, key):
        return View(self.tile, self.ap[key])

    def bitcast(self, dt):
        return View(self.tile, self.ap.bitcast(dt))

    def rearrange(self, s, **kw):
        return View(self.tile, self.ap.rearrange(s, **kw))

    def broadcast_to(self, shape):
        return View(self.tile, self.ap.broadcast_to(list(shape)))

    def unsqueeze(self, axis):
        return View(self.tile, self.ap.unsqueeze(axis))


class Tile(View):
    __slots__ = ("region", "shape", "dtype")

    def __init__(self, K, space, base_ap, lo, hi, shape, dtype, name=""):
        self.tile = self
        self.ap = base_ap
        self.region = Region(space, lo, hi, name)
        self.shape, self.dtype = shape, dtype
        K._register(self.region)


class Eng:
    def __init__(self, name, h, sem, every):
        self.name, self.h, self.sem, self.every = name, h, sem, every
        self.count = 0
        self.known = {}


class MK:
    def __init__(self, nc, sb_bytes=200 * 1024):
        self.nc = nc
        self.sb_bytes = sb_bytes
        self.sb = nc.alloc_sbuf_tensor("mk_sb", [128, sb_bytes // 4], F32)
        self.ps = nc.alloc_psum_tensor("mk_ps", [128, 8 * 512], F32)
        self.sb_ap = self.sb.ap() if hasattr(self.sb, "ap") else self.sb[:]
        self.ps_ap = self.ps.ap() if hasattr(self.ps, "ap") else self.ps[:]
        self.pages = {"sb": {}, "ps": {}}
        self._sems = []
        self.eng = {}
        for name, h, every in (("pe", nc.tensor, False), ("act", nc.scalar, True),
                               ("dve", nc.vector, True), ("pool", nc.gpsimd, True),
                               ("sp", nc.sync, True)):
            self.eng[name] = Eng(name, h, self.new_sem("c_" + name), every)
        self.dma_cum = {}
        self.sb_top = 0
        self.n_wait = 0
        self.n_inst = 0

    def new_sem(self, name):
        s = self.nc.alloc_semaphore(name)
        self._sems.append(s)
        return s

    def sb_tile(self, shape, dtype, name="", at=None):
        n = int(np.prod(shape[1:])) * DSZ[dtype]
        n = (n + 31) // 32 * 32
        if at is None:
            at = self.sb_top
            self.sb_top += n
        assert at + n <= self.sb_bytes, f"SBUF overflow {name} {at + n}"
        ap = self.sb_ap[0:shape[0], at // 4:(at + n) // 4]
        if dtype != F32:
            ap = ap.bitcast(dtype)
        nel = int(np.prod(shape[1:]))
        ap = ap[:, 0:nel]
        if len(shape) > 2:
            names = " ".join(f"d{i}" for i in range(len(shape) - 1))
            ap = ap.rearrange(f"p ({names}) -> p {names}",
                              **{f"d{i}": shape[i + 1] for i in range(len(shape) - 2)})
        return Tile(self, "sb", ap, at, at + n, list(shape), dtype, name)

    def ps_tile(self, bank, nbanks=1, shape=None, dtype=F32, name=""):
        lo = bank * 2048
        hi = (bank + nbanks) * 2048
        ap = self.ps_ap[:, bank * 512:(bank + nbanks) * 512]
        if dtype != F32:
            ap = ap.bitcast(dtype)
        if shape is not None:
            nel = int(np.prod(shape[1:]))
            ap = ap[0:shape[0], 0:nel]
            if len(shape) > 2:
                names = " ".join(f"d{i}" for i in range(len(shape) - 1))
                ap = ap.rearrange(f"p ({names}) -> p {names}",
                                  **{f"d{i}": shape[i + 1] for i in range(len(shape) - 2)})
        return Tile(self, "ps", ap, lo, hi, shape, dtype, name)

    def _register(self, r):
        pg = self.pages[r.space]
        for p in range(r.lo // PAGE, (r.hi - 1) // PAGE + 1):
            pg.setdefault(p, []).append(r)

    def _overlaps(self, r):
        pg = self.pages[r.space]
        seen = {id(r): r}
        for p in range(r.lo // PAGE, (r.hi - 1) // PAGE + 1):
            for g in pg.get(p, ()):
                if id(g) not in seen and g.lo < r.hi and r.lo < g.hi:
                    seen[id(g)] = g
        return seen.values()

    def _deps(self, reads, writes, own=None):
        need = {}

        def add(tok):
            k = id(tok[0])
            if k not in need or need[k][1] < tok[1]:
                need[k] = tok
        for t in reads:
            for g in self._overlaps(t.region):
                for tok in g.last_w.values():
                    add(tok)
                if g.space == "ps":
                    for k, tok in g.readers.items():
                        if k != own:
                            add(tok)
        for t in writes:
            for g in self._overlaps(t.region):
                for tok in g.last_w.values():
                    add(tok)
                for tok in g.readers.values():
                    add(tok)
        return need

    def _wait(self, e, need):
        for k, (sem, val) in need.items():
            if sem is e.sem and e.name == "pe":
                continue
            if e.known.get(k, 0) >= val:
                continue
            e.h.wait_ge(sem, val)
            e.known[k] = val
            self.n_wait += 1

    def _commit(self, tok, reads, writes):
        k = id(tok[0])
        for t in reads:
            r = t.region
            if k not in r.readers or r.readers[k][1] < tok[1]:
                r.readers[k] = tok
        for t in writes:
            r = t.region
            for g in self._overlaps(r):
                if g is not r and g.lo >= r.lo and g.hi <= r.hi:
                    g.last_w = {}
                    g.readers = {}
            r.last_w = {k: tok}
            r.readers = {}

    @staticmethod
    def _split(args):
        tiles, aps = [], []
        for a in args:
            if isinstance(a, View):
                tiles.append(a.tile)
                aps.append(a.ap)
            else:
                aps.append(a)
        return tiles, aps

    def op(self, engname, fn, outs, ins, sig=None, extra_reads=(), extra_writes=()):
        e = self.eng[engname]
        wt, wa = self._split(outs)
        rt, ra = self._split(ins)
        rt = rt + [v.tile for v in extra_reads]
        wt = wt + [v.tile for v in extra_writes]
        self._wait(e, self._deps(rt, wt, id(e.sem)))
        inst = fn(e.h, *wa, *ra)
        self.n_inst += 1
        if sig is None:
            sig = e.every
        if sig:
            e.count += 1
            inst.then_inc(e.sem, 1)
            tok = (e.sem, e.count)
        else:
            tok = (e.sem, e.count + 1)
        self._commit(tok, rt, wt)
        return inst

    def dma(self, qname, out, in_, sem, **kw):
        e = self.eng[qname]
        wt, wa = self._split([out])
        rt, ra = self._split([in_])
        self._wait(e, self._deps(rt, wt))
        inst = e.h.dma_start(out=wa[0], in_=ra[0], **kw)
        inst.then_inc(sem, 16)
        self.n_inst += 1
        k = id(sem)
        self.dma_cum[k] = self.dma_cum.get(k, 0) + 16
        tok = (sem, self.dma_cum[k])
        self._commit(tok, rt, wt)
        return tok

    def retoken(self, tiles, sem):
        tok = (sem, self.dma_cum[id(sem)])
        for t in tiles:
            t.region.last_w = {id(sem): tok}

    def wait_all(self, engname, toks):
        e = self.eng[engname]
        need = {}
        for tok in toks:
            k = id(tok[0])
            if k not in need or need[k][1] < tok[1]:
                need[k] = tok
        self._wait(e, need)

    def mm(self, out, lhsT, rhs, start, stop, sig=None, **kw):
        return self.op("pe", lambda h, o, l, r: h.matmul(o, l, r, start=start, stop=stop, **kw),
                       [out], [lhsT, rhs], sig=stop if sig is None else sig)

    def tr(self, out, in_, ident, sig=False):
        return self.op("pe", lambda h, o, i, d: h.transpose(o, i, d), [out], [in_, ident], sig=sig)

    def act(self, out, in_, func, bias=None, scale=None, accum_out=None, eng="act"):
        ins = [in_]
        kw = {}
        order = []
        if bias is not None:
            if isinstance(bias, View):
                ins.append(bias); order.append("bias")
            else:
                kw["bias"] = bias
        if scale is not None:
            if isinstance(scale, View):
                ins.append(scale); order.append("scale")
            else:
                kw["scale"] = scale
        outs = [out]
        if accum_out is not None:
            outs.append(accum_out)

        def fn(h, *a):
            o = a[0]
            i0 = 1
            acc = None
            if accum_out is not None:
                acc = a[1]; i0 = 2
            k2 = dict(kw)
            for j, nm in enumerate(order):
                k2[nm] = a[i0 + 1 + j]
            if acc is not None:
                k2["accum_out"] = acc
            return h.activation(o, a[i0], func, **k2)
        return self.op(eng, fn, outs, ins)

    def tt(self, out, in0, in1, op, eng="dve"):
        return self.op(eng, lambda h, o, a, b: h.tensor_tensor(o, a, b, op), [out], [in0, in1])

    def ts(self, out, in0, s1, s2, op0, op1=None, accum_out=None, eng="dve"):
        ins = [in0]
        idx = {}
        if isinstance(s1, View):
            idx["s1"] = len(ins); ins.append(s1)
        if isinstance(s2, View):
            idx["s2"] = len(ins); ins.append(s2)
        outs = [out] + ([accum_out] if accum_out is not None else [])
        no = len(outs)

        def fn(h, *a):
            o = a[0]
            i = a[no:]
            v1 = i[idx["s1"]] if "s1" in idx else s1
            v2 = i[idx["s2"]] if "s2" in idx else s2
            kw = {}
            if op1 is not None:
                kw["op1"] = op1
            if accum_out is not None:
                kw["accum_out"] = a[1]
            return h.tensor_scalar(o, i[0], v1, v2, op0, **kw)
        return self.op(eng, fn, outs, ins)

    def stt(self, out, in0, scalar, in1, op0, op1, eng="dve"):
        ins = [in0, in1]
        if isinstance(scalar, View):
            ins.append(scalar)

        def fn(h, o, a, b, *s):
            return h.scalar_tensor_tensor(o, a, s[0] if s else scalar, b, op0, op1)
        return self.op(eng, fn, [out], ins)

    def copy(self, out, in_, eng="dve"):
        return self.op(eng, lambda h, o, i: h.tensor_copy(o, i), [out], [in_])

    def memset(self, out, val, eng="dve"):
        return self.op(eng, lambda h, o: h.memset(o, val), [out], [])

    def reduce(self, out, in_, op, axis=AX.X, eng="dve"):
        return self.op(eng, lambda h, o, i: h.tensor_reduce(o, i, axis, op), [out], [in_])

    def recip(self, out, in_):
        return self.op("dve", lambda h, o, i: h.reciprocal(o, i), [out], [in_])


from contextlib import ExitStack
from concourse.bass_utils import run_bass_kernel_spmd

D = 2048
S = 2048
T = 512
NPASS = S // T
TT = T // 128
MEM = 256
DFF = 5632
NFC = DFF // 512
EPS = 1e-6
C_IN = 8960
G0 = 2816
NSLAB = 7
SLAB_EL = 4096
PER_PASS = 117
NSLAB_TOTAL = 4 + PER_PASS

DEBUG = {}


def build(npass=NPASS, dbg=None):
    nc = bass.Bass("TRN2", target_bir_lowering=False)

    def din(name, shape, dt=F32):
        return nc.dram_tensor(name, list(shape), dt, kind="ExternalInput").ap()

    x_d = din("x", [S, D])
    mem_d = din("mem", [MEM, D])
    pos_d = din("pos_t", [128, 16], I32)
    gmix_d = din("gmixT", [128, 16])
    gffn_d = din("gffnT", [128, 16])
    gmem_d = din("gmemT", [128, 16])
    wpack_d = din("wpack", [NSLAB_TOTAL, 128, SLAB_EL])
    gav_d = din("gavT", [128, 4])
    wsT_d = din("wsT", [128, 4, 128])
    bsp_d = din("bsp", [128, 512])
    gbq_d = din("gbq", [128, 64])
    gbk_d = din("gbk", [128, 64])
    sinks_d = din("sinks", [128, 16])
    gcq_d = din("gcq", [128, 1])
    gck_d = din("gck", [128, 128])
    cw_d = din("convw", [128, 3, 88])
    cb_d = din("convb", [128, 88])
    out_d = nc.dram_tensor("out", [S, D], F32, kind="ExternalOutput").ap()
    dbg_d = None
    if dbg is not None:
        dbg_d = nc.dram_tensor("dbg", [128, dbg], F32, kind="ExternalOutput").ap()

    K = MK(nc, 206 * 1024)
    sem_c = K.new_sem("const")
    sem_x = [K.new_sem(f"x{i}") for i in range(TT)]
    sem_slab = [K.new_sem(f"slab{i}") for i in range(NSLAB)]
    sem_slabx = [K.new_sem(f"slabx{i}") for i in range(2)]
    sem_xs = [K.new_sem(f"xs{i}") for i in range(TT)]
    sem_dbg = K.new_sem("dbg")

    X1 = [K.sb_tile([128, D], F32, f"x1_{i}") for i in range(TT)]
    X1m = [[K.sb_tile([128, 512], F32, f"x1_{i}_{m}", at=X1[i].region.lo + m * 2048) for m in range(4)]
           for i in range(TT)]
    hT = [K.sb_tile([128, T], BF16, f"hT{c}") for c in range(16)]
    yTa = K.sb_tile([128, 4, T], BF16, "yTa")
    yTb = K.sb_tile([128, 8, T], BF16, "yTb")
    yTc = K.sb_tile([128, 4, T], BF16, "yTc")
    mT = [K.sb_tile([128, T], BF16, f"mT{c}") for c in range(16)]
    slabs = [K.sb_tile([128, SLAB_EL], BF16, f"slab{i}") for i in range(NSLAB)]
    xstage = [K.sb_tile([128, D], F32, f"xstage{i}", at=yTa.region.lo + i * 8192) for i in range(TT)]
    assert xstage[-1].region.hi <= mT[15].region.hi
    slabs_x = [K.sb_tile([128, SLAB_EL], BF16, f"slabx{i}", at=mT[8 * i].region.lo) for i in range(2)]
    ident = K.sb_tile([128, 128], BF16, "ident")
    gmixT = K.sb_tile([128, 16], F32, "gmixT")
    gffnT = K.sb_tile([128, 16], F32, "gffnT")
    gmemT = K.sb_tile([128, 16], F32, "gmemT")
    gavT = K.sb_tile([128, 4], F32, "gavT")
    gcq = K.sb_tile([128, 1], F32, "gcq")
    convw = K.sb_tile([128, 3, 88], F32, "convw")
    convb = K.sb_tile([128, 88], F32, "convb")
    hist = K.sb_tile([128, 88, 2], F32, "hist")
    wsT = K.sb_tile([128, 4, 128], BF16, "wsT")
    bsp = K.sb_tile([128, 4, 128], F32, "bsp")
    gbq = K.sb_tile([128, 64], F32, "gbq")
    gbk = K.sb_tile([128, 64], F32, "gbk")
    gck = K.sb_tile([128, 128], F32, "gck")
    sinkexp = K.sb_tile([128, 16], F32, "sinkexp")
    cosT = K.sb_tile([128, 16, 8], F32, "cosT")
    sinT = K.sb_tile([128, 16, 8], F32, "sinT")
    mask_cur = K.sb_tile([128, 4, 128], BF16, "mask_cur")
    mask_prev = K.sb_tile([128, 4, 128], BF16, "mask_prev")
    ones_bf = K.sb_tile([128, 128], BF16, "ones_bf")
    kTmem = K.sb_tile([128, 4, MEM], BF16, "kTmem")
    vmem = K.sb_tile([128, 2, 512], BF16, "vmem")
    KT = [K.sb_tile([128, 4, 128], BF16, f"KT{i}") for i in range(3)]
    vaug = [K.sb_tile([128, 2, 65], BF16, f"vaug{i}") for i in range(3)]
    kpad = K.sb_tile([128, 4, 128], BF16, "kpad")
    ss = K.sb_tile([128, 32], F32, "ss")
    rstd = K.sb_tile([128, 32], F32, "rstd")
    ssA = K.sb_tile([128, 8], F32, "ssA")
    ssB = K.sb_tile([128, 32], F32, "ssB")
    rstdB = K.sb_tile([128, 32], F32, "rstdB")

    base = K.sb_top
    xn = [K.sb_tile([128, D], BF16, f"xn{i}") for i in range(2)]
    junk = xn[1]
    top_norm = K.sb_top
    xs = [K.sb_tile([128, 516], F32, f"xs{i}") for i in range(2)]
    acc = [K.sb_tile([128, 512], F32, f"acc{i}") for i in range(2)]
    sa = [K.sb_tile([128, 512], BF16, f"sa{c}") for c in range(4)]
    actT = [[K.sb_tile([128, 512], BF16, f"actT{j}_{c}") for c in range(4)] for j in range(2)]
    top_f = K.sb_top
    K.sb_top = top_norm
    sg = [K.sb_tile([128, 512], BF16, f"sg{i}") for i in range(6)]
    tm = [K.sb_tile([128, 512], F32, f"tm{i}") for i in range(4)]
    top_m = K.sb_top
    K.sb_top = base
    uT = [K.sb_tile([128, T], BF16, f"uT{c}") for c in range(4)]
    vn = [K.sb_tile([128, 512], BF16, f"vn{i}") for i in range(TT)]
    vg = [K.sb_tile([128, 512], F32, f"vg{i}") for i in range(2)]
    tmpA = K.sb_tile([128, 512], F32, "tmpA")
    sq = K.sb_tile([128, 1024 + 128], F32, "sq")
    qn = K.sb_tile([128, 16, 64], F32, "qn", at=sq.region.lo)
    kn = K.sb_tile([128, 2, 64], F32, "kn")
    rt = [K.sb_tile([128, 16, 8], F32, f"rt{i}") for i in range(4)]
    qr = K.sb_tile([128, 1024], BF16, "qr")
    yb = K.sb_tile([128, 1024], BF16, "yb")
    qT = [K.sb_tile([128, 1024], BF16, f"qT{i}") for i in range(2)]
    PT = [K.sb_tile([128, 512], BF16, f"PT{i}") for i in range(8)]
    den = K.sb_tile([128, 16], F32, "den")
    sqc = K.sb_tile([128, 512], BF16, "sqc")
    msc = K.sb_tile([128, 512], F32, "msc")
    qcT = K.sb_tile([128, 512], BF16, "qcT")
    PTc = [K.sb_tile([128, 512], BF16, f"PTc{i}") for i in range(2)]
    rdc = K.sb_tile([128, 512], F32, "rdc")
    top_abc = K.sb_top
    K.sb_top = max(top_f, top_m, top_abc)
    print("SBUF bytes used per partition:", K.sb_top)

    banks = [K.ps_tile(b, 1, [128, 512], F32, f"bank{b}") for b in range(8)]
    bank_i = [0]

    def bank():
        b = banks[bank_i[0] % 7]
        bank_i[0] += 1
        return b

    def bf(bk, shape):
        v = bk.bitcast(BF16)
        n = int(np.prod(shape[1:]))
        v = v[:, 0:n]
        if len(shape) == 3:
            v = v.rearrange("p (a b) -> p a b", a=shape[1])
        return v

    slab_i = [0]

    plan = []
    plan_idx = {}
    free_slots = list(range(NSLAB))

    def load_slab(parts, xslot=None):
        if xslot is None:
            assert free_slots, "slab ring exhausted"
            idx = free_slots.pop(0)
        key = tuple(tuple(p_) for p_ in parts)
        if key not in plan_idx:
            plan_idx[key] = len(plan)
            plan.append(list(parts))
        sidx = plan_idx[key]
        slab_i[0] += 1
        if xslot is not None:
            sl = slabs_x[xslot]
            K.dma("pool", sl, wpack_d[sidx], sem_slabx[xslot])
            return sl
        sl = slabs[idx]
        K.dma("pool", sl, wpack_d[sidx], sem_slab[idx])
        return sl

    def release(*views):
        for v in views:
            if v.tile in slabs_x:
                continue
            idx = slabs.index(v.tile)
            assert idx not in free_slots
            free_slots.append(idx)

    w_in_v, wmkv_v, wo_v, wup_v, wdn_v = "w_in", "w_mem_kv", "w_out", "w_up", "w_down"
    wba_v, wbb_v, wbc_v = "w_branch_a", "w_branch_b", "w_branch_c"

    def slab_k8(view, kh, c0, xslot=None):
        sl = load_slab([(0, view, kh * 8, (kh + 1) * 8, c0, c0 + 512)], xslot=xslot)
        return sl.rearrange("p (a b) -> p a b", a=8)

    def slab_k16(view, c0):
        sl = load_slab([(0, view, 0, 16, c0, c0 + 256)])
        return sl.rearrange("p (a b) -> p a b", a=16)

    dbg_off = [0]
    dbg_st = []
    dbg_n = [0]

    def dump(v, n):
        if dbg_d is None:
            return
        if not dbg_st:
            dbg_st.extend(K.sb_tile([128, 512], F32, f"dbgst{j}") for j in range(2))
        for c0 in range(0, n, 512):
            st = dbg_st[dbg_n[0] % 2]
            dbg_n[0] += 1
            K.copy(st, v[:, c0:c0 + 512])
            K.dma("sp", dbg_d[:, dbg_off[0]:dbg_off[0] + 512], st, sem_dbg)
            dbg_off[0] += 512

    for mt in range(2):
        K.dma("sp", xstage[mt], mem_d[mt * 128:(mt + 1) * 128, :], sem_xs[mt])
    cl = []
    for t, src in ((gmixT, gmix_d), (gffnT, gffn_d), (gmemT, gmem_d), (gavT, gav_d), (gcq, gcq_d),
                   (convw, cw_d), (convb, cb_d)):
        K.dma("sp", t, src, sem_c)
        cl.append(t)
    wsT_f = K.sb_tile([128, 4, 128], F32, "wsT_f", at=top_norm + 11264)
    pos_i = K.sb_tile([128, 16], I32, "pos_i", at=top_norm + 13312)
    sinks_f = K.sb_tile([128, 16], F32, "sinks_f", at=top_norm + 13376)
    K.dma("sp", wsT_f, wsT_d, sem_c)
    K.dma("sp", pos_i, pos_d, sem_c)
    K.dma("sp", bsp.rearrange("p a b -> p (a b)"), bsp_d, sem_c)
    K.dma("sp", gbq, gbq_d, sem_c)
    K.dma("sp", gbk, gbk_d, sem_c)
    K.dma("sp", gck, gck_d, sem_c)
    K.dma("sp", sinks_f, sinks_d, sem_c)
    K.retoken(cl + [wsT_f, pos_i, bsp, gbq, gbk, gck, sinks_f], sem_c)

    for i in range(TT):
        K.dma("sp", X1[i], x_d[i * 128:(i + 1) * 128, :], sem_x[i])
    K.memset(ident, 1.0, eng="pool")
    K.op("pool", lambda h, o, i: h.affine_select(o, i, [[-1, 128]], ALU.is_equal, 0.0, base=0,
                                                channel_multiplier=1), [ident], [ident])
    wk = [[slab_k8(wmkv_v, kh, n * 512) for kh in range(2)] for n in range(2)]
    junk2 = K.sb_tile([128, D], BF16, "junk2", at=actT[0][0].region.lo)

    def norm_sumsq(j, src, alt_junk=False):
        if j % 2 == 0:
            K.act(junk2 if alt_junk else junk, src, AF.Square, accum_out=ss[:, j:j + 1])
        else:
            K.op("dve", lambda h, o, acc_, a_, b_: h.scalar_tensor_tensor(o, a_, 1.0, b_, ALU.mult, ALU.mult,
                                                                          accum_out=acc_),
                 [junk2 if alt_junk else xn[0], ss[:, j:j + 1]], [src, src])

    def norm_rstd(n):
        K.ts(ss[:, 8:8 + n], ss[:, 0:n], 1.0 / D, EPS, ALU.mult, ALU.add)
        K.act(ss[:, 16:16 + n], ss[:, 8:8 + n], AF.Sqrt)
        K.recip(rstd[:, 0:n], ss[:, 16:16 + n])

    def norm_scale(j, src):
        if j % 2 == 0:
            K.act(xn[j % 2], src, AF.Identity, scale=rstd[:, j:j + 1])
        else:
            K.ts(xn[j % 2], src, rstd[:, j:j + 1], None, ALU.mult)

    def norm_transpose(j, gT, dstT):
        x_ = xn[j % 2]
        for hb in range(2):
            bk = bank()
            bv = bf(bk, [128, 8, 128])
            for jj in range(8):
                c = hb * 8 + jj
                K.tr(bv[:, jj, :], x_[:, c * 128:(c + 1) * 128], ident, sig=(jj == 7))
            for jj in range(8):
                c = hb * 8 + jj
                if hb == 0:
                    K.ts(dstT(c)[:, j * 128:(j + 1) * 128], bv[:, jj, :], gT[:, c:c + 1], None, ALU.mult)
                else:
                    K.act(dstT(c)[:, j * 128:(j + 1) * 128], bv[:, jj, :], AF.Identity, scale=gT[:, c:c + 1])

    def norm_tiles(srcs, gT, dstT, have_stats=False, prescaled=0):
        n = len(srcs)
        if not have_stats:
            K.memset(ss[:, 0:n], 0.0)
            for j, src in enumerate(srcs):
                norm_sumsq(j, src)
            norm_rstd(n)
        for j, src in enumerate(srcs):
            if j >= prescaled:
                norm_scale(j, src)
            norm_transpose(j, gT, dstT)

    memT = K.sb_tile([128, 16, MEM], BF16, "memT", at=top_norm)
    norm_tiles([xstage[0], xstage[1]], gmemT, lambda c: memT[:, c, :])
    kfm = K.sb_tile([128, 512], F32, "kfm", at=top_norm + 8192)
    kbm = K.sb_tile([128, 512], BF16, "kbm", at=top_norm + 10240)
    for mt in range(2):
        bk_k, bk_v = bank(), bank()
        for n, bk in ((0, bk_k), (1, bk_v)):
            for k in range(16):
                K.mm(bk, memT[:, k, mt * 128:(mt + 1) * 128], wk[n][k // 8][:, k % 8, :], k == 0, k == 15)
        K.copy(vmem[:, mt, :], bk_v)
        K.act(kfm, bk_k, AF.Square)
        K.reduce(ss[:, 4:8], kfm.rearrange("p (h d) -> p h d", h=4), ALU.add)
        K.ts(ss[:, 8:12], ss[:, 4:8], 1.0 / 128, EPS, ALU.mult, ALU.add)
        K.act(ss[:, 12:16], ss[:, 8:12], AF.Sqrt)
        K.recip(rstd[:, 4:8], ss[:, 12:16])
        K.tt(kfm.rearrange("p (h d) -> p h d", h=4), bk_k.rearrange("p (h d) -> p h d", h=4),
             rstd[:, 4:8].unsqueeze(2).broadcast_to([128, 4, 128]), ALU.mult)
        K.tt(kbm.rearrange("p (h d) -> p h d", h=4), kfm.rearrange("p (h d) -> p h d", h=4),
             gck.unsqueeze(1).broadcast_to([128, 4, 128]), ALU.mult)
        bk = bank()
        bv = bf(bk, [128, 4, 128])
        for h in range(4):
            K.tr(bv[:, h, :], kbm[:, h * 128:(h + 1) * 128], ident, sig=(h == 3))
        K.copy(kTmem[:, :, mt * 128:(mt + 1) * 128], bv)
    release(wk[0][0], wk[0][1], wk[1][0], wk[1][1])

    K.memset(mask_cur, 0.0, eng="pool")
    K.op("pool", lambda h, o, i: h.affine_select(o, i, [[0, 4], [1, 128]], ALU.is_ge, -30000.0, base=0,
                                                channel_multiplier=-1), [mask_cur], [mask_cur])
    K.memset(mask_prev, 0.0, eng="pool")
    K.op("pool", lambda h, o, i: h.affine_select(o, i, [[0, 4], [-1, 128]], ALU.is_gt, -30000.0, base=0,
                                                channel_multiplier=1), [mask_prev], [mask_prev])
    K.memset(ones_bf, 1.0, eng="pool")
    K.op("pool", lambda h, o, i: h.affine_select(o, i, [[0, 4], [1, 128]], ALU.is_ge, 0.0, base=0,
                                                channel_multiplier=-1), [wsT_f], [wsT_f])
    K.copy(wsT, wsT_f)
    K.memset(hist, 0.0)
    K.memset(kpad, 0.0)
    for i in range(3):
        K.memset(vaug[i], 1.0)
        K.memset(KT[i], 0.0)
    K.ts(gcq, gcq, 128.0 ** -0.5, None, ALU.mult)
    K.act(sinkexp, sinks_f, AF.Exp)

    ang = K.sb_tile([128, 16, 8], F32, "ang", at=top_norm + 13440)
    posf = K.sb_tile([128, 16], F32, "posf", at=top_norm + 13952)
    rr = K.sb_tile([128, 16, 8], F32, "rr", at=top_norm + 14464)
    rf = K.sb_tile([128, 16, 8], F32, "rf", at=top_norm + 14976)
    ri = K.sb_tile([128, 16, 8], I32, "ri", at=top_norm + 15488)
    mk_ = K.sb_tile([128, 16, 8], F32, "mk_", at=top_norm + 16000)
    K.copy(posf, pos_i)
    half = 8
    inv = (np.float32(500000.0) ** (-(np.arange(half, dtype=np.float32) / np.float32(half)))).astype(np.float32)
    for j in range(half):
        K.ts(ang[:, :, j], posf, float(inv[j]), None, ALU.mult)
    TWO_PI = 6.283185307179586
    for dst, shift in ((sinT, 0.0), (cosT, 0.25)):
        K.ts(rr, ang, 1.0 / TWO_PI, shift, ALU.mult, ALU.add)
        K.copy(ri, rr)
        K.copy(rf, ri)
        K.tt(rr, rr, rf, ALU.subtract)
        K.ts(mk_, rr, 0.5, None, ALU.is_gt)
        K.tt(rr, rr, mk_, ALU.subtract)
        K.ts(mk_, rr, -0.5, None, ALU.is_lt)
        K.tt(rr, rr, mk_, ALU.add)
        K.act(dst, rr, AF.Sin, scale=TWO_PI)


    pre_b = []
    for p in range(npass):
        t0 = p * T
        if p == 0:
            norm_tiles(X1, gmixT, lambda c: hT[c])
        else:
            norm_tiles(xstage, gmixT, lambda c: hT[c], have_stats=True, prescaled=2)
        if p == 0 and dbg_d is not None and "hT" in DEBUG:
            for c in range(16):
                dump(hT[c], T)

        def gen_a():
            Wu = [slab_k8(w_in_v, kh, 0) for kh in range(2)]
            yield
            for c in range(4):
                bk = bank()
                for k in range(16):
                    K.mm(bk, Wu[k // 8][:, k % 8, c * 128:(c + 1) * 128], hT[k], k == 0, k == 15)
                K.act(uT[c], bk, AF.Gelu_apprx_tanh)
                if c == 3:
                    release(*Wu)
                    Wv = [slab_k8(w_in_v, kh, 512) for kh in range(2)]
                yield
            for i in range(TT):
                bk = bank()
                for k in range(16):
                    K.mm(bk, hT[k][:, i * 128:(i + 1) * 128], Wv[k // 8][:, k % 8, :], k == 0, k == 15)
                g_ = vg[i % 2]
                K.act(g_, bk, AF.Gelu_apprx_tanh)
                K.memset(ssA[:, 0:1], 0.0)
                K.act(tmpA.bitcast(BF16)[:, 0:512], g_, AF.Square, accum_out=ssA[:, 0:1])
                K.ts(ssA[:, 1:2], ssA[:, 0:1], 1.0 / 512, EPS, ALU.mult, ALU.add)
                K.act(ssA[:, 2:3], ssA[:, 1:2], AF.Sqrt)
                K.recip(ssA[:, 3:4], ssA[:, 2:3])
                K.ts(vn[i], g_, ssA[:, 3:4], None, ALU.mult)
                if i == TT - 1:
                    release(*Wv)
                yield
            for g in range(4):
                bk = bank()
                for i in range(TT):
                    K.mm(bk[:, i * 128:(i + 1) * 128], vn[i][:, g * 128:(g + 1) * 128], wsT[:, g, :], True, True,
                         sig=(i == TT - 1))
                K.stt(tmpA.rearrange("p (a b) -> p a b", a=4), bk.rearrange("p (a b) -> p a b", a=4),
                      gavT[:, g:g + 1], bsp[:, g:g + 1, :].broadcast_to([128, 4, 128]), ALU.mult, ALU.add)
                K.tt(yTa[:, g, :], tmpA, uT[g], ALU.mult)
                yield

        def load_b():
            return ([[slab_k8(w_in_v, kh, 1024 + n * 512) for kh in range(2)] for n in range(2)],
                    slab_k16(w_in_v, 2048))

        def gen_b():
            Wq, Wkv = pre_b.pop() if pre_b else load_b()
            yield

            def stage_x(i):
                gi = p * TT + i
                cur = gi % 3
                bq = [bank(), bank()]
                bkv = bank()
                for n in range(2):
                    for k in range(16):
                        K.mm(bq[n], hT[k][:, i * 128:(i + 1) * 128], Wq[n][k // 8][:, k % 8, :], k == 0, k == 15)
                for k in range(16):
                    K.mm(bkv[:, 0:256], hT[k][:, i * 128:(i + 1) * 128], Wkv[:, k, :], k == 0, k == 15)
                for n in range(2):
                    K.act(sq[:, n * 512:(n + 1) * 512], bq[n], AF.Square)
                K.act(sq[:, 1024:1152], bkv[:, 0:128], AF.Square)
                K.act(vaug[cur][:, :, 0:64], bkv[:, 128:256].rearrange("p (g d) -> p g d", g=2), AF.Copy)
                K.reduce(ssB[:, 0:18], sq.rearrange("p (h d) -> p h d", h=18), ALU.add)
                K.ts(ssB[:, 0:18], ssB[:, 0:18], 1.0 / 64, EPS, ALU.mult, ALU.add)
                K.act(rstdB[:, 0:18], ssB[:, 0:18], AF.Sqrt)
                K.recip(rstdB[:, 0:18], rstdB[:, 0:18])
                K.ts(rstdB[:, 0:16], rstdB[:, 0:16], 0.125, None, ALU.mult)
                for n in range(2):
                    K.tt(qn[:, n * 8:(n + 1) * 8, :], bq[n].rearrange("p (h d) -> p h d", h=8),
                         rstdB[:, n * 8:(n + 1) * 8].unsqueeze(2).broadcast_to([128, 8, 64]), ALU.mult)
                K.tt(kn, bkv[:, 0:128].rearrange("p (g d) -> p g d", g=2),
                     rstdB[:, 16:18].unsqueeze(2).broadcast_to([128, 2, 64]), ALU.mult)
                yield
                PE_ = "dve"
                K.tt(qn, qn, gbq.unsqueeze(1).broadcast_to([128, 16, 64]), ALU.mult, eng=PE_)
                K.tt(kn, kn, gbk.unsqueeze(1).broadcast_to([128, 2, 64]), ALU.mult, eng=PE_)
                cs = cosT[:, gi, :].unsqueeze(1)
                sn = sinT[:, gi, :].unsqueeze(1)
                qr3 = qr.rearrange("p (h d) -> p h d", h=16)
                K.copy(qr3[:, :, 16:64], qn[:, :, 16:64], eng=PE_)
                K.tt(rt[0], qn[:, :, 0:8], cs.broadcast_to([128, 16, 8]), ALU.mult, eng=PE_)
                K.tt(rt[1], qn[:, :, 8:16], sn.broadcast_to([128, 16, 8]), ALU.mult, eng=PE_)
                K.tt(qr3[:, :, 0:8], rt[0], rt[1], ALU.subtract, eng=PE_)
                K.tt(rt[2], qn[:, :, 8:16], cs.broadcast_to([128, 16, 8]), ALU.mult, eng=PE_)
                K.tt(rt[3], qn[:, :, 0:8], sn.broadcast_to([128, 16, 8]), ALU.mult, eng=PE_)
                K.tt(qr3[:, :, 8:16], rt[2], rt[3], ALU.add, eng=PE_)
                kd = kpad.rearrange("p (g q) d -> p g q d", g=2)
                d0 = kd[:, :, 0, 0:64]
                K.copy(d0[:, :, 16:64], kn[:, :, 16:64], eng=PE_)
                K.tt(rt[0][:, 0:2, :], kn[:, :, 0:8], cs.broadcast_to([128, 2, 8]), ALU.mult, eng=PE_)
                K.tt(rt[1][:, 0:2, :], kn[:, :, 8:16], sn.broadcast_to([128, 2, 8]), ALU.mult, eng=PE_)
                K.tt(d0[:, :, 0:8], rt[0][:, 0:2, :], rt[1][:, 0:2, :], ALU.subtract, eng=PE_)
                K.tt(rt[2][:, 0:2, :], kn[:, :, 8:16], cs.broadcast_to([128, 2, 8]), ALU.mult, eng=PE_)
                K.tt(rt[3][:, 0:2, :], kn[:, :, 0:8], sn.broadcast_to([128, 2, 8]), ALU.mult, eng=PE_)
                K.tt(d0[:, :, 8:16], rt[2][:, 0:2, :], rt[3][:, 0:2, :], ALU.add, eng=PE_)
                K.copy(kd[:, :, 1, 64:128], d0, eng=PE_)
                yield
                bk = bank()
                bv = bf(bk, [128, 4, 128])
                for j in range(4):
                    K.tr(bv[:, j, :], kpad[:, j, :], ident, sig=(j == 3))
                K.copy(KT[cur], bv)
                bk = bank()
                bv = bf(bk, [128, 8, 128])
                for j in range(8):
                    K.tr(bv[:, j, :], qr[:, j * 128:(j + 1) * 128], ident, sig=(j == 7))
                K.act(qT[i % 2].rearrange("p (a b) -> p a b", a=8), bv, AF.Copy)
                yield

            def stage_y(i):
                gi = p * TT + i
                cur, prv = gi % 3, (gi - 1) % 3
                qT_ = qT[i % 2]
                blocks = ([(prv, mask_prev)] if gi > 0 else []) + [(cur, mask_cur)]
                pt_of = {}
                pi = 0
                for g in range(2):
                    for (kb, msk) in blocks:
                        for par in range(2):
                            bs = bank()
                            K.mm(bs, KT[kb][:, 2 * g + par, :], qT_[:, g * 512:(g + 1) * 512], True, False)
                            K.mm(bs, ident, msk.rearrange("p a b -> p (a b)"), False, True)
                            ptile = PT[pi]
                            pi += 1
                            K.act(ptile, bs, AF.Exp)
                            pt_of[(g, kb, par)] = ptile
                    yield
                ob = [bank() for _ in range(4)]
                for h in range(16):
                    g, jj, par = h // 8, (h % 8) // 2, h % 2
                    o = ob[h // 4].rearrange("p (a b) -> p a b", a=4)[:, h % 4, 0:65]
                    for bi, (kb, msk) in enumerate(blocks):
                        K.mm(o, pt_of[(g, kb, par)][:, jj * 128:(jj + 1) * 128], vaug[kb][:, g, :],
                             bi == 0, bi == len(blocks) - 1, sig=(bi == len(blocks) - 1 and h % 4 == 3))
                for b4 in range(4):
                    o3 = ob[b4].rearrange("p (a b) -> p a b", a=4)
                    K.tt(den[:, b4 * 4:(b4 + 1) * 4], o3[:, :, 64], sinkexp[:, b4 * 4:(b4 + 1) * 4], ALU.add)
                K.recip(den, den)
                for b4 in range(4):
                    o3 = ob[b4].rearrange("p (a b) -> p a b", a=4)
                    K.tt(yb[:, b4 * 256:(b4 + 1) * 256].rearrange("p (a b) -> p a b", a=4), o3[:, :, 0:64],
                         den[:, b4 * 4:(b4 + 1) * 4].unsqueeze(2).broadcast_to([128, 4, 64]), ALU.mult)
                yield
                bk = bank()
                bv = bf(bk, [128, 8, 128])
                for j in range(8):
                    K.tr(bv[:, j, :], yb[:, j * 128:(j + 1) * 128], ident, sig=(j == 7))
                K.act(yTb[:, :, i * 128:(i + 1) * 128], bv, AF.Copy)
                yield

            def chain_x():
                for i in range(TT):
                    while y_done[0] < i - 1:
                        yield
                    yield from stage_x(i)
                    x_done[0] = i + 1
                release(Wq[0][0], Wq[0][1], Wq[1][0], Wq[1][1], Wkv)

            def chain_y():
                for i in range(TT):
                    while x_done[0] < i + 1:
                        yield
                    yield from stage_y(i)
                    y_done[0] = i + 1

            b_chains.extend([chain_x(), chain_y()])

        def gen_c():
            Wqc = [slab_k8(w_in_v, kh, 2304, xslot=kh) for kh in range(2)]
            yield
            for h in range(4):
                bqc = banks[7]
                for k in range(16):
                    K.mm(bqc, Wqc[k // 8][:, k % 8, h * 128:(h + 1) * 128], hT[k], k == 0, k == 15)
                K.act(sqc, bqc, AF.Square)
                yield
                bss = bank()
                K.mm(bss, ones_bf, sqc, True, True)
                K.ts(msc, bss, 1.0 / 128, EPS, ALU.mult, ALU.add)
                K.act(msc, msc, AF.Sqrt)
                K.recip(msc, msc)
                K.stt(qcT, bqc, gcq[:, 0:1], msc, ALU.mult, ALU.mult)
                yield
                for mt in range(2):
                    bs = bank()
                    K.mm(bs, kTmem[:, h, mt * 128:(mt + 1) * 128], qcT, True, True)
                    K.act(PTc[mt], bs, AF.Exp)
                yield
                bo, bd = bank(), bank()
                for mt in range(2):
                    K.mm(bo, vmem[:, mt, h * 128:(h + 1) * 128], PTc[mt], mt == 0, mt == 1)
                for mt in range(2):
                    K.mm(bd, ones_bf, PTc[mt], mt == 0, mt == 1)
                K.recip(rdc, bd)
                K.tt(yTc[:, h, :], bo, rdc, ALU.mult)
                if h == 3:
                    release(*Wqc)
                yield

        x_done, y_done, b_chains = [0], [0], []
        ga, gb, gc = gen_a(), gen_b(), gen_c()
        next(gb)
        for _ in gb:
            pass
        gens = [b_chains[0], b_chains[1], gc, ga]
        c_started = True
        while gens:
            for g_ in list(gens):
                try:
                    next(g_)
                except StopIteration:
                    gens.remove(g_)
                    if g_ is ga and not c_started:
                        gens.append(gc)
                        c_started = True
        if p == 0 and dbg_d is not None:
            if "yTa" in DEBUG:
                for c in range(4):
                    dump(yTa[:, c, :], T)
            if "yTb" in DEBUG:
                for c in range(8):
                    dump(yTb[:, c, :], T)
            if "yTc" in DEBUG:
                for c in range(4):
                    dump(yTc[:, c, :], T)

        for np_ in range(8):
            Gs = [slab_k16(w_in_v, G0 + gidx * D + np_ * 256) for gidx in range(3)]
            c0 = np_ * 256
            BR = load_slab([(0, wba_v, 0, 4, c0, c0 + 256),
                            (1024, wbb_v, 0, 8, c0, c0 + 256),
                            (3072, wbc_v, 0, 4, c0, c0 + 256)]).rearrange("p (a b) -> p a b", a=16)
            ysrc = [(yTa, 0, 4), (yTb, 4, 8), (yTc, 12, 4)]
            for nn in range(2):
                n = np_ * 2 + nn
                cols = slice(nn * 128, (nn + 1) * 128)
                tms = []
                for gidx in range(3):
                    bg = bank()
                    for k in range(16):
                        K.mm(bg, Gs[gidx][:, k, cols], hT[k], k == 0, k == 15)
                    s_ = sg[(n * 3 + gidx) % 6]
                    K.act(s_, bg, AF.Sigmoid)
                    yt, boff, nk = ysrc[gidx]
                    bb = bank()
                    for kc in range(nk):
                        K.mm(bb, BR[:, boff + kc, cols], yt[:, kc, :], kc == 0, kc == nk - 1)
                    t_ = tm[gidx]
                    K.tt(t_, bb, s_, ALU.mult)
                    tms.append(t_)
                K.tt(tm[3], tms[0], tms[1], ALU.add)
                K.tt(mT[n], tm[3], tms[2], ALU.add)
            release(Gs[0], Gs[1], Gs[2], BR)
        if p == 0 and dbg_d is not None and "mT" in DEBUG:
            for c in range(16):
                dump(mT[c], T)

        for m in range(4):
            Wo = [slab_k8(wo_v, kh, m * 512) for kh in range(2)]
            for i in range(TT):
                bk = bank()
                for k in range(16):
                    K.mm(bk, mT[k][:, i * 128:(i + 1) * 128], Wo[k // 8][:, k % 8, :], k == 0, k == 15)
                K.tt(X1m[i][m], X1m[i][m], bk, ALU.add)
                if m == 3:
                    if i == 0:
                        K.memset(ss[:, 0:TT], 0.0)
                    norm_sumsq(i, X1[i], alt_junk=True)
                    K.ts(ss[:, 8 + i:9 + i], ss[:, i:i + 1], 1.0 / D, EPS, ALU.mult, ALU.add)
                    K.act(ss[:, 16 + i:17 + i], ss[:, 8 + i:9 + i], AF.Sqrt)
                    K.recip(rstd[:, i:i + 1], ss[:, 16 + i:17 + i])
                    if i >= 2:
                        norm_transpose(i - 2, gffnT, lambda c: hT[c])
                    norm_scale(i, X1[i])
            release(*Wo)
        if p == 0 and dbg_d is not None and "x1" in DEBUG:
            for i in range(TT):
                dump(X1[i], D)

        norm_transpose(2, gffnT, lambda c: hT[c])
        norm_transpose(3, gffnT, lambda c: hT[c])

        def f_up(fc, ab):
            aT = actT[fc % 2]
            U = [slab_k8(wup_v, kh, ab * DFF + fc * 512) for kh in range(2)]
            for c in range(4):
                cg = ab * 44 + fc * 4 + c
                bk = bank()
                for k in range(16):
                    K.mm(bk, U[k // 8][:, k % 8, c * 128:(c + 1) * 128], hT[k], k == 0, k == 15)
                x_ = xs[c % 2]
                a_ = acc[c % 2]
                K.copy(x_[:, 0:2], hist[:, cg, :])
                K.act(x_[:, 2:514], bk, AF.Copy)
                K.copy(hist[:, cg, :], x_[:, 512:514])
                K.act(a_, bk, AF.Identity, bias=convb[:, cg:cg + 1], scale=convw[:, 2, cg:cg + 1])
                K.stt(a_, x_[:, 1:513], convw[:, 1, cg:cg + 1], a_, ALU.mult, ALU.add)
                K.stt(a_, x_[:, 0:512], convw[:, 0, cg:cg + 1], a_, ALU.mult, ALU.add)
                if ab == 0:
                    K.act(sa[c], a_, AF.Silu)
                else:
                    K.tt(aT[c], sa[c], a_, ALU.mult)
            release(*U)

        def load_dn(fc):
            return [load_slab([(0, wdn_v, fc * 4 + 2 * j, fc * 4 + 2 * j + 2, 0, 2048)]).rearrange(
                "p (a b) -> p a b", a=2) for j in range(2)]

        def f_down(fc, Dn=None):
            aT = actT[fc % 2]
            if Dn is None:
                Dn = load_dn(fc)
            for i in range(TT):
                for m in range(4):
                    bk = bank()
                    for c in range(4):
                        K.mm(bk, aT[c][:, i * 128:(i + 1) * 128], Dn[c // 2][:, c % 2, m * 512:(m + 1) * 512],
                             c == 0, c == 3)
                    K.tt(X1m[i][m], X1m[i][m], bk, ALU.add)
            release(*Dn)

        f_up(0, 0)
        f_up(0, 1)
        if p + 1 < npass:
            for i in range(TT):
                K.dma("sp", xstage[i], x_d[t0 + T + i * 128:t0 + T + (i + 1) * 128, :], sem_xs[i])
        for fc in range(NFC):
            if fc + 1 < NFC:
                f_up(fc + 1, 0)
            if fc == NFC - 1 and p + 1 < npass:
                K.memset(ss[:, 0:TT], 0.0)
                for j in range(TT):
                    norm_sumsq(j, xstage[j])
                norm_rstd(TT)
                norm_scale(0, xstage[0])
                norm_scale(1, xstage[1])
                dn_last = load_dn(fc)
                pre_b.append(load_b())
                f_down(fc, dn_last)
                continue
            f_down(fc)
            if fc + 1 < NFC:
                f_up(fc + 1, 1)
        for i in range(TT):
            K.dma("sp", out_d[t0 + i * 128:t0 + (i + 1) * 128, :], X1[i], sem_x[i])
        if p + 1 < npass:
            for i in (2, 3, 0, 1):
                K.dma("sp", X1[i], xstage[i], sem_x[i])

    toks = [(sem_x[i], K.dma_cum[id(sem_x[i])]) for i in range(TT)]
    if dbg_d is not None and id(sem_dbg) in K.dma_cum:
        toks.append((sem_dbg, K.dma_cum[id(sem_dbg)]))
    K.wait_all("sp", toks)
    print("instructions", K.n_inst, "waits", K.n_wait, "slabs", slab_i[0], "dbg cols", dbg_off[0])
    assert len(plan) == NSLAB_TOTAL, len(plan)
    nc.mk_plan = plan
    return nc


_PLAN = []


def pack_weights(inputs, plan):
    wp = np.zeros((len(plan), 128, SLAB_EL), dtype=np.float32)
    views = {}
    for si, parts in enumerate(plan):
        for off, name, k0, k1, c0, c1 in parts:
            if name not in views:
                w = np.asarray(inputs[name][0], dtype=np.float32)
                views[name] = w.reshape(w.shape[0] // 128, 128, w.shape[1])
            blk = views[name][k0:k1, :, c0:c1]
            n = (k1 - k0) * (c1 - c0)
            wp[si, :, off:off + n] = np.transpose(blk, (1, 0, 2)).reshape(128, n)
    return wp


def prep_inputs(inputs, plan):
    f = lambda a: np.ascontiguousarray(np.asarray(a, dtype=np.float32))

    def rep(v, n):
        return np.ascontiguousarray(np.broadcast_to(f(v).reshape(1, n), (128, n)))

    def colT(v, nchunk):
        return np.ascontiguousarray(f(v).reshape(nchunk, 128).T)
    shared = {
        "gmixT": colT(inputs["g_mix"][0], 16),
        "gffnT": colT(inputs["g_ffn"][0], 16),
        "gmemT": colT(inputs["g_mem"][0], 16),
        "gavT": colT(inputs["g_a_v"][0], 4),
        "wsT": np.ascontiguousarray(np.transpose(f(inputs["w_spatial"][0]), (2, 0, 1))),
        "bsp": rep(inputs["b_spatial"][0], 512),
        "gbq": rep(inputs["g_b_q"][0], 64),
        "gbk": rep(inputs["g_b_k"][0], 64),
        "sinks": rep(inputs["sinks"][0], 16),
        "gcq": f(inputs["g_c_q"][0]).reshape(128, 1),
        "gck": rep(inputs["g_c_k"][0], 128),
        "convw": np.ascontiguousarray(np.transpose(f(inputs["conv_w"][0]).reshape(3, 88, 128), (2, 0, 1))),
        "convb": colT(inputs["conv_b"][0], 88),
    }
    shared["wpack"] = pack_weights(inputs, plan)
    x = np.asarray(inputs["x"], dtype=np.float32)
    mem = np.asarray(inputs["mem"], dtype=np.float32)
    pos = np.asarray(inputs["positions"], dtype=np.int32)
    maps = []
    for b in range(x.shape[0]):
        m = dict(shared)
        m["x"] = np.ascontiguousarray(x[b])
        m["mem"] = np.ascontiguousarray(mem[b])
        m["pos_t"] = np.ascontiguousarray(pos[b].reshape(16, 128).T)
        maps.append(m)
    return maps


def kernel(**inputs):
    nc = build()
    maps = prep_inputs(inputs, nc.mk_plan)
    res = run_bass_kernel_spmd(nc, maps, core_ids=list(range(len(maps))))
    return np.stack([np.asarray(r["out"], dtype=np.float32) for r in res.results], axis=0)
```
